# Optimizing a Trainium2 kernel written in Bass

```python
import math
import jax, jax.numpy as jnp
from jax import lax
import numpy as np


D_MODEL = 2048
BATCH = 8
SEQ = 2048
DEPTH = 2

HEAD_DIM = 128
FOX_HEADS = 8
DIFF_HEADS = 4
DIFF_V_DIM = 2 * HEAD_DIM
DIL_HEADS = D_MODEL // HEAD_DIM
DIL_PATTERNS = ((128, 1), (512, 4), (2048, 16))
Q_BLOCK = 128
N_BUCKETS = 32
BUCKET_MAX_EXACT = 16
BUCKET_MAX_DIST = 2048
N_BIAS_HEADS = DIL_HEADS
PEER_HEADS = 8
PEER_NKEYS = 128
PEER_EXPERTS = PEER_NKEYS * PEER_NKEYS
PEER_DKEY = 256
PEER_TOPK = 16
PEER_CHUNK = 128
NORM_EPS = 1e-6
FORGET_BIAS_INIT = 3.0

FOX_W = FOX_HEADS * HEAD_DIM
DIFF_QK_W = DIFF_HEADS * 2 * HEAD_DIM
DIFF_V_W = DIFF_HEADS * DIFF_V_DIM
EVEN_IN_W = 3 * FOX_W + FOX_HEADS + 2 * DIFF_QK_W + DIFF_V_W
EVEN_SPLITS = [int(v) for v in np.cumsum([FOX_W, FOX_W, FOX_W, FOX_HEADS, DIFF_QK_W, DIFF_QK_W])]
EVEN_OUT_W = FOX_W + DIFF_V_W

kernel_name = "hybrid_fox_diff_dilated_peer_adaln"


def rms_norm(x, gain):
    xf = x.astype(jnp.float32)
    y = xf * lax.rsqrt(jnp.mean(xf * xf, axis=-1, keepdims=True) + NORM_EPS)
    return (y * gain.astype(jnp.float32)).astype(x.dtype)


def t5_bucket(dist):
    n = jnp.maximum(dist, 0)
    nf = jnp.maximum(n, 1).astype(jnp.float32)
    large = BUCKET_MAX_EXACT + (jnp.log(nf / BUCKET_MAX_EXACT)
                                / math.log(BUCKET_MAX_DIST / BUCKET_MAX_EXACT)
                                * (N_BUCKETS - BUCKET_MAX_EXACT)).astype(jnp.int32)
    large = jnp.minimum(large, N_BUCKETS - 1)
    return jnp.where(n < BUCKET_MAX_EXACT, n, large)


def fox_attention(q, k, v, log_f_cum):
    b, s, h, dh = q.shape
    scale = dh ** -0.5
    kpos = jnp.arange(s)
    f_keys = log_f_cum.transpose(0, 2, 1)

    def block(i):
        t0 = i * Q_BLOCK
        qb = lax.dynamic_slice_in_dim(q, t0, Q_BLOCK, axis=1)
        fq = lax.dynamic_slice_in_dim(f_keys, t0, Q_BLOCK, axis=2)
        qpos = t0 + jnp.arange(Q_BLOCK)
        logits = jnp.einsum('bqhd,bkhd->bhqk', qb, k, preferred_element_type=jnp.float32) * scale
        logits = logits + fq[..., :, None] - f_keys[..., None, :]
        logits = jnp.where((qpos[:, None] >= kpos[None, :])[None, None], logits, -jnp.inf)
        p = jax.nn.softmax(logits, axis=-1)
        return jnp.einsum('bhqk,bkhd->bqhd', p.astype(v.dtype), v)

    out = lax.map(block, jnp.arange(s // Q_BLOCK))
    return out.transpose(1, 0, 2, 3, 4).reshape(b, s, h, dh)


def diff_attention(q, k, v, lam, bias_table):
    b, s, h, _, dh = q.shape
    scale = dh ** -0.5
    kpos = jnp.arange(s)

    def block(i):
        t0 = i * Q_BLOCK
        qb = lax.dynamic_slice_in_dim(q, t0, Q_BLOCK, axis=1)
        qpos = t0 + jnp.arange(Q_BLOCK)
        dist = qpos[:, None] - kpos[None, :]
        bias = bias_table[t5_bucket(dist)].astype(jnp.float32).transpose(2, 0, 1)
        logits = jnp.einsum('bqhmd,bkhmd->bhmqk', qb, k, preferred_element_type=jnp.float32) * scale
        logits = logits + bias[None, :, None]
        logits = jnp.where((dist >= 0)[None, None, None], logits, -jnp.inf)
        p = jax.nn.softmax(logits, axis=-1)
        attn = p[:, :, 0] - lam * p[:, :, 1]
        return jnp.einsum('bhqk,bkhe->bqhe', attn.astype(v.dtype), v)

    out = lax.map(block, jnp.arange(s // Q_BLOCK))
    return out.transpose(1, 0, 2, 3, 4).reshape(b, s, h, 2 * dh)


def dilated_attention(q, k, v, bias_table):
    b, s, h, dh = q.shape
    scale = dh ** -0.5
    outs, lses = [], []
    for window, dil in DIL_PATTERNS:
        span = window // dil
        n_sub = s // dil
        qb_len = math.gcd(Q_BLOCK, n_sub)
        nblk = n_sub // qb_len
        qs = q.reshape(b, nblk, qb_len, dil, h, dh)
        pad = ((0, 0), (span, 0), (0, 0), (0, 0), (0, 0))
        ks = jnp.pad(k.reshape(b, n_sub, dil, h, dh), pad)
        vs = jnp.pad(v.reshape(b, n_sub, dil, h, dh), pad)
        kidx = jnp.arange(nblk)[:, None] * qb_len + jnp.arange(qb_len + span)[None, :]
        kb = ks[:, kidx]
        vb = vs[:, kidx]
        step = jnp.arange(qb_len)[:, None] + span - jnp.arange(qb_len + span)[None, :]
        valid = ((step >= 0) & (step <= span))[None] & ((kidx - span) >= 0)[:, None, :]
        bias = bias_table[t5_bucket(step * dil)].astype(jnp.float32).transpose(2, 0, 1)
        logits = jnp.einsum('bnqrhd,bnkrhd->bnrhqk', qs, kb, preferred_element_type=jnp.float32) * scale
        logits = logits + bias[None, None, None]
        logits = jnp.where(valid[None, :, None, None], logits, -jnp.inf)
        lse = jax.nn.logsumexp(logits, axis=-1)
        probs = jnp.exp(logits - lse[..., None])
        o = jnp.einsum('bnrhqk,bnkrhd->bnqrhd', probs.astype(v.dtype), vb)
        outs.append(o.reshape(b, s, h, dh))
        lses.append(lse.transpose(0, 1, 4, 2, 3).reshape(b, s, h))
    w = jax.nn.softmax(jnp.stack(lses, axis=-1), axis=-1)
    return jnp.einsum('bshp,pbshd->bshd', w.astype(v.dtype), jnp.stack(outs))


def even_mixer(h, w_in, b_forget, fox_qk_gain, diff_qk_gain, diff_lambda,
               diff_subln_gain, w_out, diff_bias, lam_init):
    b, s, _ = h.shape
    proj = h @ w_in
    fq, fk, fv, ff, dq, dk, dv = jnp.split(proj, EVEN_SPLITS, axis=-1)
    fq = rms_norm(fq.reshape(b, s, FOX_HEADS, HEAD_DIM), fox_qk_gain[0])
    fk = rms_norm(fk.reshape(b, s, FOX_HEADS, HEAD_DIM), fox_qk_gain[1])
    fv = fv.reshape(b, s, FOX_HEADS, HEAD_DIM)
    log_f = jax.nn.log_sigmoid(ff.astype(jnp.float32) + b_forget.astype(jnp.float32))
    log_f_cum = jnp.cumsum(log_f, axis=1)
    fox_o = fox_attention(fq, fk, fv, log_f_cum)
    dq = rms_norm(dq.reshape(b, s, DIFF_HEADS, 2, HEAD_DIM), diff_qk_gain[0])
    dk = rms_norm(dk.reshape(b, s, DIFF_HEADS, 2, HEAD_DIM), diff_qk_gain[1])
    dv = dv.reshape(b, s, DIFF_HEADS, DIFF_V_DIM)
    lam_f = diff_lambda.astype(jnp.float32)
    lam = (jnp.exp(jnp.sum(lam_f[0] * lam_f[1])) - jnp.exp(jnp.sum(lam_f[2] * lam_f[3]))
           + lam_init)
    diff_o = diff_attention(dq, dk, dv, lam, diff_bias)
    diff_o = rms_norm(diff_o, diff_subln_gain) * (1.0 - lam_init)
    mixed = jnp.concatenate([fox_o.reshape(b, s, FOX_W),
                             diff_o.reshape(b, s, DIFF_V_W).astype(fox_o.dtype)], axis=-1)
    return mixed @ w_out


def odd_mixer(h, w_qkv, qk_gain, w_out, bias_table):
    b, s, _ = h.shape
    qkv = (h @ w_qkv).reshape(b, s, 3, DIL_HEADS, HEAD_DIM)
    q = rms_norm(qkv[:, :, 0], qk_gain[0])
    k = rms_norm(qkv[:, :, 1], qk_gain[1])
    o = dilated_attention(q, k, qkv[:, :, 2], bias_table)
    return o.reshape(b, s, DIL_HEADS * HEAD_DIM) @ w_out


def peer_ffn(h, w_query, sub_keys, expert_u, expert_v):
    b, s, d = h.shape
    t = b * s
    hf = h.reshape(t, d)
    q = (hf @ w_query).reshape(t, PEER_HEADS, 2, PEER_DKEY // 2)
    scores = jnp.einsum('thpd,pnd->thpn', q, sub_keys, preferred_element_type=jnp.float32)
    top_s, top_i = lax.top_k(scores, PEER_TOPK)
    cand_s = (top_s[:, :, 0, :, None] + top_s[:, :, 1, None, :]).reshape(t, PEER_HEADS, -1)
    cand_i = (top_i[:, :, 0, :, None] * PEER_NKEYS + top_i[:, :, 1, None, :]).reshape(t, PEER_HEADS, -1)
    best_s, best_pos = lax.top_k(cand_s, PEER_TOPK)
    expert_idx = jnp.take_along_axis(cand_i, best_pos, axis=-1)
    gates = jax.nn.softmax(best_s, axis=-1).astype(h.dtype)
    n_chunk = t // PEER_CHUNK

    def chunk(args):
        x_c, idx_c, g_c = args
        act = jax.nn.gelu(jnp.einsum('cd,chkd->chk', x_c, expert_u[idx_c]), approximate=False)
        return jnp.einsum('chk,chkd->cd', g_c * act, expert_v[idx_c])

    y = lax.map(chunk, (hf.reshape(n_chunk, PEER_CHUNK, d),
                        expert_idx.reshape(n_chunk, PEER_CHUNK, PEER_HEADS, PEER_TOPK),
                        gates.reshape(n_chunk, PEER_CHUNK, PEER_HEADS, PEER_TOPK)))
    return y.reshape(b, s, d)


def modulate(h, shift, scale):
    return h * (1.0 + scale[:, None, :]) + shift[:, None, :]


def setup_inputs(seed: int = 0) -> dict:
    key = jax.random.key(seed)
    ks = jax.random.split(key, 20)
    n_even = (DEPTH + 1) // 2
    n_odd = DEPTH // 2
    nrm = jax.random.normal
    f32 = jnp.float32
    sd = D_MODEL ** -0.5
    return {
        "x": nrm(ks[0], (BATCH, SEQ, D_MODEL), f32),
        "c": nrm(ks[1], (BATCH, D_MODEL), f32),
        "rel_bias": 0.3 * nrm(ks[2], (N_BUCKETS, N_BIAS_HEADS), f32),
        "norm_gain": 1.0 + 0.1 * nrm(ks[3], (DEPTH, 2, D_MODEL), f32),
        "w_ada": 0.5 * sd * nrm(ks[4], (DEPTH, D_MODEL, 6 * D_MODEL), f32),
        "b_ada": 0.02 * nrm(ks[5], (DEPTH, 6 * D_MODEL), f32),
        "even_w_in": sd * nrm(ks[6], (n_even, D_MODEL, EVEN_IN_W), f32),
        "even_b_forget": FORGET_BIAS_INIT + 0.5 * nrm(ks[7], (n_even, FOX_HEADS), f32),
        "even_fox_qk_gain": 1.0 + 0.1 * nrm(ks[8], (n_even, 2, HEAD_DIM), f32),
        "even_diff_qk_gain": 1.0 + 0.1 * nrm(ks[9], (n_even, 2, HEAD_DIM), f32),
        "even_diff_lambda": 0.1 * nrm(ks[10], (n_even, 4, HEAD_DIM), f32),
        "even_diff_subln_gain": 1.0 + 0.1 * nrm(ks[11], (n_even, DIFF_V_DIM), f32),
        "even_w_out": EVEN_OUT_W ** -0.5 * nrm(ks[12], (n_even, EVEN_OUT_W, D_MODEL), f32),
        "odd_w_qkv": sd * nrm(ks[13], (n_odd, D_MODEL, 3 * DIL_HEADS * HEAD_DIM), f32),
        "odd_qk_gain": 1.0 + 0.1 * nrm(ks[14], (n_odd, 2, HEAD_DIM), f32),
        "odd_w_out": sd * nrm(ks[15], (n_odd, DIL_HEADS * HEAD_DIM, D_MODEL), f32),
        "peer_w_query": sd * nrm(ks[16], (DEPTH, D_MODEL, PEER_HEADS * PEER_DKEY), f32),
        "peer_sub_keys": (PEER_DKEY // 2) ** -0.5 * nrm(ks[17], (DEPTH, 2, PEER_NKEYS, PEER_DKEY // 2), f32),
        "peer_u": sd * nrm(ks[18], (DEPTH, PEER_EXPERTS, D_MODEL), f32),
        "peer_v": PEER_TOPK ** -0.5 * nrm(ks[19], (DEPTH, PEER_EXPERTS, D_MODEL), f32),
    }


def reference(x, c, rel_bias, norm_gain, w_ada, b_ada,
              even_w_in, even_b_forget, even_fox_qk_gain, even_diff_qk_gain,
              even_diff_lambda, even_diff_subln_gain, even_w_out,
              odd_w_qkv, odd_qk_gain, odd_w_out,
              peer_w_query, peer_sub_keys, peer_u, peer_v):
    cond = jax.nn.silu(c)
    for i in range(DEPTH):
        mod = cond @ w_ada[i] + b_ada[i]
        sh1, sc1, g1, sh2, sc2, g2 = jnp.split(mod, 6, axis=-1)
        h = modulate(rms_norm(x, norm_gain[i, 0]), sh1, sc1)
        j = i // 2
        if i % 2 == 0:
            lam_init = 0.8 - 0.6 * math.exp(-0.3 * i)
            y = even_mixer(h, even_w_in[j], even_b_forget[j], even_fox_qk_gain[j],
                           even_diff_qk_gain[j], even_diff_lambda[j], even_diff_subln_gain[j],
                           even_w_out[j], rel_bias[:, :DIFF_HEADS], lam_init)
        else:
            y = odd_mixer(h, odd_w_qkv[j], odd_qk_gain[j], odd_w_out[j], rel_bias)
        x = x + g1[:, None, :] * y.astype(x.dtype)
        h = modulate(rms_norm(x, norm_gain[i, 1]), sh2, sc2)
        y = peer_ffn(h, peer_w_query[i], peer_sub_keys[i], peer_u[i], peer_v[i])
        x = x + g2[:, None, :] * y.astype(x.dtype)
    return x
```

```python
import math
from contextlib import ExitStack
import numpy as np
import concourse.bass as bass
import concourse.mybir as mybir
from concourse.bass_utils import run_bass_kernel_spmd

F32 = mybir.dt.float32
BF16 = mybir.dt.bfloat16
U32 = mybir.dt.uint32
AF = mybir.ActivationFunctionType
ALU = mybir.AluOpType
AX = mybir.AxisListType

D = 2048
NCH = 16
EPS = 1e-6
NEG = -30000.0
SCALE = 128 ** -0.5
NEXP = 16384
SBUF_BCAST = False
C_C, C_BADA, C_GAIN, C_FOXG, C_DIFFG, C_LAM, C_SUBLN, C_ODDG, C_BF, C_ID, NCST = 0, 16, 208, 272, 274, 276, 280, 282, 284, 288, 448
C_IOTA, C_LO16 = 416, 432


class Ev:
    __slots__ = ("sem", "val", "key", "pe")

    def __init__(self, sem, val, key, pe=False):
        self.sem, self.val, self.key, self.pe = sem, val, key, pe


class Tok:
    __slots__ = ("name", "w", "r", "dsem", "dcnt", "dkey")

    def __init__(self, name):
        self.name = name
        self.w = None
        self.r = {}
        self.dsem = None
        self.dcnt = 0
        self.dkey = None


class Eng:
    def __init__(self, ctx, eng, name):
        self.ctx, self.eng, self.name = ctx, eng, name
        self.sem = ctx.nc.alloc_semaphore("s_" + name)
        self.key = "E" + name
        self.n = 0
        self.seen = {}

    def wait(self, ev):
        if ev is None:
            return
        if ev.pe and self.name == "pe":
            return
        if self.seen.get(ev.key, 0) >= ev.val:
            return
        self.eng.wait_ge(ev.sem, ev.val)
        self.seen[ev.key] = ev.val


class Ctx:
    def __init__(self, nc):
        self.nc = nc
        self.pe = Eng(self, nc.tensor, "pe")
        self.act = Eng(self, nc.scalar, "act")
        self.dve = Eng(self, nc.vector, "dve")
        self.pool = Eng(self, nc.gpsimd, "pool")
        self.sp = Eng(self, nc.sync, "sp")
        self.engs = dict(pe=self.pe, act=self.act, dve=self.dve, pool=self.pool, sp=self.sp)
        self.ndsem = 0
        self.dtoks = []

    def tok(self, name):
        return Tok(name)

    def _dsem(self, tok):
        if tok.dsem is None:
            tok.dsem = self.nc.alloc_semaphore("d%d" % self.ndsem)
            tok.dkey = "D%d" % self.ndsem
            self.ndsem += 1
            self.dtoks.append(tok)
        return tok.dsem

    def op(self, en, fn, reads=(), writes=()):
        E = self.engs[en]
        for t in reads:
            E.wait(t.w)
        for t in writes:
            E.wait(t.w)
            for ev in t.r.values():
                E.wait(ev)
        inst = fn(E.eng)
        E.n += 1
        inst.then_inc(E.sem, 1)
        ev = Ev(E.sem, E.n, E.key, pe=(en == "pe"))
        for t in writes:
            t.w = ev
            t.r = {}
        for t in reads:
            t.r[E.key] = ev
        return inst

    def dma(self, out_ap, in_ap, reads=(), writes=(), q="sp", fn=None, **kw):
        E = self.engs[q]
        for t in reads:
            E.wait(t.w)
        for t in writes:
            if not (t.w is not None and t.w.key == t.dkey):
                E.wait(t.w)
            for ev in t.r.values():
                E.wait(ev)
        t0 = writes[0]
        sem = self._dsem(t0)
        if fn is None:
            inst = E.eng.dma_start(out=out_ap, in_=in_ap, **kw)
        else:
            inst = fn(E.eng)
        inst.then_inc(sem, 16)
        t0.dcnt += 16
        ev = Ev(sem, t0.dcnt, t0.dkey)
        for t in writes:
            t.w = ev
            t.r = {}
        for t in reads:
            t.r[t0.dkey] = ev
        return inst

    def barrier(self):
        dve = self.dve
        for t in self.dtoks:
            dve.wait(Ev(t.dsem, t.dcnt, t.dkey))
        others = [self.pe, self.act, self.pool, self.sp]
        for E in others:
            if E.n:
                dve.wait(Ev(E.sem, E.n, E.key))
        if dve.n:
            dve.wait(Ev(dve.sem, dve.n, dve.key))
        inst = dve.eng.memset(self.bar_tile[:, 0:1], 0.0)
        dve.n += 1
        inst.then_inc(dve.sem, 1)
        ev = Ev(dve.sem, dve.n, dve.key)
        for E in others:
            E.wait(ev)


def mkap(ap, dims):
    return bass.AP(tensor=ap.tensor, offset=ap.offset, ap=[list(ap.ap[0])] + [list(d) for d in dims])


class KK:
    pass


_uid = [0]


def _nm(s):
    _uid[0] += 1
    return "%s_%d" % (s, _uid[0])


def SB(K, es, name, shape, dt):
    return es.enter_context(K.nc.sbuf_tensor(_nm(name), list(shape), dt))


def PS(K, es, name, shape=(128, 512), dt=F32):
    return es.enter_context(K.nc.psum_tensor(_nm(name), list(shape), dt))


def build(S, dbg=False, stop=99):
    nc = bass.Bass("TRN2", target_bir_lowering=False)
    K = KK()
    K.nc, K.S = nc, S
    K.c = c = Ctx(nc)
    K.TQ = TQ = min(512, S)
    K.NQ = S // TQ
    K.QB = TQ // 128
    K.NB = S // 128
    K.PADL = TQ - 128
    K.MW = K.PADL + S
    K.dbg = dbg

    def din(name, shape, dt=F32):
        return nc.dram_tensor(name, list(shape), dt, kind="ExternalInput").ap()

    def dint(name, shape, dt=F32):
        return nc.dram_tensor(name, list(shape), dt, kind="Internal").ap()

    K.d_xT = din("xT", [D, S])
    K.d_cst = din("cst", [128, NCST])
    K.d_wada = din("wada", [2, 24, 128, 16, 512])
    K.d_win = din("win", [48, 128, 16, 128])
    K.d_wff = din("wff", [128, 16, 8])
    K.d_wout0 = din("wout0", [16, 128, 16, 128])
    K.d_wqkv = din("wqkv", [48, 128, 16, 128])
    K.d_wout1 = din("wout1", [16, 128, 16, 128])
    K.d_wquery = din("wquery", [2, 16, 128, 16, 128])
    K.d_keys = din("keysT", [2, 128, 2, 128])
    K.d_uv = [din("uv0", [64, 128, 4 * D]), din("uv1", [64, 128, 4 * D])]
    _uvb = dint("uvb", [NEXP, 2 * D], BF16)
    K.d_uvb = [_uvb, _uvb]
    _tuvb = c.tok("uvb")
    K.t_uvb = [_tuvb, _tuvb]
    K.d_biasT = din("biasT", [16, 128, K.MW])
    K.d_multT = din("multT", [128, K.MW])
    K.d_triT = din("triT", [128, K.PADL + TQ])
    K.d_y = nc.dram_tensor("y", [D, S], F32, kind="ExternalOutput").ap()
    dscr = (lambda n, sh: nc.dram_tensor(n, list(sh), F32, kind="ExternalOutput").ap()) if dbg else dint
    K.d_xA = dscr("xA", [D, S])
    K.d_xB = dscr("xB", [D, S])
    K.d_xC = dscr("xC", [D, S])
    K.d_mixed = dint("mixed", [D, S], BF16)
    K.d_fsc = dint("fsc", [4, 8, S], BF16)
    K.d_h2 = dint("h2tok", [S, D], BF16)
    K.t_xA, K.t_xB, K.t_xC, K.t_y = c.tok("xA"), c.tok("xB"), c.tok("xC"), c.tok("y")
    K.t_mixed, K.t_fsc, K.t_h2 = c.tok("mixed"), c.tok("fsc"), c.tok("h2")

    with ExitStack() as pers:
        prologue(K, pers)
        convert_tables(K, 0)
        mixer(K, 0, K.d_xT, None, K.d_xA, K.t_xA)
        if stop >= 3:
            peer(K, 0, K.d_xA, K.t_xA, K.d_xB, K.t_xB)
        if stop >= 4:
            mixer(K, 1, K.d_xB, K.t_xB, K.d_xC, K.t_xC)
        if stop >= 5:
            peer(K, 1, K.d_xC, K.t_xC, K.d_y, K.t_y)
        c.barrier()
        for t in c.dtoks:
            c.sp.wait(Ev(t.dsem, t.dcnt, t.dkey))
    return nc


def prologue(K, pers):
    c, nc = K.c, K.nc
    c.bar_tile = SB(K, pers, "bar", [128, 2], F32)
    K.bc_reg = nc.gpsimd.to_reg(NEXP - 1)
    K.cst = SB(K, pers, "cst", [128, NCST], F32)
    K.t_cst = c.tok("cst")
    c.dma(K.cst[:], K.d_cst[:, :], writes=[K.t_cst])
    K.ident_f = K.cst[:, C_ID:C_ID + 128]
    K.ones_bf = SB(K, pers, "ones", [128, 128], BF16)
    K.t_ones = c.tok("ones")
    c.op("dve", lambda e: e.memset(K.ones_bf[:], 1.0), writes=[K.t_ones])
    K.ident_bf = SB(K, pers, "identb", [128, 128], BF16)
    K.t_identb = c.tok("identb")
    c.op("dve", lambda e: e.tensor_copy(out=K.ident_bf[:], in_=K.ident_f), reads=[K.t_cst], writes=[K.t_identb])
    K.mod = SB(K, pers, "mod", [128, 192], F32)
    K.t_mod = c.tok("mod")
    K.amod = SB(K, pers, "amod", [128, 64], F32)
    K.misc = SB(K, pers, "misc", [128, 16], F32)
    K.t_misc = c.tok("misc")
    K.wf = [SB(K, pers, "wf", [128, 16, 128], F32) for _ in range(2)]
    K.t_wf = [c.tok("wf0"), c.tok("wf1")]
    K.wb = [SB(K, pers, "wb", [128, 16, 256], BF16) for _ in range(2)]
    K.t_wb = [c.tok("wb0"), c.tok("wb1")]
    K.wcnt = 0

    with ExitStack() as es:
        cond = SB(K, es, "cond", [128, 16], F32)
        t_cond = c.tok("cond")
        c.op("act", lambda e: e.activation(out=cond[:], in_=K.cst[:, C_C:C_C + 16], func=AF.Silu),
             reads=[K.t_cst], writes=[t_cond])
        wad = [SB(K, es, "wad", [128, 16, 512], F32) for _ in range(2)]
        t_wad = [c.tok("wad0"), c.tok("wad1")]
        modps = PS(K, es, "modps")
        t_modps = c.tok("modps")
        condb = SB(K, es, "condb", [128, 16], BF16)
        t_condb = c.tok("condb")
        c.op("dve", lambda e: e.tensor_copy(out=condb[:], in_=cond[:]), reads=[t_cond], writes=[t_condb])
        wadb = [SB(K, es, "wadb", [128, 16, 512], BF16) for _ in range(2)]
        t_wadb = [c.tok("wadb0"), c.tok("wadb1")]
        for i in range(2):
            for g in range(24):
                s = (i * 24 + g) % 2
                c.dma(wad[s][:], K.d_wada[i, g], writes=[t_wad[s]])
                c.op("dve", lambda e, s=s: e.tensor_copy(out=wadb[s][:, 0:6, :], in_=wad[s][:, 0:6, :]),
                     reads=[t_wad[s]], writes=[t_wadb[s]])
                c.op("act", lambda e, s=s: e.activation(out=wadb[s][:, 6:11, :], in_=wad[s][:, 6:11, :], func=AF.Copy),
                     reads=[t_wad[s]], writes=[t_wadb[s]])
                c.op("pool", lambda e, s=s: e.tensor_copy(out=wadb[s][:, 11:16, :], in_=wad[s][:, 11:16, :]),
                     reads=[t_wad[s]], writes=[t_wadb[s]])
                for j in range(4):
                    nb = g * 4 + j
                    col = i * 96 + nb
                    for cc in range(NCH):
                        c.op("pe", lambda e, s=s, j=j, cc=cc, col=col: e.matmul(
                            modps[:, col:col + 1], lhsT=wadb[s][:, cc, j * 128:(j + 1) * 128],
                            rhs=condb[:, cc:cc + 1], start=(cc == 0), stop=(cc == NCH - 1)),
                            reads=[t_wadb[s], t_condb], writes=[t_modps])
        c.op("dve", lambda e: e.tensor_tensor(out=K.mod[:], in0=modps[:, 0:192], in1=K.cst[:, C_BADA:C_BADA + 192],
                                              op=ALU.add), reads=[t_modps, K.t_cst], writes=[K.t_mod])
        for i in range(2):
            for s in range(2):
                sc = K.mod[:, i * 96 + (1 if s == 0 else 4) * 16: i * 96 + (1 if s == 0 else 4) * 16 + 16]
                gn = K.cst[:, C_GAIN + (i * 2 + s) * 16: C_GAIN + (i * 2 + s) * 16 + 16]
                o = K.amod[:, (i * 2 + s) * 16:(i * 2 + s) * 16 + 16]
                c.op("dve", lambda e, sc=sc, gn=gn, o=o: e.scalar_tensor_tensor(
                    out=o, in0=sc, scalar=1.0, in1=gn, op0=ALU.add, op1=ALU.mult),
                    reads=[K.t_mod, K.t_cst], writes=[K.t_mod])
        M = K.misc
        c.op("dve", lambda e: e.tensor_scalar(out=M[:, 0:1], in0=K.cst[:, C_FOXG:C_FOXG + 1], scalar1=SCALE, scalar2=None,
                                              op0=ALU.mult), reads=[K.t_cst], writes=[K.t_misc])
        c.op("dve", lambda e: e.tensor_scalar(out=M[:, 1:2], in0=K.cst[:, C_DIFFG:C_DIFFG + 1], scalar1=SCALE, scalar2=None,
                                              op0=ALU.mult), reads=[K.t_cst], writes=[K.t_misc])
        c.op("dve", lambda e: e.tensor_scalar(out=M[:, 2:3], in0=K.cst[:, C_ODDG:C_ODDG + 1], scalar1=SCALE, scalar2=None,
                                              op0=ALU.mult), reads=[K.t_cst], writes=[K.t_misc])
        c.op("dve", lambda e: e.tensor_scalar(out=M[:, 3:4], in0=K.cst[:, C_BF:C_BF + 1], scalar1=-1.0, scalar2=None,
                                              op0=ALU.mult), reads=[K.t_cst], writes=[K.t_misc])
        lam_init = 0.8 - 0.6 * math.exp(-0.3 * 0)
        K.lam_init = lam_init
        lp = SB(K, es, "lp", [128, 2], F32)
        t_lp = c.tok("lp")
        c.op("dve", lambda e: e.tensor_tensor(out=lp[:, 0:1], in0=K.cst[:, C_LAM:C_LAM + 1], in1=K.cst[:, C_LAM + 1:C_LAM + 2],
                                              op=ALU.mult), reads=[K.t_cst], writes=[t_lp])
        c.op("dve", lambda e: e.tensor_tensor(out=lp[:, 1:2], in0=K.cst[:, C_LAM + 2:C_LAM + 3], in1=K.cst[:, C_LAM + 3:C_LAM + 4],
                                              op=ALU.mult), reads=[K.t_cst], writes=[t_lp])
        onesf = SB(K, es, "onesf", [128, 128], F32)
        t_onesf = c.tok("onesf")
        c.op("dve", lambda e: e.memset(onesf[:], 1.0), writes=[t_onesf])
        c.op("pe", lambda e: e.matmul(modps[:, 200:202], lhsT=onesf[:], rhs=lp[:, 0:2], start=True, stop=True),
             reads=[t_onesf, t_lp], writes=[t_modps])
        le = SB(K, es, "le", [128, 2], F32)
        t_le = c.tok("le")
        c.op("act", lambda e: e.activation(out=le[:], in_=modps[:, 200:202], func=AF.Exp), reads=[t_modps], writes=[t_le])
        c.op("dve", lambda e: e.scalar_tensor_tensor(out=M[:, 4:5], in0=le[:, 1:2], scalar=-lam_init, in1=le[:, 0:1],
                                                     op0=ALU.add, op1=ALU.subtract), reads=[t_le], writes=[K.t_misc])
        c.op("dve", lambda e: e.tensor_scalar(out=M[:, 5:7], in0=K.cst[:, C_SUBLN:C_SUBLN + 2], scalar1=(1.0 - lam_init),
                                              scalar2=None, op0=ALU.mult), reads=[K.t_cst], writes=[K.t_misc])
        c.barrier()


def convert_tables(K, layer):
    c = K.c
    with ExitStack() as es:
        NSL = 3
        W8 = 4 * D
        fin = [SB(K, es, "cvi", [128, W8], F32) for _ in range(NSL)]
        t_fin = [c.tok("cvi%d" % i) for i in range(NSL)]
        fout = [SB(K, es, "cvo", [128, W8], BF16) for _ in range(NSL)]
        t_fout = [c.tok("cvo%d" % i) for i in range(NSL)]
        cnt = 0
        if True:
            for blk in range(64):
                s = cnt % NSL
                cnt += 1
                c.dma(fin[s][:], K.d_uv[layer][blk], writes=[t_fin[s]])
                c.op("dve", lambda e, s=s: e.tensor_copy(out=fout[s][:, 0:3072], in_=fin[s][:, 0:3072]),
                     reads=[t_fin[s]], writes=[t_fout[s]])
                c.op("act", lambda e, s=s: e.activation(out=fout[s][:, 3072:6144], in_=fin[s][:, 3072:6144], func=AF.Copy),
                     reads=[t_fin[s]], writes=[t_fout[s]])
                c.op("pool", lambda e, s=s: e.tensor_copy(out=fout[s][:, 6144:W8], in_=fin[s][:, 6144:W8]),
                     reads=[t_fin[s]], writes=[t_fout[s]])
                base = K.d_uvb[layer]
                dst = bass.AP(tensor=base.tensor, offset=base.offset + blk * 128 * W8, ap=[[W8, 128], [1, W8]])
                c.dma(dst, fout[s][:], reads=[t_fout[s]], writes=[K.t_uvb[layer]])
        c.barrier()


class Conv:
    def __init__(self, K, es, layer):
        c = K.c
        self.K, self.layer, self.i, self.N = K, layer, 0, 256
        self.fin = [SB(K, es, "cvi", [128, D], F32) for _ in range(2)]
        self.t_fin = [c.tok("cvi0"), c.tok("cvi1")]
        self.fout = [SB(K, es, "cvo", [128, D], BF16) for _ in range(2)]
        self.t_fout = [c.tok("cvo0"), c.tok("cvo1")]

    def step(self, n):
        K, c = self.K, self.K.c
        for _ in range(n):
            if self.i >= self.N:
                return
            i = self.i
            self.i += 1
            s = i % 2
            sb_ = K.d_uv[self.layer]
            src = bass.AP(tensor=sb_.tensor, offset=sb_.offset + i * 128 * D, ap=[[D, 128], [1, D]])
            c.dma(self.fin[s][:], src, writes=[self.t_fin[s]])
            c.op("pool", lambda e, s=s: e.tensor_copy(out=self.fout[s][:], in_=self.fin[s][:]),
                 reads=[self.t_fin[s]], writes=[self.t_fout[s]])
            db_ = K.d_uvb[self.layer]
            dst = bass.AP(tensor=db_.tensor, offset=db_.offset + i * 128 * D, ap=[[D, 128], [1, D]])
            c.dma(dst, self.fout[s][:], reads=[self.t_fout[s]], writes=[K.t_uvb[self.layer]])

    def finish(self):
        self.step(self.N)

def load_unit(K, src_ap, dv_off=0):
    c = K.c
    s = K.wcnt % 2
    c.dma(K.wf[s][:], src_ap, writes=[K.t_wf[s]])
    return s


def get_weights(K, srcs):
    c = K.c
    ws = K.wcnt % 2
    wb, t_wb = K.wb[ws], K.t_wb[ws]
    for j, src in enumerate(srcs):
        fs = (K.wcnt * 2 + j) % 2
        c.dma(K.wf[fs][:], src, writes=[K.t_wf[fs]])
        eng = "dve" if (j % 2 == 0) else "pool"
        c.op(eng, lambda e, fs=fs, j=j: e.tensor_copy(out=wb[:, :, j * 128:(j + 1) * 128], in_=K.wf[fs][:]),
             reads=[K.t_wf[fs]], writes=[t_wb])
    K.wcnt += 1
    return wb, t_wb


def norm_mod(K, xsrc, t_xsrc, i, s, hT, t_hT):
    c = K.c
    S, TQ, NQ = K.S, K.TQ, K.NQ
    xv = xsrc.rearrange("(c p) t -> p c t", p=128)
    acol = (i * 2 + s) * 16
    shcol = i * 96 + (0 if s == 0 else 3) * 16
    with ExitStack() as es:
        xb = SB(K, es, "xb", [128, 16, TQ], F32)
        t_xb = c.tok("xb")
        sq = [SB(K, es, "sq", [128, TQ], BF16) for _ in range(2)]
        t_sq = [c.tok("sq0"), c.tok("sq1")]
        rs = SB(K, es, "rs", [128, TQ], F32)
        t_rs = c.tok("rs")
        rinv = SB(K, es, "rinv", [128, TQ], F32)
        t_rinv = c.tok("rinv")
        tmp = [SB(K, es, "tmp", [128, TQ], F32) for _ in range(2)]
        t_tmp = [c.tok("tmp0"), c.tok("tmp1")]
        ssq = PS(K, es, "ssq")
        t_ssq = c.tok("ssq")
        for tq in range(NQ):
            sl = slice(tq * TQ, (tq + 1) * TQ)
            c.dma(xb[:], xv[:, :, sl], reads=([t_xsrc] if t_xsrc is not None else []), writes=[t_xb])
            for cc in range(NCH):
                k = cc % 2
                c.op("act", lambda e, cc=cc, k=k: e.activation(out=sq[k][:], in_=xb[:, cc, :], func=AF.Square),
                     reads=[t_xb], writes=[t_sq[k]])
                c.op("pe", lambda e, cc=cc, k=k: e.matmul(ssq[:, 0:TQ], lhsT=K.ones_bf[:], rhs=sq[k][:],
                                                         start=(cc == 0), stop=(cc == NCH - 1)),
                     reads=[t_sq[k], K.t_ones], writes=[t_ssq])
            c.op("act", lambda e: e.activation(out=rs[:], in_=ssq[:, 0:TQ], func=AF.Sqrt, scale=1.0 / D, bias=EPS),
                 reads=[t_ssq], writes=[t_rs])
            c.op("dve", lambda e: e.reciprocal(out=rinv[:], in_=rs[:]), reads=[t_rs], writes=[t_rinv])
            for cc in range(NCH):
                k = cc % 2
                c.op("dve", lambda e, cc=cc, k=k: e.scalar_tensor_tensor(
                    out=tmp[k][:], in0=xb[:, cc, :], scalar=K.amod[:, acol + cc:acol + cc + 1], in1=rinv[:],
                    op0=ALU.mult, op1=ALU.mult), reads=[t_xb, t_rinv, K.t_mod], writes=[t_tmp[k]])
                c.op("act", lambda e, cc=cc, k=k: e.activation(
                    out=hT[:, cc, sl], in_=tmp[k][:], func=AF.Identity, bias=K.mod[:, shcol + cc:shcol + cc + 1], scale=1.0),
                    reads=[t_tmp[k], K.t_mod], writes=[t_hT])
        c.barrier()


def proj_fm(K, wb, t_wb, woff, hT, t_hT, pj, t_pj, post):
    c = K.c
    for tq in range(K.NQ):
        k = K.pjcnt % 2
        K.pjcnt += 1
        sl = slice(tq * K.TQ, (tq + 1) * K.TQ)
        for cc in range(NCH):
            c.op("pe", lambda e, cc=cc, k=k: e.matmul(pj[k][:, 0:K.TQ], lhsT=wb[:, cc, woff:woff + 128], rhs=hT[:, cc, sl],
                                                     start=(cc == 0), stop=(cc == NCH - 1)),
                 reads=[t_wb, t_hT], writes=[t_pj[k]])
        post(pj[k], t_pj[k], tq, sl)


def qknorm_post(K, W, gain_ap, dst, t_dst):
    c = K.c
    TQ = K.TQ

    def post(ps, t_ps, tq, sl):
        c.op("act", lambda e: e.activation(out=W.sq[:], in_=ps[:, 0:TQ], func=AF.Square), reads=[t_ps], writes=[W.t_sq])
        c.op("pe", lambda e: e.matmul(W.ss[:, 0:TQ], lhsT=K.ones_bf[:], rhs=W.sq[:], start=True, stop=True),
             reads=[W.t_sq, K.t_ones], writes=[W.t_ss])
        c.op("act", lambda e: e.activation(out=W.rs[:], in_=W.ss[:, 0:TQ], func=AF.Sqrt, scale=1.0 / 128, bias=EPS),
             reads=[W.t_ss], writes=[W.t_rs])
        c.op("dve", lambda e: e.reciprocal(out=W.rinv[:], in_=W.rs[:]), reads=[W.t_rs], writes=[W.t_rinv])
        c.op("dve", lambda e: e.scalar_tensor_tensor(out=dst[:, sl], in0=ps[:, 0:TQ], scalar=gain_ap, in1=W.rinv[:],
                                                     op0=ALU.mult, op1=ALU.mult),
             reads=[t_ps, W.t_rinv, K.t_misc, K.t_cst], writes=[t_dst])
    return post


def proj_tm(K, wb, t_wb, dv, hT, t_hT, pj, t_pj, V, t_V):
    c = K.c
    per = 512 // dv
    for tb in range(K.NB):
        k = K.pjcnt % 2
        if tb % per == 0:
            K.pjcnt += 1
            k = (K.pjcnt - 1) % 2
            cur = (pj[k], t_pj[k])
        o = (tb % per) * dv
        for cc in range(NCH):
            c.op("pe", lambda e, cc=cc, cur=cur, o=o: e.matmul(cur[0][:, o:o + dv], lhsT=hT[:, cc, tb * 128:(tb + 1) * 128],
                                                               rhs=wb[:, cc, 0:dv], start=(cc == 0), stop=(cc == NCH - 1)),
                 reads=[t_wb, t_hT], writes=[cur[1]])
        if tb % per == per - 1 or tb == K.NB - 1:
            n = (tb % per) + 1
            tb0 = tb - (tb % per)
            c.op("act", lambda e, cur=cur, n=n, tb0=tb0: e.activation(
                out=V[:, tb0:tb0 + n, 0:dv], in_=cur[0][:, 0:n * dv].rearrange("p (a b) -> p a b", b=dv), func=AF.Copy),
                reads=[cur[1]], writes=[t_V])


def attn_core(K, W, qT, t_q, kT, t_k, V, t_V, ndv, mask, t_mask, fin, AB=None):
    c = K.c
    TQ, NQ, QB, PADL = K.TQ, K.NQ, K.QB, K.PADL
    for qc in range(NQ):
        sl = slice(qc * TQ, (qc + 1) * TQ)
        nkb = (qc + 1) * QB
        for kb in range(nkb):
            it = W.it
            W.it += 1
            st, t_st = W.st[it % 2], W.t_st[it % 2]
            ks = slice(kb * 128, (kb + 1) * 128)
            c.op("pe", lambda e, st=st: e.matmul(st[:, 0:TQ], lhsT=kT[:, ks], rhs=qT[:, sl], start=True, stop=(AB is None)),
                 reads=[t_k, t_q], writes=[t_st])
            diag = kb >= qc * QB
            J0 = qc * TQ - kb * 128 + PADL
            if AB is not None:
                A, B, t_AB = AB
                c.op("pe", lambda e, st=st: e.matmul(st[:, 0:TQ], lhsT=A[:, ks], rhs=B[:, sl], start=False, stop=(not diag)),
                     reads=[t_AB], writes=[t_st])
                if diag:
                    c.op("pe", lambda e, st=st: e.matmul(st[:, 0:TQ], lhsT=K.ident_bf[:], rhs=K.tri[:, J0:J0 + TQ],
                                                         start=False, stop=True),
                         reads=[K.t_identb, K.t_tri], writes=[t_st])
            ptb, t_ptb = W.ptb[it % 2], W.t_ptb[it % 2]
            if mask is not None:
                ptf, t_ptf = W.ptf[it % 2], W.t_ptf[it % 2]
                c.op("act", lambda e, st=st, ptf=ptf: e.activation(out=ptf[:], in_=st[:, 0:TQ], func=AF.Exp),
                     reads=[t_st], writes=[t_ptf])
                mk, t_mk = mask[:, J0:J0 + TQ], t_mask
                c.op("dve", lambda e, ptf=ptf, ptb=ptb, mk=mk: e.tensor_tensor(out=ptb[:], in0=ptf[:], in1=mk, op=ALU.mult),
                     reads=[t_ptf, t_mk], writes=[t_ptb])
            else:
                c.op("act", lambda e, st=st, ptb=ptb: e.activation(out=ptb[:], in_=st[:, 0:TQ], func=AF.Exp),
                     reads=[t_st], writes=[t_ptb])
            for j in range(ndv):
                c.op("pe", lambda e, j=j, ptb=ptb: e.matmul(W.o[j][:, 0:TQ], lhsT=V[:, kb, j * 128:(j + 1) * 128], rhs=ptb[:],
                                                            start=(kb == 0), stop=(kb == nkb - 1)),
                     reads=[t_V, t_ptb], writes=[W.t_o[j]])
            c.op("pe", lambda e, ptb=ptb: e.matmul(W.den[:, 0:TQ], lhsT=K.ones_bf[:], rhs=ptb[:],
                                                   start=(kb == 0), stop=(kb == nkb - 1)),
                 reads=[K.t_ones, t_ptb], writes=[W.t_den])
        c.op("dve", lambda e: e.reciprocal(out=W.rden[:], in_=W.den[:, 0:TQ]), reads=[W.t_den], writes=[W.t_rden])
        fin(qc, sl)


class WS:
    pass


def attn_ws(K, es):
    c = K.c
    TQ, S = K.TQ, K.S
    W = WS()
    W.it = 0
    W.pj = [PS(K, es, "pj") for _ in range(2)]
    W.t_pj = [c.tok("pj0"), c.tok("pj1")]
    W.ss = PS(K, es, "ss")
    W.t_ss = c.tok("ss")
    W.st = [PS(K, es, "st") for _ in range(2)]
    W.t_st = [c.tok("st0"), c.tok("st1")]
    W.o = [PS(K, es, "o") for _ in range(2)]
    W.t_o = [c.tok("o0"), c.tok("o1")]
    W.den = PS(K, es, "den")
    W.t_den = c.tok("den")
    W.sq = SB(K, es, "sqn", [128, TQ], BF16)
    W.t_sq = c.tok("sqn")
    W.rs = SB(K, es, "rsn", [128, TQ], F32)
    W.t_rs = c.tok("rsn")
    W.rinv = SB(K, es, "rinvn", [128, TQ], F32)
    W.t_rinv = c.tok("rinvn")
    W.ptf = [SB(K, es, "ptf", [128, TQ], F32) for _ in range(2)]
    W.t_ptf = [c.tok("ptf0"), c.tok("ptf1")]
    W.ptb = [SB(K, es, "ptb", [128, TQ], BF16) for _ in range(2)]
    W.t_ptb = [c.tok("ptb0"), c.tok("ptb1")]
    W.rden = SB(K, es, "rden", [128, TQ], F32)
    W.t_rden = c.tok("rden")
    W.qT = SB(K, es, "qT", [128, S], BF16)
    W.t_qT = c.tok("qT")
    W.kT = SB(K, es, "kT", [128, S], BF16)
    W.t_kT = c.tok("kT")
    W.V = SB(K, es, "V", [128, K.NB, 256], BF16)
    W.t_V = c.tok("V")
    W.ob = SB(K, es, "ob", [128, 2, S], BF16)
    W.t_ob = c.tok("ob")
    W.G = SB(K, es, "G", [128, K.MW], F32)
    W.t_G = c.tok("G")
    return W


def mixer(K, layer, xin, t_xin, xout, t_xout):
    c = K.c
    S, TQ, NQ, NB = K.S, K.TQ, K.NQ, K.NB
    K.pjcnt = 0
    with ExitStack() as es:
        hT = SB(K, es, "hT", [128, NCH, S], BF16)
        t_hT = c.tok("hT")
        norm_mod(K, xin, t_xin, layer, 0, hT, t_hT)
        W = attn_ws(K, es)
        trif = SB(K, es, "trif", [128, K.PADL + TQ], F32)
        t_trif = c.tok("trif")
        c.dma(trif[:], K.d_triT[:, :], writes=[t_trif])
        K.tri = SB(K, es, "tri", [128, K.PADL + TQ], BF16)
        K.t_tri = c.tok("tri")
        c.op("dve", lambda e: e.tensor_copy(out=K.tri[:], in_=trif[:]), reads=[t_trif], writes=[K.t_tri])
        if layer == 0:
            mixer0_heads(K, es, W, hT, t_hT)
        else:
            mixer1_heads(K, es, W, hT, t_hT)
        c.barrier()
    out_proj(K, K.d_wout0 if layer == 0 else K.d_wout1, xin, t_xin, xout, t_xout, layer * 96 + 2 * 16)


def store_mixed(K, W, j, row0):
    K.c.dma(K.d_mixed[row0:row0 + 128, :], W.ob[:, j, :], reads=[W.t_ob], writes=[K.t_mixed])


def mixer0_heads(K, es, W, hT, t_hT):
    c = K.c
    S, TQ, NQ, NB = K.S, K.TQ, K.NQ, K.NB
    M = K.misc
    with ExitStack() as e2:
        wffb = SB(K, e2, "wffb", [128, 16, 8], BF16)
        wfff = SB(K, e2, "wfff", [128, 16, 8], F32)
        t_wff = c.tok("wff")
        c.dma(wfff[:], K.d_wff[:, :, :], writes=[t_wff])
        t_wffb = c.tok("wffb")
        c.op("dve", lambda e: e.tensor_copy(out=wffb[:], in_=wfff[:]), reads=[t_wff], writes=[t_wffb])
        ef = SB(K, e2, "ef", [8, S], F32)
        t_ef = c.tok("ef")
        lf = SB(K, e2, "lf", [8, S], F32)
        t_lf = c.tok("lf")
        Cf = SB(K, e2, "Cf", [8, S], F32)
        t_Cf = c.tok("Cf")
        on8 = SB(K, e2, "on8", [8, S], F32)
        t_on8 = c.tok("on8")
        c.op("pool", lambda e: e.memset(on8[:], 1.0), writes=[t_on8])
        for tq in range(NQ):
            sl = slice(tq * TQ, (tq + 1) * TQ)
            for cc in range(NCH):
                c.op("pe", lambda e, cc=cc: e.matmul(W.ss[0:8, 0:TQ], lhsT=wffb[:, cc, :], rhs=hT[:, cc, sl],
                                                     start=(cc == 0), stop=(cc == NCH - 1)),
                     reads=[t_wffb, t_hT], writes=[W.t_ss])
            c.op("act", lambda e: e.activation(out=ef[:, sl], in_=W.ss[0:8, 0:TQ], func=AF.Exp, scale=-1.0, bias=M[0:8, 3:4]),
                 reads=[W.t_ss, K.t_misc], writes=[t_ef])
        c.op("act", lambda e: e.activation(out=lf[:], in_=ef[:], func=AF.Ln, scale=1.0, bias=1.0), reads=[t_ef], writes=[t_lf])
        c.op("dve", lambda e: e.tensor_tensor_scan(out=Cf[:], data0=on8[:], data1=lf[:], initial=0.0, op0=ALU.mult, op1=ALU.add),
             reads=[t_on8, t_lf], writes=[t_Cf])
        hl = SB(K, e2, "hl", [8, 4, S], BF16)
        t_hl = c.tok("hl")
        c.op("dve", lambda e: e.tensor_copy(out=hl[:, 0, :], in_=Cf[:]), reads=[t_Cf], writes=[t_hl])
        c.op("dve", lambda e: e.tensor_tensor(out=lf[:], in0=Cf[:], in1=hl[:, 0, :], op=ALU.subtract),
             reads=[t_Cf, t_hl], writes=[t_lf])
        c.op("dve", lambda e: e.tensor_copy(out=hl[:, 1, :], in_=lf[:]), reads=[t_lf], writes=[t_hl])
        c.op("dve", lambda e: e.tensor_scalar(out=hl[:, 2:4, :], in0=hl[:, 0:2, :], scalar1=-1.0, scalar2=None, op0=ALU.mult),
             reads=[t_hl], writes=[t_hl])
        for r in range(4):
            c.dma(K.d_fsc[r], hl[:, r, :], reads=[t_hl], writes=[K.t_fsc])
        c.barrier()
    A = SB(K, es, "Afox", [128, S], BF16)
    B = SB(K, es, "Bfox", [128, S], BF16)
    t_AB = c.tok("AB")
    c.op("pool", lambda e: e.memset(A[:], 0.0), writes=[t_AB])
    c.op("pool", lambda e: e.memset(B[:], 0.0), writes=[t_AB])
    c.op("pool", lambda e: e.memset(A[0:4, :], 1.0), writes=[t_AB])
    c.op("pool", lambda e: e.memset(B[0:4, :], 1.0), writes=[t_AB])

    for hh in range(8):
        wq, t_wq = get_weights(K, [K.d_win[hh]])
        proj_fm(K, wq, t_wq, 0, hT, t_hT, W.pj, W.t_pj, qknorm_post(K, W, M[:, 0:1], W.qT, W.t_qT))
        wk, t_wk = get_weights(K, [K.d_win[8 + hh]])
        proj_fm(K, wk, t_wk, 0, hT, t_hT, W.pj, W.t_pj, qknorm_post(K, W, K.cst[:, C_FOXG + 1:C_FOXG + 2], W.kT, W.t_kT))
        wv, t_wv = get_weights(K, [K.d_win[16 + hh]])
        proj_tm(K, wv, t_wv, 128, hT, t_hT, W.pj, W.t_pj, W.V, W.t_V)
        c.dma(A[0:2, :], K.d_fsc[0:2, hh, :], reads=[K.t_fsc], writes=[t_AB])
        c.dma(B[2:4, :], K.d_fsc[2:4, hh, :], reads=[K.t_fsc], writes=[t_AB])

        def fin(qc, sl):
            c.op("dve", lambda e: e.tensor_tensor(out=W.ob[:, 0, sl], in0=W.o[0][:, 0:TQ], in1=W.rden[:], op=ALU.mult),
                 reads=[W.t_o[0], W.t_rden], writes=[W.t_ob])
        attn_core(K, W, W.qT, W.t_qT, W.kT, W.t_kT, W.V, W.t_V, 1, None, None, fin, AB=(A, B, t_AB))
        store_mixed(K, W, 0, hh * 128)

    O = [SB(K, es, "Od", [128, 2, S], F32) for _ in range(2)]
    t_O = [c.tok("Od0"), c.tok("Od1")]
    for hh in range(4):
        c.dma(W.G[:], K.d_biasT[hh], writes=[W.t_G])
        c.op("act", lambda e: e.activation(out=W.G[:], in_=W.G[:], func=AF.Exp), reads=[W.t_G], writes=[W.t_G])
        wv, t_wv = get_weights(K, [K.d_win[40 + 2 * hh], K.d_win[41 + 2 * hh]])
        proj_tm(K, wv, t_wv, 256, hT, t_hT, W.pj, W.t_pj, W.V, W.t_V)
        for m in range(2):
            wq, t_wq = get_weights(K, [K.d_win[24 + 2 * hh + m]])
            proj_fm(K, wq, t_wq, 0, hT, t_hT, W.pj, W.t_pj, qknorm_post(K, W, M[:, 1:2], W.qT, W.t_qT))
            wk, t_wk = get_weights(K, [K.d_win[32 + 2 * hh + m]])
            proj_fm(K, wk, t_wk, 0, hT, t_hT, W.pj, W.t_pj,
                    qknorm_post(K, W, K.cst[:, C_DIFFG + 1:C_DIFFG + 2], W.kT, W.t_kT))

            def fin(qc, sl, m=m):
                for j in range(2):
                    c.op("dve", lambda e, j=j: e.tensor_tensor(out=O[m][:, j, sl], in0=W.o[j][:, 0:TQ], in1=W.rden[:], op=ALU.mult),
                         reads=[W.t_o[j], W.t_rden], writes=[t_O[m]])
            attn_core(K, W, W.qT, W.t_qT, W.kT, W.t_kT, W.V, W.t_V, 2, W.G, W.t_G, fin)
        for tq in range(NQ):
            sl = slice(tq * TQ, (tq + 1) * TQ)
            for j in range(2):
                c.op("dve", lambda e, j=j: e.scalar_tensor_tensor(out=O[0][:, j, sl], in0=O[1][:, j, sl], scalar=M[:, 4:5],
                                                                  in1=O[0][:, j, sl], op0=ALU.mult, op1=ALU.add),
                     reads=[t_O[1], t_O[0], K.t_misc], writes=[t_O[0]])
                c.op("act", lambda e, j=j: e.activation(out=W.sq[:], in_=O[0][:, j, sl], func=AF.Square),
                     reads=[t_O[0]], writes=[W.t_sq])
                c.op("pe", lambda e, j=j: e.matmul(W.ss[:, 0:TQ], lhsT=K.ones_bf[:], rhs=W.sq[:], start=(j == 0), stop=(j == 1)),
                     reads=[W.t_sq, K.t_ones], writes=[W.t_ss])
            c.op("act", lambda e: e.activation(out=W.rs[:], in_=W.ss[:, 0:TQ], func=AF.Sqrt, scale=1.0 / 256, bias=EPS),
                 reads=[W.t_ss], writes=[W.t_rs])
            c.op("dve", lambda e: e.reciprocal(out=W.rinv[:], in_=W.rs[:]), reads=[W.t_rs], writes=[W.t_rinv])
            for j in range(2):
                c.op("dve", lambda e, j=j: e.scalar_tensor_tensor(out=W.ob[:, j, sl], in0=O[0][:, j, sl], scalar=M[:, 5 + j:6 + j],
                                                                  in1=W.rinv[:], op0=ALU.mult, op1=ALU.mult),
                     reads=[t_O[0], W.t_rinv, K.t_misc], writes=[W.t_ob])
        for j in range(2):
            store_mixed(K, W, j, 1024 + hh * 256 + j * 128)


def mixer1_heads(K, es, W, hT, t_hT):
    c = K.c
    S, TQ, NQ, NB = K.S, K.TQ, K.NQ, K.NB
    M = K.misc
    mult = SB(K, es, "mult", [128, K.MW], F32)
    t_mult = c.tok("mult")
    c.dma(mult[:], K.d_multT[:, :], writes=[t_mult])
    cv = Conv(K, es, 1)
    for hh in range(16):
        c.dma(W.G[:], K.d_biasT[hh], writes=[W.t_G])
        c.op("act", lambda e: e.activation(out=W.G[:], in_=W.G[:], func=AF.Exp), reads=[W.t_G], writes=[W.t_G])
        c.op("pool", lambda e: e.tensor_tensor(out=W.G[:], in0=W.G[:], in1=mult[:], op=ALU.mult),
             reads=[W.t_G, t_mult], writes=[W.t_G])
        wq, t_wq = get_weights(K, [K.d_wqkv[hh]])
        proj_fm(K, wq, t_wq, 0, hT, t_hT, W.pj, W.t_pj, qknorm_post(K, W, M[:, 2:3], W.qT, W.t_qT))
        wk, t_wk = get_weights(K, [K.d_wqkv[16 + hh]])
        proj_fm(K, wk, t_wk, 0, hT, t_hT, W.pj, W.t_pj, qknorm_post(K, W, K.cst[:, C_ODDG + 1:C_ODDG + 2], W.kT, W.t_kT))
        wv, t_wv = get_weights(K, [K.d_wqkv[32 + hh]])
        proj_tm(K, wv, t_wv, 128, hT, t_hT, W.pj, W.t_pj, W.V, W.t_V)

        def fin(qc, sl):
            c.op("dve", lambda e: e.tensor_tensor(out=W.ob[:, 0, sl], in0=W.o[0][:, 0:TQ], in1=W.rden[:], op=ALU.mult),
                 reads=[W.t_o[0], W.t_rden], writes=[W.t_ob])
        attn_core(K, W, W.qT, W.t_qT, W.kT, W.t_kT, W.V, W.t_V, 1, W.G, W.t_G, fin)
        store_mixed(K, W, 0, hh * 128)
        cv.step(16)
    cv.finish()


def out_proj(K, wsrc, xin, t_xin, xout, t_xout, gcol):
    c = K.c
    S, TQ, NQ = K.S, K.TQ, K.NQ
    K.pjcnt = 0
    with ExitStack() as es:
        mT = SB(K, es, "mT", [128, NCH, S], BF16)
        t_mT = c.tok("mT")
        mv = K.d_mixed.rearrange("(c p) t -> p c t", p=128)
        for cc in range(NCH):
            c.dma(mT[:, cc, :], mv[:, cc, :], reads=[K.t_mixed], writes=[t_mT])
        pj = [PS(K, es, "pjo") for _ in range(2)]
        t_pj = [c.tok("pjo0"), c.tok("pjo1")]
        xc = [SB(K, es, "xc", [128, TQ], F32) for _ in range(2)]
        t_xc = [c.tok("xc0"), c.tok("xc1")]
        xo = [SB(K, es, "xo", [128, TQ], F32) for _ in range(2)]
        t_xo = [c.tok("xo0"), c.tok("xo1")]
        it = 0
        for nb in range(16):
            wb, t_wb = get_weights(K, [wsrc[nb]])
            rows = slice(nb * 128, (nb + 1) * 128)

            def post(ps, t_ps, tq, sl, nb=nb, rows=rows):
                nonlocal it
                k = it % 2
                it += 1
                c.dma(xc[k][:], xin[rows, sl], reads=([t_xin] if t_xin is not None else []), writes=[t_xc[k]])
                c.op("dve", lambda e: e.scalar_tensor_tensor(out=xo[k][:], in0=ps[:, 0:TQ], scalar=K.mod[:, gcol + nb:gcol + nb + 1],
                                                             in1=xc[k][:], op0=ALU.mult, op1=ALU.add),
                     reads=[t_ps, t_xc[k], K.t_mod], writes=[t_xo[k]])
                c.dma(xout[rows, sl], xo[k][:], reads=[t_xo[k]], writes=[t_xout])
            proj_fm(K, wb, t_wb, 0, mT, t_mT, pj, t_pj, post)
        c.barrier()


def peer(K, layer, xin, t_xin, xout, t_xout):
    c = K.c
    S, TQ, NQ, NB, QB = K.S, K.TQ, K.NQ, K.NB, K.QB
    K.pjcnt = 0
    gcol = layer * 96 + 5 * 16
    with ExitStack() as pe_s:
        idxT = SB(K, pe_s, "idxT", [128, S], U32)
        t_idxT = c.tok("idxT")
        gT = SB(K, pe_s, "gT", [128, S], F32)
        t_gT = c.tok("gT")
        with ExitStack() as es:
            hT = SB(K, es, "hT2", [128, NCH, S], BF16)
            t_hT = c.tok("hT2")
            norm_mod(K, xin, t_xin, layer, 1, hT, t_hT)
            keys = SB(K, es, "keys", [128, 2, 128], F32)
            t_keys = c.tok("keys")
            c.dma(keys[:], K.d_keys[layer], writes=[t_keys])
            pj = [PS(K, es, "pjq") for _ in range(2)]
            t_pj = [c.tok("pjq0"), c.tok("pjq1")]
            scps = [PS(K, es, "scps") for _ in range(4)]
            t_scps = [c.tok("scps%d" % i) for i in range(4)]
            tps = [PS(K, es, "tps", [128, 1024], BF16), PS(K, es, "tps2")]
            t_tps = [c.tok("tpsA"), c.tok("tpsB")]
            qf = SB(K, es, "qf", [128, 16, TQ], F32)
            t_qf = c.tok("qf")
            sc = SB(K, es, "sc", [128, 2048], F32)
            t_sc = c.tok("sc")
            sc2 = SB(K, es, "sc2", [128, 256], F32)
            t_sc2 = c.tok("sc2")
            tops = SB(K, es, "tops", [128, 16, 16], F32)
            t_tops = c.tok("tops")
            tiu = SB(K, es, "tiu", [128, 16, 16], U32)
            t_tiu = c.tok("tiu")
            topi = SB(K, es, "topi", [128, 16, 16], F32)
            t_topi = c.tok("topi")
            cs = SB(K, es, "cs", [128, 256], F32)
            t_cs = c.tok("cs")
            ci = SB(K, es, "ci", [128, 256], F32)
            t_ci = c.tok("ci")
            best = SB(K, es, "best", [128, 8, 16], F32)
            t_best = c.tok("best")
            eq3 = SB(K, es, "eq3", [128, 16, 256], F32)
            t_eq3 = c.tok("eq3")
            idxf = SB(K, es, "idxf", [128, 128], F32)
            t_idxf = c.tok("idxf")
            posu = SB(K, es, "posu", [128, 8, 16], U32)
            t_posu = c.tok("posu")
            posf = SB(K, es, "posf", [128, 5, 128], F32)
            t_posf = c.tok("posf")
            eg = SB(K, es, "eg", [128, 8, 16], F32)
            t_eg = c.tok("eg")
            gates = SB(K, es, "gates", [128, 128], F32)
            t_gates = c.tok("gates")
            sm = SB(K, es, "sm", [128, 32], F32)
            t_sm = c.tok("sm")
            h2b = [SB(K, es, "h2b", [128, D], BF16) for _ in range(2)]
            t_h2b = [c.tok("h2b0"), c.tok("h2b1")]

            for tq in range(NQ):
                sl = slice(tq * TQ, (tq + 1) * TQ)
                for u in range(16):
                    wb, t_wb = get_weights(K, [K.d_wquery[layer, u]])

                    def post(ps, t_ps, tq_, sl_, u=u):
                        c.op("act", lambda e: e.activation(out=qf[:, u, :], in_=ps[:, 0:TQ], func=AF.Copy),
                             reads=[t_ps], writes=[t_qf])
                    k = K.pjcnt % 2
                    K.pjcnt += 1
                    for cc in range(NCH):
                        c.op("pe", lambda e, cc=cc, k=k, wb=wb: e.matmul(pj[k][:, 0:TQ], lhsT=wb[:, cc, 0:128], rhs=hT[:, cc, sl],
                                                                         start=(cc == 0), stop=(cc == NCH - 1)),
                             reads=[t_wb, t_hT], writes=[t_pj[k]])
                    post(pj[k], t_pj[k], tq, sl)
                for tbl in range(QB):
                    tb = tq * QB + tbl
                    ts = slice(tbl * 128, (tbl + 1) * 128)
                    tsg = slice(tb * 128, (tb + 1) * 128)
                    hb_ = h2b[tb % 2]
                    for half in range(2):
                        for cc8 in range(8):
                            cc = half * 8 + cc8
                            c.op("pe", lambda e, cc=cc, cc8=cc8: e.transpose(out=tps[0][:, cc8 * 128:(cc8 + 1) * 128],
                                                                             in_=hT[:, cc, tsg], identity=K.ident_bf[:]),
                                 reads=[t_hT, K.t_identb], writes=[t_tps[0]])
                        c.op("act", lambda e, half=half, hb_=hb_: e.activation(out=hb_[:, half * 1024:(half + 1) * 1024],
                                                                               in_=tps[0][:, 0:1024], func=AF.Copy),
                             reads=[t_tps[0]], writes=[t_h2b[tb % 2]])
                    c.dma(K.d_h2[tsg, :], hb_[:], reads=[t_h2b[tb % 2]], writes=[K.t_h2])
                    for u in range(16):
                        c.op("pe", lambda e, u=u: e.matmul(scps[u // 4][:, (u % 4) * 128:(u % 4 + 1) * 128], lhsT=qf[:, u, ts],
                                                           rhs=keys[:, u % 2, :], start=True, stop=True),
                             reads=[t_qf, t_keys], writes=[t_scps[u // 4]])
                    for b4 in range(4):
                        c.op("act", lambda e, b4=b4: e.activation(out=sc[:, b4 * 512:(b4 + 1) * 512], in_=scps[b4][:, :], func=AF.Copy),
                             reads=[t_scps[b4]], writes=[t_sc])
                    for u in range(16):
                        su = sc[:, u * 128:(u + 1) * 128]
                        c.op("dve", lambda e, u=u, su=su: e.max(out=tops[:, u, 0:8], in_=su), reads=[t_sc], writes=[t_tops])
                        c.op("dve", lambda e, u=u, su=su: e.max_index(out=tiu[:, u, 0:8], in_max=tops[:, u, 0:8], in_values=su),
                             reads=[t_sc, t_tops], writes=[t_tiu])
                        c.op("dve", lambda e, u=u, su=su: e.match_replace(out=sc2[:, 0:128], in_to_replace=tops[:, u, 0:8],
                                                                          in_values=su, imm_value=-1e30),
                             reads=[t_sc, t_tops], writes=[t_sc2])
                        c.op("dve", lambda e, u=u: e.max(out=tops[:, u, 8:16], in_=sc2[:, 0:128]), reads=[t_sc2], writes=[t_tops])
                        c.op("dve", lambda e, u=u: e.max_index(out=tiu[:, u, 8:16], in_max=tops[:, u, 8:16], in_values=sc2[:, 0:128]),
                             reads=[t_sc2, t_tops], writes=[t_tiu])
                    c.op("dve", lambda e: e.tensor_copy(out=topi[:], in_=tiu[:]), reads=[t_tiu], writes=[t_topi])
                    for h in range(8):
                        s0, s1 = tops[:, 2 * h, :], tops[:, 2 * h + 1, :]
                        cs3 = cs[:].rearrange("p (a b) -> p a b", b=16)
                        c.op("dve", lambda e, s0=s0, s1=s1, cs3=cs3: e.tensor_tensor(
                            out=cs3, in0=mkap(s0, [[1, 16], [0, 16]]), in1=mkap(s1, [[0, 16], [1, 16]]), op=ALU.add),
                            reads=[t_tops], writes=[t_cs])
                        c.op("dve", lambda e, h=h: e.max(out=best[:, h, 0:8], in_=cs[:]), reads=[t_cs], writes=[t_best])
                        c.op("dve", lambda e, h=h: e.max_index(out=posu[:, h, 0:8], in_max=best[:, h, 0:8], in_values=cs[:]),
                             reads=[t_cs, t_best], writes=[t_posu])
                        c.op("dve", lambda e, h=h: e.match_replace(out=sc2[:], in_to_replace=best[:, h, 0:8], in_values=cs[:],
                                                                   imm_value=-1e30), reads=[t_cs, t_best], writes=[t_sc2])
                        c.op("dve", lambda e, h=h: e.max(out=best[:, h, 8:16], in_=sc2[:]), reads=[t_sc2], writes=[t_best])
                        c.op("dve", lambda e, h=h: e.max_index(out=posu[:, h, 8:16], in_max=best[:, h, 8:16], in_values=sc2[:]),
                             reads=[t_sc2, t_best], writes=[t_posu])
                    T3 = eq3[:, 0:8, :].rearrange("p h (k a) -> p (h k) a", a=16)
                    T4 = eq3[:, 0:8, :].rearrange("p h (k a) -> p h k a", a=16)
                    iota_b = mkap(K.cst[:, C_IOTA:C_IOTA + 16], [[0, 128], [1, 16]])
                    lo_b = mkap(K.cst[:, C_LO16:C_LO16 + 16], [[0, 128], [1, 16]])
                    c.op("dve", lambda e: e.tensor_copy(out=posf[:, 0, :], in_=posu[:].rearrange("p h k -> p (h k)")),
                         reads=[t_posu], writes=[t_posf])
                    c.op("dve", lambda e: e.tensor_tensor(out=T3, in0=mkap(posf[:, 0, :], [[1, 128], [0, 16]]), in1=lo_b, op=ALU.is_ge),
                         reads=[t_posf, K.t_cst], writes=[t_eq3])
                    c.op("dve", lambda e: e.tensor_reduce(out=posf[:, 1, :], in_=T3, axis=AX.X, op=ALU.add),
                         reads=[t_eq3], writes=[t_posf])
                    c.op("dve", lambda e: e.tensor_scalar(out=posf[:, 1, :], in0=posf[:, 1, :], scalar1=-1.0, scalar2=None, op0=ALU.add),
                         reads=[t_posf], writes=[t_posf])
                    c.op("dve", lambda e: e.scalar_tensor_tensor(out=posf[:, 2, :], in0=posf[:, 1, :], scalar=-16.0, in1=posf[:, 0, :],
                                                                 op0=ALU.mult, op1=ALU.add), reads=[t_posf], writes=[t_posf])
                    for w_, (src_, dst_) in enumerate(((1, 3), (2, 4))):
                        c.op("dve", lambda e, src_=src_: e.tensor_tensor(out=T3, in0=mkap(posf[:, src_, :], [[1, 128], [0, 16]]),
                                                                         in1=iota_b, op=ALU.is_equal),
                             reads=[t_posf, K.t_cst], writes=[t_eq3])
                        c.op("dve", lambda e, w_=w_: e.tensor_tensor(out=T4, in0=T4, in1=mkap(topi[:, w_, :], [[32, 8], [0, 16], [1, 16]]),
                                                                     op=ALU.mult), reads=[t_eq3, t_topi], writes=[t_eq3])
                        c.op("dve", lambda e, dst_=dst_: e.tensor_reduce(out=posf[:, dst_, :], in_=T3, axis=AX.X, op=ALU.add),
                             reads=[t_eq3], writes=[t_posf])
                    c.op("dve", lambda e: e.scalar_tensor_tensor(out=idxf[:], in0=posf[:, 3, :], scalar=128.0, in1=posf[:, 4, :],
                                                                 op0=ALU.mult, op1=ALU.add), reads=[t_posf], writes=[t_idxf])
                    c.op("dve", lambda e: e.tensor_scalar(out=sm[:, 0:8], in0=best[:, :, 0], scalar1=-1.0, scalar2=None, op0=ALU.mult),
                         reads=[t_best], writes=[t_sm])
                    for h in range(8):
                        c.op("act", lambda e, h=h: e.activation(out=eg[:, h, :], in_=best[:, h, :], func=AF.Exp, bias=sm[:, h:h + 1],
                                                                scale=1.0, accum_out=sm[:, 8 + h:9 + h]),
                             reads=[t_best, t_sm], writes=[t_eg, t_sm])
                    c.op("dve", lambda e: e.reciprocal(out=sm[:, 16:24], in_=sm[:, 8:16]), reads=[t_sm], writes=[t_sm])
                    c.op("dve", lambda e: e.tensor_tensor(out=gates[:].rearrange("p (h k) -> p h k", k=16), in0=eg[:],
                                                          in1=mkap(sm[:, 16:24], [[1, 8], [0, 16]]), op=ALU.mult),
                         reads=[t_eg, t_sm], writes=[t_gates])
                    c.op("dve", lambda e: e.tensor_scalar(out=idxf[:], in0=idxf[:], scalar1=0.0, scalar2=float(NEXP - 1),
                                                          op0=ALU.max, op1=ALU.min), reads=[t_idxf], writes=[t_idxf])
                    c.op("pe", lambda e: e.transpose(out=tps[1][:, 0:128], in_=idxf[:], identity=K.ident_f),
                         reads=[t_idxf, K.t_cst], writes=[t_tps[1]])
                    c.op("pe", lambda e: e.transpose(out=tps[1][:, 128:256], in_=gates[:], identity=K.ident_f),
                         reads=[t_gates, K.t_cst], writes=[t_tps[1]])
                    c.op("dve", lambda e: e.tensor_copy(out=idxT[:, tsg], in_=tps[1][:, 0:128]), reads=[t_tps[1]], writes=[t_idxT])
                    c.op("dve", lambda e: e.tensor_copy(out=gT[:, tsg], in_=tps[1][:, 128:256]),
                         reads=[t_tps[1]], writes=[t_gT])
            c.barrier()

        with ExitStack() as es:
            NS = 4
            UV = [SB(K, es, "UV", [128, 2 * D], BF16) for _ in range(NS)]
            t_UV = [c.tok("UV%d" % i) for i in range(NS)]
            hbc = [SB(K, es, "hbc", [128, D], BF16) for _ in range(NS)]
            t_hbc = [c.tok("hbc%d" % i) for i in range(NS)]
            h2 = [SB(K, es, "h2", [128, D], BF16) for _ in range(2)]
            t_h2s = [c.tok("h2s0"), c.tok("h2s1")]
            yT = [PS(K, es, "yT", [128, 16, 128], F32) for _ in range(2)]
            t_yT = [c.tok("yT0"), c.tok("yT1")]
            junk = SB(K, es, "junk", [128, D], BF16)
            t_junk = c.tok("junk")
            acc = [SB(K, es, "acc", [128, 8], F32) for _ in range(2)]
            t_acc = [c.tok("acc0"), c.tok("acc1")]
            wv_ = [SB(K, es, "wv", [128, 2], BF16) for _ in range(2)]
            t_wv = [c.tok("wv0"), c.tok("wv1")]
            xs = SB(K, es, "xs", [128, NCH, 128], F32)
            t_xs = c.tok("xs")
            xo = SB(K, es, "xo2", [128, NCH, 128], F32)
            t_xo = c.tok("xo2")
            xiv = xin.rearrange("(c p) t -> p c t", p=128)
            xov = xout.rearrange("(c p) t -> p c t", p=128)
            for tb in range(NB):
                tsg = slice(tb * 128, (tb + 1) * 128)
                yTt, t_yTt = yT[tb % 2], t_yT[tb % 2]
                h2t, t_h2t = h2[tb % 2], t_h2s[tb % 2]
                c.dma(h2t[:], K.d_h2[tsg, :], reads=[K.t_h2], writes=[t_h2t])
                c.dma(xs[:], xiv[:, :, tsg], reads=([t_xin] if t_xin is not None else []), writes=[t_xs])
                for j in range(128):
                    t = tb * 128 + j
                    s = t % NS
                    s2 = t % 2
                    c.dma(None, None, reads=[t_idxT, K.t_uvb[layer]], writes=[t_UV[s]], q="pool",
                          fn=lambda e, s=s, t=t: e.indirect_dma_start(
                              out=UV[s][:], out_offset=None, in_=K.d_uvb[layer][:, :],
                              in_offset=bass.IndirectOffsetOnAxis(ap=idxT[:, t:t + 1], axis=0),
                              bounds_check=K.bc_reg, oob_is_err=False))
                    if t % 4 != 3:
                        hrow = K.d_h2[t:t + 1, :]
                        src = bass.AP(tensor=hrow.tensor, offset=hrow.offset, ap=[[0, 4], [1, D]])
                        hb0 = hbc[s][0:1, :]
                        dst = bass.AP(tensor=hb0.tensor, offset=hb0.offset, ap=[[32 * D, 4], [1, D]])
                        c.dma(dst, src, reads=[K.t_h2], writes=[t_hbc[s]])
                        c.op("dve", lambda e, s=s: e.stream_shuffle(out=hbc[s][:], in_=hbc[s][:], mask=[0] * 32),
                             reads=[t_hbc[s]], writes=[t_hbc[s]])
                    else:
                        hrow = K.d_h2[t:t + 1, :]
                        src = bass.AP(tensor=hrow.tensor, offset=hrow.offset, ap=[[0, 128], [1, D]])
                        c.dma(hbc[s][:], src, reads=[K.t_h2], writes=[t_hbc[s]])
                    c.op("dve", lambda e, s=s, s2=s2: e.scalar_tensor_tensor(
                        out=junk[:], in0=UV[s][:, 0:D], scalar=1.0, in1=hbc[s][:],
                        op0=ALU.mult, op1=ALU.mult, accum_out=acc[s2][:, 4:5]),
                        reads=[t_UV[s], t_hbc[s]], writes=[t_junk, t_acc[s2]])
                    c.op("act", lambda e, s2=s2: e.activation(out=acc[s2][:, 5:6], in_=acc[s2][:, 4:5], func=AF.Gelu),
                         reads=[t_acc[s2]], writes=[t_acc[s2]])
                    c.op("dve", lambda e, s2=s2, t=t: e.tensor_tensor(out=wv_[s2][:, 0:1], in0=acc[s2][:, 5:6], in1=gT[:, t:t + 1],
                                                                      op=ALU.mult),
                         reads=[t_acc[s2], t_gT], writes=[t_wv[s2]])
                    for cc in range(NCH):
                        c.op("pe", lambda e, cc=cc, s=s, s2=s2, j=j: e.matmul(
                            yTt[:, cc, j:j + 1], lhsT=UV[s][:, D + cc * 128:D + (cc + 1) * 128],
                            rhs=wv_[s2][:, 0:1], start=True, stop=True),
                            reads=[t_UV[s], t_wv[s2]], writes=[t_yTt])
                for cc in range(NCH):
                    c.op("dve", lambda e, cc=cc: e.scalar_tensor_tensor(out=xo[:, cc, :], in0=yTt[:, cc, :],
                                                                        scalar=K.mod[:, gcol + cc:gcol + cc + 1], in1=xs[:, cc, :],
                                                                        op0=ALU.mult, op1=ALU.add),
                         reads=[t_yTt, t_xs, K.t_mod], writes=[t_xo])
                c.dma(xov[:, :, tsg], xo[:], reads=[t_xo], writes=[t_xout])
            c.barrier()


def t5_bucket_np(dist):
    n = np.maximum(dist, 0)
    nf = np.maximum(n, 1).astype(np.float32)
    large = 16 + (np.log(nf / np.float32(16)) / np.float32(math.log(2048 / 16)) * np.float32(16)).astype(np.int32)
    large = np.minimum(large, 31)
    return np.where(n < 16, n, large)


def prep_w(W):
    n = W.shape[1]
    return np.ascontiguousarray(W.reshape(16, 128, n // 128, 128).transpose(2, 1, 0, 3))


def host_prep(inp, S, cores):
    f = lambda a: np.ascontiguousarray(np.asarray(a, dtype=np.float32))
    TQ = min(512, S)
    PADL = TQ - 128
    MW = PADL + S
    shared = {}
    w_ada = f(inp["w_ada"])
    shared["wada"] = np.ascontiguousarray(w_ada.reshape(2, 16, 128, 24, 512).transpose(0, 3, 2, 1, 4))
    w_in = f(inp["even_w_in"])[0]
    shared["win"] = prep_w(np.concatenate([w_in[:, :3072], w_in[:, 3080:]], axis=1))
    shared["wff"] = np.ascontiguousarray(w_in[:, 3072:3080].reshape(16, 128, 8).transpose(1, 0, 2))
    shared["wout0"] = prep_w(f(inp["even_w_out"])[0])
    shared["wqkv"] = prep_w(f(inp["odd_w_qkv"])[0])
    shared["wout1"] = prep_w(f(inp["odd_w_out"])[0])
    wq = f(inp["peer_w_query"])
    shared["wquery"] = np.stack([prep_w(wq[0]), prep_w(wq[1])])
    sk = f(inp["peer_sub_keys"])
    shared["keysT"] = np.ascontiguousarray(sk.transpose(0, 3, 1, 2))
    pu, pv = f(inp["peer_u"]), f(inp["peer_v"])
    shared["uv0"] = np.concatenate([pu[0], pv[0]], axis=1).reshape(64, 128, 4 * D)
    shared["uv1"] = np.concatenate([pu[1], pv[1]], axis=1).reshape(64, 128, 4 * D)
    rb = f(inp["rel_bias"])
    kl = np.arange(128)[:, None]
    jj = np.arange(MW)[None, :]
    delta = jj - PADL - kl
    bidx = t5_bucket_np(delta)
    bt = rb[bidx]
    bt = np.where((delta >= 0)[:, :, None], bt, np.float32(NEG))
    shared["biasT"] = np.ascontiguousarray(bt.transpose(2, 0, 1)).astype(np.float32)
    mult = np.zeros_like(delta, dtype=np.float32)
    for (wdw, dil) in ((128, 1), (512, 4), (2048, 16)):
        mult += ((delta >= 0) & (delta % dil == 0) & (delta // dil <= wdw // dil)).astype(np.float32)
    shared["multT"] = np.ascontiguousarray(mult)
    jj2 = np.arange(PADL + TQ)[None, :]
    shared["triT"] = np.ascontiguousarray(np.where((jj2 - PADL - kl) >= 0, np.float32(0.0), np.float32(NEG)))
    x = np.asarray(inp["x"], dtype=np.float32)
    cvec = f(inp["c"])
    b_ada = f(inp["b_ada"])
    ng = f(inp["norm_gain"])
    maps = []
    for b in cores:
        cst = np.zeros((128, NCST), np.float32)
        cst[:, C_C:C_C + 16] = cvec[b].reshape(16, 128).T
        cst[:, C_BADA:C_BADA + 192] = b_ada.reshape(2, 96, 128).transpose(2, 0, 1).reshape(128, 192)
        cst[:, C_GAIN:C_GAIN + 64] = ng.reshape(2, 2, 16, 128).transpose(3, 0, 1, 2).reshape(128, 64)
        cst[:, C_FOXG:C_FOXG + 2] = f(inp["even_fox_qk_gain"])[0].T
        cst[:, C_DIFFG:C_DIFFG + 2] = f(inp["even_diff_qk_gain"])[0].T
        cst[:, C_LAM:C_LAM + 4] = f(inp["even_diff_lambda"])[0].T
        cst[:, C_SUBLN:C_SUBLN + 2] = f(inp["even_diff_subln_gain"])[0].reshape(2, 128).T
        cst[:, C_ODDG:C_ODDG + 2] = f(inp["odd_qk_gain"])[0].T
        cst[0:8, C_BF] = f(inp["even_b_forget"])[0]
        cst[:, C_ID:C_ID + 128] = np.eye(128, dtype=np.float32)
        cst[:, C_IOTA:C_IOTA + 16] = np.arange(16, dtype=np.float32)[None, :]
        cst[:, C_LO16:C_LO16 + 16] = 16.0 * np.arange(16, dtype=np.float32)[None, :]
        m = dict(shared)
        m["cst"] = cst
        m["xT"] = np.ascontiguousarray(x[b, :S, :].T)
        maps.append(m)
    return maps


def kernel(**inputs):
    S = 2048
    cores = list(range(8))
    maps = host_prep(inputs, S, cores)
    nc = build(S)
    res = run_bass_kernel_spmd(nc, maps, core_ids=cores)
    out = np.stack([np.ascontiguousarray(res.results[i]["y"].T) for i in range(8)], axis=0)
    return out.astype(np.float32)
```

```python
import math
from contextlib import ExitStack
import numpy as np
import concourse.bass as bass
import concourse.mybir as mybir
from concourse.bass_utils import run_bass_kernel_spmd

F32 = mybir.dt.float32
BF16 = mybir.dt.bfloat16
U32 = mybir.dt.uint32
AF = mybir.ActivationFunctionType
ALU = mybir.AluOpType
AX = mybir.AxisListType

D = 2048
NCH = 16
EPS = 1e-6
NEG = -30000.0
SCALE = 128 ** -0.5
NEXP = 16384
SBUF_BCAST = False
C_C, C_BADA, C_GAIN, C_FOXG, C_DIFFG, C_LAM, C_SUBLN, C_ODDG, C_BF, C_ID, NCST = 0, 16, 208, 272, 274, 276, 280, 282, 284, 288, 448
C_IOTA, C_LO16 = 416, 432


class Ev:
    __slots__ = ("sem", "val", "key", "pe")

    def __init__(self, sem, val, key, pe=False):
        self.sem, self.val, self.key, self.pe = sem, val, key, pe


class Tok:
    __slots__ = ("name", "w", "r", "dsem", "dcnt", "dkey")

    def __init__(self, name):
        self.name = name
        self.w = None
        self.r = {}
        self.dsem = None
        self.dcnt = 0
        self.dkey = None


class Eng:
    def __init__(self, ctx, eng, name):
        self.ctx, self.eng, self.name = ctx, eng, name
        self.sem = ctx.nc.alloc_semaphore("s_" + name)
        self.key = "E" + name
        self.n = 0
        self.seen = {}

    def wait(self, ev):
        if ev is None:
            return
        if ev.pe and self.name == "pe":
            return
        if self.seen.get(ev.key, 0) >= ev.val:
            return
        self.eng.wait_ge(ev.sem, ev.val)
        self.seen[ev.key] = ev.val


class Ctx:
    def __init__(self, nc):
        self.nc = nc
        self.pe = Eng(self, nc.tensor, "pe")
        self.act = Eng(self, nc.scalar, "act")
        self.dve = Eng(self, nc.vector, "dve")
        self.pool = Eng(self, nc.gpsimd, "pool")
        self.sp = Eng(self, nc.sync, "sp")
        self.engs = dict(pe=self.pe, act=self.act, dve=self.dve, pool=self.pool, sp=self.sp)
        self.ndsem = 0
        self.dtoks = []

    def tok(self, name):
        return Tok(name)

    def _dsem(self, tok):
        if tok.dsem is None:
            tok.dsem = self.nc.alloc_semaphore("d%d" % self.ndsem)
            tok.dkey = "D%d" % self.ndsem
            self.ndsem += 1
            self.dtoks.append(tok)
        return tok.dsem

    def op(self, en, fn, reads=(), writes=()):
        E = self.engs[en]
        for t in reads:
            E.wait(t.w)
        for t in writes:
            E.wait(t.w)
            for ev in t.r.values():
                E.wait(ev)
        inst = fn(E.eng)
        E.n += 1
        inst.then_inc(E.sem, 1)
        ev = Ev(E.sem, E.n, E.key, pe=(en == "pe"))
        for t in writes:
            t.w = ev
            t.r = {}
        for t in reads:
            t.r[E.key] = ev
        return inst

    def dma(self, out_ap, in_ap, reads=(), writes=(), q="sp", fn=None, **kw):
        E = self.engs[q]
        for t in reads:
            E.wait(t.w)
        for t in writes:
            if not (t.w is not None and t.w.key == t.dkey):
                E.wait(t.w)
            for ev in t.r.values():
                E.wait(ev)
        t0 = writes[0]
        sem = self._dsem(t0)
        if fn is None:
            inst = E.eng.dma_start(out=out_ap, in_=in_ap, **kw)
        else:
            inst = fn(E.eng)
        inst.then_inc(sem, 16)
        t0.dcnt += 16
        ev = Ev(sem, t0.dcnt, t0.dkey)
        for t in writes:
            t.w = ev
            t.r = {}
        for t in reads:
            t.r[t0.dkey] = ev
        return inst

    def barrier(self):
        dve = self.dve
        for t in self.dtoks:
            dve.wait(Ev(t.dsem, t.dcnt, t.dkey))
        others = [self.pe, self.act, self.pool, self.sp]
        for E in others:
            if E.n:
                dve.wait(Ev(E.sem, E.n, E.key))
        if dve.n:
            dve.wait(Ev(dve.sem, dve.n, dve.key))
        inst = dve.eng.memset(self.bar_tile[:, 0:1], 0.0)
        dve.n += 1
        inst.then_inc(dve.sem, 1)
        ev = Ev(dve.sem, dve.n, dve.key)
        for E in others:
            E.wait(ev)


def mkap(ap, dims):
    return bass.AP(tensor=ap.tensor, offset=ap.offset, ap=[list(ap.ap[0])] + [list(d) for d in dims])


class KK:
    pass


_uid = [0]


def _nm(s):
    _uid[0] += 1
    return "%s_%d" % (s, _uid[0])


def SB(K, es, name, shape, dt):
    return es.enter_context(K.nc.sbuf_tensor(_nm(name), list(shape), dt))


def PS(K, es, name, shape=(128, 512), dt=F32):
    return es.enter_context(K.nc.psum_tensor(_nm(name), list(shape), dt))


def build(S, dbg=False, stop=99):
    nc = bass.Bass("TRN2", target_bir_lowering=False)
    K = KK()
    K.nc, K.S = nc, S
    K.c = c = Ctx(nc)
    K.TQ = TQ = min(512, S)
    K.NQ = S // TQ
    K.QB = TQ // 128
    K.NB = S // 128
    K.PADL = TQ - 128
    K.MW = K.PADL + S
    K.dbg = dbg

    def din(name, shape, dt=F32):
        return nc.dram_tensor(name, list(shape), dt, kind="ExternalInput").ap()

    def dint(name, shape, dt=F32):
        return nc.dram_tensor(name, list(shape), dt, kind="Internal").ap()

    K.d_xT = din("xT", [D, S])
    K.d_cst = din("cst", [128, NCST])
    K.d_wada = din("wada", [2, 24, 128, 16, 512])
    K.d_win = din("win", [48, 128, 16, 128])
    K.d_wff = din("wff", [128, 16, 8])
    K.d_wout0 = din("wout0", [16, 128, 16, 128])
    K.d_wqkv = din("wqkv", [48, 128, 16, 128])
    K.d_wout1 = din("wout1", [16, 128, 16, 128])
    K.d_wquery = din("wquery", [2, 16, 128, 16, 128])
    K.d_keys = din("keysT", [2, 128, 2, 128])
    K.d_uv = [din("uv0", [64, 128, 4 * D]), din("uv1", [64, 128, 4 * D])]
    _uvb = dint("uvb", [NEXP, 2 * D], BF16)
    K.d_uvb = [_uvb, _uvb]
    _tuvb = c.tok("uvb")
    K.t_uvb = [_tuvb, _tuvb]
    K.d_biasT = din("biasT", [16, 128, K.MW])
    K.d_multT = din("multT", [128, K.MW])
    K.d_triT = din("triT", [128, K.PADL + TQ])
    K.d_y = nc.dram_tensor("y", [D, S], F32, kind="ExternalOutput").ap()
    dscr = (lambda n, sh: nc.dram_tensor(n, list(sh), F32, kind="ExternalOutput").ap()) if dbg else dint
    K.d_xA = dscr("xA", [D, S])
    K.d_xB = dscr("xB", [D, S])
    K.d_xC = dscr("xC", [D, S])
    K.d_mixed = dint("mixed", [D, S], BF16)
    K.d_fsc = dint("fsc", [4, 8, S], BF16)
    K.d_h2 = dint("h2tok", [S, D], BF16)
    K.t_xA, K.t_xB, K.t_xC, K.t_y = c.tok("xA"), c.tok("xB"), c.tok("xC"), c.tok("y")
    K.t_mixed, K.t_fsc, K.t_h2 = c.tok("mixed"), c.tok("fsc"), c.tok("h2")

    with ExitStack() as pers:
        prologue(K, pers)
        convert_tables(K, 0)
        mixer(K, 0, K.d_xT, None, K.d_xA, K.t_xA)
        if stop >= 3:
            peer(K, 0, K.d_xA, K.t_xA, K.d_xB, K.t_xB)
        if stop >= 4:
            convert_tables(K, 1)
            mixer(K, 1, K.d_xB, K.t_xB, K.d_xC, K.t_xC)
        if stop >= 5:
            peer(K, 1, K.d_xC, K.t_xC, K.d_y, K.t_y)
        c.barrier()
        for t in c.dtoks:
            c.sp.wait(Ev(t.dsem, t.dcnt, t.dkey))
    return nc


def prologue(K, pers):
    c, nc = K.c, K.nc
    c.bar_tile = SB(K, pers, "bar", [128, 2], F32)
    K.bc_reg = nc.gpsimd.to_reg(NEXP - 1)
    K.cst = SB(K, pers, "cst", [128, NCST], F32)
    K.t_cst = c.tok("cst")
    c.dma(K.cst[:], K.d_cst[:, :], writes=[K.t_cst])
    K.ident_f = K.cst[:, C_ID:C_ID + 128]
    K.ones_bf = SB(K, pers, "ones", [128, 128], BF16)
    K.t_ones = c.tok("ones")
    c.op("dve", lambda e: e.memset(K.ones_bf[:], 1.0), writes=[K.t_ones])
    K.ident_bf = SB(K, pers, "identb", [128, 128], BF16)
    K.t_identb = c.tok("identb")
    c.op("dve", lambda e: e.tensor_copy(out=K.ident_bf[:], in_=K.ident_f), reads=[K.t_cst], writes=[K.t_identb])
    K.mod = SB(K, pers, "mod", [128, 192], F32)
    K.t_mod = c.tok("mod")
    K.amod = SB(K, pers, "amod", [128, 64], F32)
    K.misc = SB(K, pers, "misc", [128, 16], F32)
    K.t_misc = c.tok("misc")
    K.wf = [SB(K, pers, "wf", [128, 16, 128], F32) for _ in range(2)]
    K.t_wf = [c.tok("wf0"), c.tok("wf1")]
    K.wb = [SB(K, pers, "wb", [128, 16, 256], BF16) for _ in range(2)]
    K.t_wb = [c.tok("wb0"), c.tok("wb1")]
    K.wcnt = 0

    with ExitStack() as es:
        cond = SB(K, es, "cond", [128, 16], F32)
        t_cond = c.tok("cond")
        c.op("act", lambda e: e.activation(out=cond[:], in_=K.cst[:, C_C:C_C + 16], func=AF.Silu),
             reads=[K.t_cst], writes=[t_cond])
        wad = [SB(K, es, "wad", [128, 16, 512], F32) for _ in range(2)]
        t_wad = [c.tok("wad0"), c.tok("wad1")]
        modps = PS(K, es, "modps")
        t_modps = c.tok("modps")
        condb = SB(K, es, "condb", [128, 16], BF16)
        t_condb = c.tok("condb")
        c.op("dve", lambda e: e.tensor_copy(out=condb[:], in_=cond[:]), reads=[t_cond], writes=[t_condb])
        wadb = [SB(K, es, "wadb", [128, 16, 512], BF16) for _ in range(2)]
        t_wadb = [c.tok("wadb0"), c.tok("wadb1")]
        for i in range(2):
            for g in range(24):
                s = (i * 24 + g) % 2
                c.dma(wad[s][:], K.d_wada[i, g], writes=[t_wad[s]])
                c.op("dve", lambda e, s=s: e.tensor_copy(out=wadb[s][:, 0:6, :], in_=wad[s][:, 0:6, :]),
                     reads=[t_wad[s]], writes=[t_wadb[s]])
                c.op("act", lambda e, s=s: e.activation(out=wadb[s][:, 6:11, :], in_=wad[s][:, 6:11, :], func=AF.Copy),
                     reads=[t_wad[s]], writes=[t_wadb[s]])
                c.op("pool", lambda e, s=s: e.tensor_copy(out=wadb[s][:, 11:16, :], in_=wad[s][:, 11:16, :]),
                     reads=[t_wad[s]], writes=[t_wadb[s]])
                for j in range(4):
                    nb = g * 4 + j
                    col = i * 96 + nb
                    for cc in range(NCH):
                        c.op("pe", lambda e, s=s, j=j, cc=cc, col=col: e.matmul(
                            modps[:, col:col + 1], lhsT=wadb[s][:, cc, j * 128:(j + 1) * 128],
                            rhs=condb[:, cc:cc + 1], start=(cc == 0), stop=(cc == NCH - 1)),
                            reads=[t_wadb[s], t_condb], writes=[t_modps])
        c.op("dve", lambda e: e.tensor_tensor(out=K.mod[:], in0=modps[:, 0:192], in1=K.cst[:, C_BADA:C_BADA + 192],
                                              op=ALU.add), reads=[t_modps, K.t_cst], writes=[K.t_mod])
        for i in range(2):
            for s in range(2):
                sc = K.mod[:, i * 96 + (1 if s == 0 else 4) * 16: i * 96 + (1 if s == 0 else 4) * 16 + 16]
                gn = K.cst[:, C_GAIN + (i * 2 + s) * 16: C_GAIN + (i * 2 + s) * 16 + 16]
                o = K.amod[:, (i * 2 + s) * 16:(i * 2 + s) * 16 + 16]
                c.op("dve", lambda e, sc=sc, gn=gn, o=o: e.scalar_tensor_tensor(
                    out=o, in0=sc, scalar=1.0, in1=gn, op0=ALU.add, op1=ALU.mult),
                    reads=[K.t_mod, K.t_cst], writes=[K.t_mod])
        M = K.misc
        c.op("dve", lambda e: e.tensor_scalar(out=M[:, 0:1], in0=K.cst[:, C_FOXG:C_FOXG + 1], scalar1=SCALE, scalar2=None,
                                              op0=ALU.mult), reads=[K.t_cst], writes=[K.t_misc])
        c.op("dve", lambda e: e.tensor_scalar(out=M[:, 1:2], in0=K.cst[:, C_DIFFG:C_DIFFG + 1], scalar1=SCALE, scalar2=None,
                                              op0=ALU.mult), reads=[K.t_cst], writes=[K.t_misc])
        c.op("dve", lambda e: e.tensor_scalar(out=M[:, 2:3], in0=K.cst[:, C_ODDG:C_ODDG + 1], scalar1=SCALE, scalar2=None,
                                              op0=ALU.mult), reads=[K.t_cst], writes=[K.t_misc])
        c.op("dve", lambda e: e.tensor_scalar(out=M[:, 3:4], in0=K.cst[:, C_BF:C_BF + 1], scalar1=-1.0, scalar2=None,
                                              op0=ALU.mult), reads=[K.t_cst], writes=[K.t_misc])
        lam_init = 0.8 - 0.6 * math.exp(-0.3 * 0)
        K.lam_init = lam_init
        lp = SB(K, es, "lp", [128, 2], F32)
        t_lp = c.tok("lp")
        c.op("dve", lambda e: e.tensor_tensor(out=lp[:, 0:1], in0=K.cst[:, C_LAM:C_LAM + 1], in1=K.cst[:, C_LAM + 1:C_LAM + 2],
                                              op=ALU.mult), reads=[K.t_cst], writes=[t_lp])
        c.op("dve", lambda e: e.tensor_tensor(out=lp[:, 1:2], in0=K.cst[:, C_LAM + 2:C_LAM + 3], in1=K.cst[:, C_LAM + 3:C_LAM + 4],
                                              op=ALU.mult), reads=[K.t_cst], writes=[t_lp])
        onesf = SB(K, es, "onesf", [128, 128], F32)
        t_onesf = c.tok("onesf")
        c.op("dve", lambda e: e.memset(onesf[:], 1.0), writes=[t_onesf])
        c.op("pe", lambda e: e.matmul(modps[:, 200:202], lhsT=onesf[:], rhs=lp[:, 0:2], start=True, stop=True),
             reads=[t_onesf, t_lp], writes=[t_modps])
        le = SB(K, es, "le", [128, 2], F32)
        t_le = c.tok("le")
        c.op("act", lambda e: e.activation(out=le[:], in_=modps[:, 200:202], func=AF.Exp), reads=[t_modps], writes=[t_le])
        c.op("dve", lambda e: e.scalar_tensor_tensor(out=M[:, 4:5], in0=le[:, 1:2], scalar=-lam_init, in1=le[:, 0:1],
                                                     op0=ALU.add, op1=ALU.subtract), reads=[t_le], writes=[K.t_misc])
        c.op("dve", lambda e: e.tensor_scalar(out=M[:, 5:7], in0=K.cst[:, C_SUBLN:C_SUBLN + 2], scalar1=(1.0 - lam_init),
                                              scalar2=None, op0=ALU.mult), reads=[K.t_cst], writes=[K.t_misc])
        c.barrier()


def convert_tables(K, layer):
    c = K.c
    with ExitStack() as es:
        NSL = 3
        W8 = 4 * D
        fin = [SB(K, es, "cvi", [128, W8], F32) for _ in range(NSL)]
        t_fin = [c.tok("cvi%d" % i) for i in range(NSL)]
        fout = [SB(K, es, "cvo", [128, W8], BF16) for _ in range(NSL)]
        t_fout = [c.tok("cvo%d" % i) for i in range(NSL)]
        cnt = 0
        if True:
            for blk in range(64):
                s = cnt % NSL
                cnt += 1
                c.dma(fin[s][:], K.d_uv[layer][blk], writes=[t_fin[s]])
                c.op("dve", lambda e, s=s: e.tensor_copy(out=fout[s][:, 0:3072], in_=fin[s][:, 0:3072]),
                     reads=[t_fin[s]], writes=[t_fout[s]])
                c.op("act", lambda e, s=s: e.activation(out=fout[s][:, 3072:6144], in_=fin[s][:, 3072:6144], func=AF.Copy),
                     reads=[t_fin[s]], writes=[t_fout[s]])
                c.op("pool", lambda e, s=s: e.tensor_copy(out=fout[s][:, 6144:W8], in_=fin[s][:, 6144:W8]),
                     reads=[t_fin[s]], writes=[t_fout[s]])
                base = K.d_uvb[layer]
                dst = bass.AP(tensor=base.tensor, offset=base.offset + blk * 128 * W8, ap=[[W8, 128], [1, W8]])
                c.dma(dst, fout[s][:], reads=[t_fout[s]], writes=[K.t_uvb[layer]])
        c.barrier()

def load_unit(K, src_ap, dv_off=0):
    c = K.c
    s = K.wcnt % 2
    c.dma(K.wf[s][:], src_ap, writes=[K.t_wf[s]])
    return s


def get_weights(K, srcs):
    c = K.c
    ws = K.wcnt % 2
    wb, t_wb = K.wb[ws], K.t_wb[ws]
    for j, src in enumerate(srcs):
        fs = (K.wcnt * 2 + j) % 2
        c.dma(K.wf[fs][:], src, writes=[K.t_wf[fs]])
        eng = "dve" if (j % 2 == 0) else "pool"
        c.op(eng, lambda e, fs=fs, j=j: e.tensor_copy(out=wb[:, :, j * 128:(j + 1) * 128], in_=K.wf[fs][:]),
             reads=[K.t_wf[fs]], writes=[t_wb])
    K.wcnt += 1
    return wb, t_wb


def norm_mod(K, xsrc, t_xsrc, i, s, hT, t_hT):
    c = K.c
    S, TQ, NQ = K.S, K.TQ, K.NQ
    xv = xsrc.rearrange("(c p) t -> p c t", p=128)
    acol = (i * 2 + s) * 16
    shcol = i * 96 + (0 if s == 0 else 3) * 16
    with ExitStack() as es:
        xb = SB(K, es, "xb", [128, 16, TQ], F32)
        t_xb = c.tok("xb")
        sq = [SB(K, es, "sq", [128, TQ], BF16) for _ in range(2)]
        t_sq = [c.tok("sq0"), c.tok("sq1")]
        rs = SB(K, es, "rs", [128, TQ], F32)
        t_rs = c.tok("rs")
        rinv = SB(K, es, "rinv", [128, TQ], F32)
        t_rinv = c.tok("rinv")
        tmp = [SB(K, es, "tmp", [128, TQ], F32) for _ in range(2)]
        t_tmp = [c.tok("tmp0"), c.tok("tmp1")]
        ssq = PS(K, es, "ssq")
        t_ssq = c.tok("ssq")
        for tq in range(NQ):
            sl = slice(tq * TQ, (tq + 1) * TQ)
            c.dma(xb[:], xv[:, :, sl], reads=([t_xsrc] if t_xsrc is not None else []), writes=[t_xb])
            for cc in range(NCH):
                k = cc % 2
                c.op("act", lambda e, cc=cc, k=k: e.activation(out=sq[k][:], in_=xb[:, cc, :], func=AF.Square),
                     reads=[t_xb], writes=[t_sq[k]])
                c.op("pe", lambda e, cc=cc, k=k: e.matmul(ssq[:, 0:TQ], lhsT=K.ones_bf[:], rhs=sq[k][:],
                                                         start=(cc == 0), stop=(cc == NCH - 1)),
                     reads=[t_sq[k], K.t_ones], writes=[t_ssq])
            c.op("act", lambda e: e.activation(out=rs[:], in_=ssq[:, 0:TQ], func=AF.Sqrt, scale=1.0 / D, bias=EPS),
                 reads=[t_ssq], writes=[t_rs])
            c.op("dve", lambda e: e.reciprocal(out=rinv[:], in_=rs[:]), reads=[t_rs], writes=[t_rinv])
            for cc in range(NCH):
                k = cc % 2
                c.op("dve", lambda e, cc=cc, k=k: e.scalar_tensor_tensor(
                    out=tmp[k][:], in0=xb[:, cc, :], scalar=K.amod[:, acol + cc:acol + cc + 1], in1=rinv[:],
                    op0=ALU.mult, op1=ALU.mult), reads=[t_xb, t_rinv, K.t_mod], writes=[t_tmp[k]])
                c.op("act", lambda e, cc=cc, k=k: e.activation(
                    out=hT[:, cc, sl], in_=tmp[k][:], func=AF.Identity, bias=K.mod[:, shcol + cc:shcol + cc + 1], scale=1.0),
                    reads=[t_tmp[k], K.t_mod], writes=[t_hT])
        c.barrier()


def proj_fm(K, wb, t_wb, woff, hT, t_hT, pj, t_pj, post):
    c = K.c
    for tq in range(K.NQ):
        k = K.pjcnt % 2
        K.pjcnt += 1
        sl = slice(tq * K.TQ, (tq + 1) * K.TQ)
        for cc in range(NCH):
            c.op("pe", lambda e, cc=cc, k=k: e.matmul(pj[k][:, 0:K.TQ], lhsT=wb[:, cc, woff:woff + 128], rhs=hT[:, cc, sl],
                                                     start=(cc == 0), stop=(cc == NCH - 1)),
                 reads=[t_wb, t_hT], writes=[t_pj[k]])
        post(pj[k], t_pj[k], tq, sl)


def qknorm_post(K, W, gain_ap, dst, t_dst):
    c = K.c
    TQ = K.TQ

    def post(ps, t_ps, tq, sl):
        c.op("act", lambda e: e.activation(out=W.sq[:], in_=ps[:, 0:TQ], func=AF.Square), reads=[t_ps], writes=[W.t_sq])
        c.op("pe", lambda e: e.matmul(W.ss[:, 0:TQ], lhsT=K.ones_bf[:], rhs=W.sq[:], start=True, stop=True),
             reads=[W.t_sq, K.t_ones], writes=[W.t_ss])
        c.op("act", lambda e: e.activation(out=W.rs[:], in_=W.ss[:, 0:TQ], func=AF.Sqrt, scale=1.0 / 128, bias=EPS),
             reads=[W.t_ss], writes=[W.t_rs])
        c.op("dve", lambda e: e.reciprocal(out=W.rinv[:], in_=W.rs[:]), reads=[W.t_rs], writes=[W.t_rinv])
        c.op("dve", lambda e: e.scalar_tensor_tensor(out=dst[:, sl], in0=ps[:, 0:TQ], scalar=gain_ap, in1=W.rinv[:],
                                                     op0=ALU.mult, op1=ALU.mult),
             reads=[t_ps, W.t_rinv, K.t_misc, K.t_cst], writes=[t_dst])
    return post


def proj_tm(K, wb, t_wb, dv, hT, t_hT, pj, t_pj, V, t_V):
    c = K.c
    per = 512 // dv
    for tb in range(K.NB):
        k = K.pjcnt % 2
        if tb % per == 0:
            K.pjcnt += 1
            k = (K.pjcnt - 1) % 2
            cur = (pj[k], t_pj[k])
        o = (tb % per) * dv
        for cc in range(NCH):
            c.op("pe", lambda e, cc=cc, cur=cur, o=o: e.matmul(cur[0][:, o:o + dv], lhsT=hT[:, cc, tb * 128:(tb + 1) * 128],
                                                               rhs=wb[:, cc, 0:dv], start=(cc == 0), stop=(cc == NCH - 1)),
                 reads=[t_wb, t_hT], writes=[cur[1]])
        if tb % per == per - 1 or tb == K.NB - 1:
            n = (tb % per) + 1
            tb0 = tb - (tb % per)
            c.op("act", lambda e, cur=cur, n=n, tb0=tb0: e.activation(
                out=V[:, tb0:tb0 + n, 0:dv], in_=cur[0][:, 0:n * dv].rearrange("p (a b) -> p a b", b=dv), func=AF.Copy),
                reads=[cur[1]], writes=[t_V])


def attn_core(K, W, qT, t_q, kT, t_k, V, t_V, ndv, mask, t_mask, fin, AB=None):
    c = K.c
    TQ, NQ, QB, PADL = K.TQ, K.NQ, K.QB, K.PADL
    for qc in range(NQ):
        sl = slice(qc * TQ, (qc + 1) * TQ)
        nkb = (qc + 1) * QB
        for kb in range(nkb):
            it = W.it
            W.it += 1
            st, t_st = W.st[it % 2], W.t_st[it % 2]
            ks = slice(kb * 128, (kb + 1) * 128)
            c.op("pe", lambda e, st=st: e.matmul(st[:, 0:TQ], lhsT=kT[:, ks], rhs=qT[:, sl], start=True, stop=(AB is None)),
                 reads=[t_k, t_q], writes=[t_st])
            diag = kb >= qc * QB
            J0 = qc * TQ - kb * 128 + PADL
            if AB is not None:
                A, B, t_AB = AB
                c.op("pe", lambda e, st=st: e.matmul(st[:, 0:TQ], lhsT=A[:, ks], rhs=B[:, sl], start=False, stop=(not diag)),
                     reads=[t_AB], writes=[t_st])
                if diag:
                    c.op("pe", lambda e, st=st: e.matmul(st[:, 0:TQ], lhsT=K.ident_bf[:], rhs=K.tri[:, J0:J0 + TQ],
                                                         start=False, stop=True),
                         reads=[K.t_identb, K.t_tri], writes=[t_st])
            ptb, t_ptb = W.ptb[it % 2], W.t_ptb[it % 2]
            if mask is not None:
                ptf, t_ptf = W.ptf[it % 2], W.t_ptf[it % 2]
                c.op("act", lambda e, st=st, ptf=ptf: e.activation(out=ptf[:], in_=st[:, 0:TQ], func=AF.Exp),
                     reads=[t_st], writes=[t_ptf])
                mk, t_mk = mask[:, J0:J0 + TQ], t_mask
                c.op("dve", lambda e, ptf=ptf, ptb=ptb, mk=mk: e.tensor_tensor(out=ptb[:], in0=ptf[:], in1=mk, op=ALU.mult),
                     reads=[t_ptf, t_mk], writes=[t_ptb])
            else:
                c.op("act", lambda e, st=st, ptb=ptb: e.activation(out=ptb[:], in_=st[:, 0:TQ], func=AF.Exp),
                     reads=[t_st], writes=[t_ptb])
            for j in range(ndv):
                c.op("pe", lambda e, j=j, ptb=ptb: e.matmul(W.o[j][:, 0:TQ], lhsT=V[:, kb, j * 128:(j + 1) * 128], rhs=ptb[:],
                                                            start=(kb == 0), stop=(kb == nkb - 1)),
                     reads=[t_V, t_ptb], writes=[W.t_o[j]])
            c.op("pe", lambda e, ptb=ptb: e.matmul(W.den[:, 0:TQ], lhsT=K.ones_bf[:], rhs=ptb[:],
                                                   start=(kb == 0), stop=(kb == nkb - 1)),
                 reads=[K.t_ones, t_ptb], writes=[W.t_den])
        c.op("dve", lambda e: e.reciprocal(out=W.rden[:], in_=W.den[:, 0:TQ]), reads=[W.t_den], writes=[W.t_rden])
        fin(qc, sl)


class WS:
    pass


def attn_ws(K, es):
    c = K.c
    TQ, S = K.TQ, K.S
    W = WS()
    W.it = 0
    W.pj = [PS(K, es, "pj") for _ in range(2)]
    W.t_pj = [c.tok("pj0"), c.tok("pj1")]
    W.ss = PS(K, es, "ss")
    W.t_ss = c.tok("ss")
    W.st = [PS(K, es, "st") for _ in range(2)]
    W.t_st = [c.tok("st0"), c.tok("st1")]
    W.o = [PS(K, es, "o") for _ in range(2)]
    W.t_o = [c.tok("o0"), c.tok("o1")]
    W.den = PS(K, es, "den")
    W.t_den = c.tok("den")
    W.sq = SB(K, es, "sqn", [128, TQ], BF16)
    W.t_sq = c.tok("sqn")
    W.rs = SB(K, es, "rsn", [128, TQ], F32)
    W.t_rs = c.tok("rsn")
    W.rinv = SB(K, es, "rinvn", [128, TQ], F32)
    W.t_rinv = c.tok("rinvn")
    W.ptf = [SB(K, es, "ptf", [128, TQ], F32) for _ in range(2)]
    W.t_ptf = [c.tok("ptf0"), c.tok("ptf1")]
    W.ptb = [SB(K, es, "ptb", [128, TQ], BF16) for _ in range(2)]
    W.t_ptb = [c.tok("ptb0"), c.tok("ptb1")]
    W.rden = SB(K, es, "rden", [128, TQ], F32)
    W.t_rden = c.tok("rden")
    W.qT = SB(K, es, "qT", [128, S], BF16)
    W.t_qT = c.tok("qT")
    W.kT = SB(K, es, "kT", [128, S], BF16)
    W.t_kT = c.tok("kT")
    W.V = SB(K, es, "V", [128, K.NB, 256], BF16)
    W.t_V = c.tok("V")
    W.ob = SB(K, es, "ob", [128, 2, S], BF16)
    W.t_ob = c.tok("ob")
    W.G = SB(K, es, "G", [128, K.MW], F32)
    W.t_G = c.tok("G")
    return W


def mixer(K, layer, xin, t_xin, xout, t_xout):
    c = K.c
    S, TQ, NQ, NB = K.S, K.TQ, K.NQ, K.NB
    K.pjcnt = 0
    with ExitStack() as es:
        hT = SB(K, es, "hT", [128, NCH, S], BF16)
        t_hT = c.tok("hT")
        norm_mod(K, xin, t_xin, layer, 0, hT, t_hT)
        W = attn_ws(K, es)
        trif = SB(K, es, "trif", [128, K.PADL + TQ], F32)
        t_trif = c.tok("trif")
        c.dma(trif[:], K.d_triT[:, :], writes=[t_trif])
        K.tri = SB(K, es, "tri", [128, K.PADL + TQ], BF16)
        K.t_tri = c.tok("tri")
        c.op("dve", lambda e: e.tensor_copy(out=K.tri[:], in_=trif[:]), reads=[t_trif], writes=[K.t_tri])
        if layer == 0:
            mixer0_heads(K, es, W, hT, t_hT)
        else:
            mixer1_heads(K, es, W, hT, t_hT)
        c.barrier()
    out_proj(K, K.d_wout0 if layer == 0 else K.d_wout1, xin, t_xin, xout, t_xout, layer * 96 + 2 * 16)


def store_mixed(K, W, j, row0):
    K.c.dma(K.d_mixed[row0:row0 + 128, :], W.ob[:, j, :], reads=[W.t_ob], writes=[K.t_mixed])


def mixer0_heads(K, es, W, hT, t_hT):
    c = K.c
    S, TQ, NQ, NB = K.S, K.TQ, K.NQ, K.NB
    M = K.misc
    with ExitStack() as e2:
        wffb = SB(K, e2, "wffb", [128, 16, 8], BF16)
        wfff = SB(K, e2, "wfff", [128, 16, 8], F32)
        t_wff = c.tok("wff")
        c.dma(wfff[:], K.d_wff[:, :, :], writes=[t_wff])
        t_wffb = c.tok("wffb")
        c.op("dve", lambda e: e.tensor_copy(out=wffb[:], in_=wfff[:]), reads=[t_wff], writes=[t_wffb])
        ef = SB(K, e2, "ef", [8, S], F32)
        t_ef = c.tok("ef")
        lf = SB(K, e2, "lf", [8, S], F32)
        t_lf = c.tok("lf")
        Cf = SB(K, e2, "Cf", [8, S], F32)
        t_Cf = c.tok("Cf")
        on8 = SB(K, e2, "on8", [8, S], F32)
        t_on8 = c.tok("on8")
        c.op("pool", lambda e: e.memset(on8[:], 1.0), writes=[t_on8])
        for tq in range(NQ):
            sl = slice(tq * TQ, (tq + 1) * TQ)
            for cc in range(NCH):
                c.op("pe", lambda e, cc=cc: e.matmul(W.ss[0:8, 0:TQ], lhsT=wffb[:, cc, :], rhs=hT[:, cc, sl],
                                                     start=(cc == 0), stop=(cc == NCH - 1)),
                     reads=[t_wffb, t_hT], writes=[W.t_ss])
            c.op("act", lambda e: e.activation(out=ef[:, sl], in_=W.ss[0:8, 0:TQ], func=AF.Exp, scale=-1.0, bias=M[0:8, 3:4]),
                 reads=[W.t_ss, K.t_misc], writes=[t_ef])
        c.op("act", lambda e: e.activation(out=lf[:], in_=ef[:], func=AF.Ln, scale=1.0, bias=1.0), reads=[t_ef], writes=[t_lf])
        c.op("dve", lambda e: e.tensor_tensor_scan(out=Cf[:], data0=on8[:], data1=lf[:], initial=0.0, op0=ALU.mult, op1=ALU.add),
             reads=[t_on8, t_lf], writes=[t_Cf])
        hl = SB(K, e2, "hl", [8, 4, S], BF16)
        t_hl = c.tok("hl")
        c.op("dve", lambda e: e.tensor_copy(out=hl[:, 0, :], in_=Cf[:]), reads=[t_Cf], writes=[t_hl])
        c.op("dve", lambda e: e.tensor_tensor(out=lf[:], in0=Cf[:], in1=hl[:, 0, :], op=ALU.subtract),
             reads=[t_Cf, t_hl], writes=[t_lf])
        c.op("dve", lambda e: e.tensor_copy(out=hl[:, 1, :], in_=lf[:]), reads=[t_lf], writes=[t_hl])
        c.op("dve", lambda e: e.tensor_scalar(out=hl[:, 2:4, :], in0=hl[:, 0:2, :], scalar1=-1.0, scalar2=None, op0=ALU.mult),
             reads=[t_hl], writes=[t_hl])
        for r in range(4):
            c.dma(K.d_fsc[r], hl[:, r, :], reads=[t_hl], writes=[K.t_fsc])
        c.barrier()
    A = SB(K, es, "Afox", [128, S], BF16)
    B = SB(K, es, "Bfox", [128, S], BF16)
    t_AB = c.tok("AB")
    c.op("pool", lambda e: e.memset(A[:], 0.0), writes=[t_AB])
    c.op("pool", lambda e: e.memset(B[:], 0.0), writes=[t_AB])
    c.op("pool", lambda e: e.memset(A[0:4, :], 1.0), writes=[t_AB])
    c.op("pool", lambda e: e.memset(B[0:4, :], 1.0), writes=[t_AB])

    for hh in range(8):
        wq, t_wq = get_weights(K, [K.d_win[hh]])
        proj_fm(K, wq, t_wq, 0, hT, t_hT, W.pj, W.t_pj, qknorm_post(K, W, M[:, 0:1], W.qT, W.t_qT))
        wk, t_wk = get_weights(K, [K.d_win[8 + hh]])
        proj_fm(K, wk, t_wk, 0, hT, t_hT, W.pj, W.t_pj, qknorm_post(K, W, K.cst[:, C_FOXG + 1:C_FOXG + 2], W.kT, W.t_kT))
        wv, t_wv = get_weights(K, [K.d_win[16 + hh]])
        proj_tm(K, wv, t_wv, 128, hT, t_hT, W.pj, W.t_pj, W.V, W.t_V)
        c.dma(A[0:2, :], K.d_fsc[0:2, hh, :], reads=[K.t_fsc], writes=[t_AB])
        c.dma(B[2:4, :], K.d_fsc[2:4, hh, :], reads=[K.t_fsc], writes=[t_AB])

        def fin(qc, sl):
            c.op("dve", lambda e: e.tensor_tensor(out=W.ob[:, 0, sl], in0=W.o[0][:, 0:TQ], in1=W.rden[:], op=ALU.mult),
                 reads=[W.t_o[0], W.t_rden], writes=[W.t_ob])
        attn_core(K, W, W.qT, W.t_qT, W.kT, W.t_kT, W.V, W.t_V, 1, None, None, fin, AB=(A, B, t_AB))
        store_mixed(K, W, 0, hh * 128)

    O = [SB(K, es, "Od", [128, 2, S], F32) for _ in range(2)]
    t_O = [c.tok("Od0"), c.tok("Od1")]
    for hh in range(4):
        c.dma(W.G[:], K.d_biasT[hh], writes=[W.t_G])
        c.op("act", lambda e: e.activation(out=W.G[:], in_=W.G[:], func=AF.Exp), reads=[W.t_G], writes=[W.t_G])
        wv, t_wv = get_weights(K, [K.d_win[40 + 2 * hh], K.d_win[41 + 2 * hh]])
        proj_tm(K, wv, t_wv, 256, hT, t_hT, W.pj, W.t_pj, W.V, W.t_V)
        for m in range(2):
            wq, t_wq = get_weights(K, [K.d_win[24 + 2 * hh + m]])
            proj_fm(K, wq, t_wq, 0, hT, t_hT, W.pj, W.t_pj, qknorm_post(K, W, M[:, 1:2], W.qT, W.t_qT))
            wk, t_wk = get_weights(K, [K.d_win[32 + 2 * hh + m]])
            proj_fm(K, wk, t_wk, 0, hT, t_hT, W.pj, W.t_pj,
                    qknorm_post(K, W, K.cst[:, C_DIFFG + 1:C_DIFFG + 2], W.kT, W.t_kT))

            def fin(qc, sl, m=m):
                for j in range(2):
                    c.op("dve", lambda e, j=j: e.tensor_tensor(out=O[m][:, j, sl], in0=W.o[j][:, 0:TQ], in1=W.rden[:], op=ALU.mult),
                         reads=[W.t_o[j], W.t_rden], writes=[t_O[m]])
            attn_core(K, W, W.qT, W.t_qT, W.kT, W.t_kT, W.V, W.t_V, 2, W.G, W.t_G, fin)
        for tq in range(NQ):
            sl = slice(tq * TQ, (tq + 1) * TQ)
            for j in range(2):
                c.op("dve", lambda e, j=j: e.scalar_tensor_tensor(out=O[0][:, j, sl], in0=O[1][:, j, sl], scalar=M[:, 4:5],
                                                                  in1=O[0][:, j, sl], op0=ALU.mult, op1=ALU.add),
                     reads=[t_O[1], t_O[0], K.t_misc], writes=[t_O[0]])
                c.op("act", lambda e, j=j: e.activation(out=W.sq[:], in_=O[0][:, j, sl], func=AF.Square),
                     reads=[t_O[0]], writes=[W.t_sq])
                c.op("pe", lambda e, j=j: e.matmul(W.ss[:, 0:TQ], lhsT=K.ones_bf[:], rhs=W.sq[:], start=(j == 0), stop=(j == 1)),
                     reads=[W.t_sq, K.t_ones], writes=[W.t_ss])
            c.op("act", lambda e: e.activation(out=W.rs[:], in_=W.ss[:, 0:TQ], func=AF.Sqrt, scale=1.0 / 256, bias=EPS),
                 reads=[W.t_ss], writes=[W.t_rs])
            c.op("dve", lambda e: e.reciprocal(out=W.rinv[:], in_=W.rs[:]), reads=[W.t_rs], writes=[W.t_rinv])
            for j in range(2):
                c.op("dve", lambda e, j=j: e.scalar_tensor_tensor(out=W.ob[:, j, sl], in0=O[0][:, j, sl], scalar=M[:, 5 + j:6 + j],
                                                                  in1=W.rinv[:], op0=ALU.mult, op1=ALU.mult),
                     reads=[t_O[0], W.t_rinv, K.t_misc], writes=[W.t_ob])
        for j in range(2):
            store_mixed(K, W, j, 1024 + hh * 256 + j * 128)


def mixer1_heads(K, es, W, hT, t_hT):
    c = K.c
    S, TQ, NQ, NB = K.S, K.TQ, K.NQ, K.NB
    M = K.misc
    mult = SB(K, es, "mult", [128, K.MW], F32)
    t_mult = c.tok("mult")
    c.dma(mult[:], K.d_multT[:, :], writes=[t_mult])
    for hh in range(16):
        c.dma(W.G[:], K.d_biasT[hh], writes=[W.t_G])
        c.op("act", lambda e: e.activation(out=W.G[:], in_=W.G[:], func=AF.Exp), reads=[W.t_G], writes=[W.t_G])
        c.op("pool", lambda e: e.tensor_tensor(out=W.G[:], in0=W.G[:], in1=mult[:], op=ALU.mult),
             reads=[W.t_G, t_mult], writes=[W.t_G])
        wq, t_wq = get_weights(K, [K.d_wqkv[hh]])
        proj_fm(K, wq, t_wq, 0, hT, t_hT, W.pj, W.t_pj, qknorm_post(K, W, M[:, 2:3], W.qT, W.t_qT))
        wk, t_wk = get_weights(K, [K.d_wqkv[16 + hh]])
        proj_fm(K, wk, t_wk, 0, hT, t_hT, W.pj, W.t_pj, qknorm_post(K, W, K.cst[:, C_ODDG + 1:C_ODDG + 2], W.kT, W.t_kT))
        wv, t_wv = get_weights(K, [K.d_wqkv[32 + hh]])
        proj_tm(K, wv, t_wv, 128, hT, t_hT, W.pj, W.t_pj, W.V, W.t_V)

        def fin(qc, sl):
            c.op("dve", lambda e: e.tensor_tensor(out=W.ob[:, 0, sl], in0=W.o[0][:, 0:TQ], in1=W.rden[:], op=ALU.mult),
                 reads=[W.t_o[0], W.t_rden], writes=[W.t_ob])
        attn_core(K, W, W.qT, W.t_qT, W.kT, W.t_kT, W.V, W.t_V, 1, W.G, W.t_G, fin)
        store_mixed(K, W, 0, hh * 128)


def out_proj(K, wsrc, xin, t_xin, xout, t_xout, gcol):
    c = K.c
    S, TQ, NQ = K.S, K.TQ, K.NQ
    K.pjcnt = 0
    with ExitStack() as es:
        mT = SB(K, es, "mT", [128, NCH, S], BF16)
        t_mT = c.tok("mT")
        mv = K.d_mixed.rearrange("(c p) t -> p c t", p=128)
        for cc in range(NCH):
            c.dma(mT[:, cc, :], mv[:, cc, :], reads=[K.t_mixed], writes=[t_mT])
        pj = [PS(K, es, "pjo") for _ in range(2)]
        t_pj = [c.tok("pjo0"), c.tok("pjo1")]
        xc = [SB(K, es, "xc", [128, TQ], F32) for _ in range(2)]
        t_xc = [c.tok("xc0"), c.tok("xc1")]
        xo = [SB(K, es, "xo", [128, TQ], F32) for _ in range(2)]
        t_xo = [c.tok("xo0"), c.tok("xo1")]
        it = 0
        for nb in range(16):
            wb, t_wb = get_weights(K, [wsrc[nb]])
            rows = slice(nb * 128, (nb + 1) * 128)

            def post(ps, t_ps, tq, sl, nb=nb, rows=rows):
                nonlocal it
                k = it % 2
                it += 1
                c.dma(xc[k][:], xin[rows, sl], reads=([t_xin] if t_xin is not None else []), writes=[t_xc[k]])
                c.op("dve", lambda e: e.scalar_tensor_tensor(out=xo[k][:], in0=ps[:, 0:TQ], scalar=K.mod[:, gcol + nb:gcol + nb + 1],
                                                             in1=xc[k][:], op0=ALU.mult, op1=ALU.add),
                     reads=[t_ps, t_xc[k], K.t_mod], writes=[t_xo[k]])
                c.dma(xout[rows, sl], xo[k][:], reads=[t_xo[k]], writes=[t_xout])
            proj_fm(K, wb, t_wb, 0, mT, t_mT, pj, t_pj, post)
        c.barrier()


def peer(K, layer, xin, t_xin, xout, t_xout):
    c = K.c
    S, TQ, NQ, NB, QB = K.S, K.TQ, K.NQ, K.NB, K.QB
    K.pjcnt = 0
    gcol = layer * 96 + 5 * 16
    with ExitStack() as pe_s:
        idxT = SB(K, pe_s, "idxT", [128, S], U32)
        t_idxT = c.tok("idxT")
        gT = SB(K, pe_s, "gT", [128, S], F32)
        t_gT = c.tok("gT")
        with ExitStack() as es:
            hT = SB(K, es, "hT2", [128, NCH, S], BF16)
            t_hT = c.tok("hT2")
            norm_mod(K, xin, t_xin, layer, 1, hT, t_hT)
            keys = SB(K, es, "keys", [128, 2, 128], F32)
            t_keys = c.tok("keys")
            c.dma(keys[:], K.d_keys[layer], writes=[t_keys])
            pj = [PS(K, es, "pjq") for _ in range(2)]
            t_pj = [c.tok("pjq0"), c.tok("pjq1")]
            scps = [PS(K, es, "scps") for _ in range(4)]
            t_scps = [c.tok("scps%d" % i) for i in range(4)]
            tps = [PS(K, es, "tps", [128, 1024], BF16), PS(K, es, "tps2")]
            t_tps = [c.tok("tpsA"), c.tok("tpsB")]
            qf = SB(K, es, "qf", [128, 16, TQ], F32)
            t_qf = c.tok("qf")
            sc = SB(K, es, "sc", [128, 2048], F32)
            t_sc = c.tok("sc")
            sc2 = SB(K, es, "sc2", [128, 256], F32)
            t_sc2 = c.tok("sc2")
            tops = SB(K, es, "tops", [128, 16, 16], F32)
            t_tops = c.tok("tops")
            tiu = SB(K, es, "tiu", [128, 16, 16], U32)
            t_tiu = c.tok("tiu")
            topi = SB(K, es, "topi", [128, 16, 16], F32)
            t_topi = c.tok("topi")
            cs = SB(K, es, "cs", [128, 256], F32)
            t_cs = c.tok("cs")
            ci = SB(K, es, "ci", [128, 256], F32)
            t_ci = c.tok("ci")
            best = SB(K, es, "best", [128, 8, 16], F32)
            t_best = c.tok("best")
            eq3 = SB(K, es, "eq3", [128, 16, 256], F32)
            t_eq3 = c.tok("eq3")
            idxf = SB(K, es, "idxf", [128, 128], F32)
            t_idxf = c.tok("idxf")
            posu = SB(K, es, "posu", [128, 8, 16], U32)
            t_posu = c.tok("posu")
            posf = SB(K, es, "posf", [128, 5, 128], F32)
            t_posf = c.tok("posf")
            eg = SB(K, es, "eg", [128, 8, 16], F32)
            t_eg = c.tok("eg")
            gates = SB(K, es, "gates", [128, 128], F32)
            t_gates = c.tok("gates")
            sm = SB(K, es, "sm", [128, 32], F32)
            t_sm = c.tok("sm")
            h2b = [SB(K, es, "h2b", [128, D], BF16) for _ in range(2)]
            t_h2b = [c.tok("h2b0"), c.tok("h2b1")]

            for tq in range(NQ):
                sl = slice(tq * TQ, (tq + 1) * TQ)
                for u in range(16):
                    wb, t_wb = get_weights(K, [K.d_wquery[layer, u]])

                    def post(ps, t_ps, tq_, sl_, u=u):
                        c.op("act", lambda e: e.activation(out=qf[:, u, :], in_=ps[:, 0:TQ], func=AF.Copy),
                             reads=[t_ps], writes=[t_qf])
                    k = K.pjcnt % 2
                    K.pjcnt += 1
                    for cc in range(NCH):
                        c.op("pe", lambda e, cc=cc, k=k, wb=wb: e.matmul(pj[k][:, 0:TQ], lhsT=wb[:, cc, 0:128], rhs=hT[:, cc, sl],
                                                                         start=(cc == 0), stop=(cc == NCH - 1)),
                             reads=[t_wb, t_hT], writes=[t_pj[k]])
                    post(pj[k], t_pj[k], tq, sl)
                for tbl in range(QB):
                    tb = tq * QB + tbl
                    ts = slice(tbl * 128, (tbl + 1) * 128)
                    tsg = slice(tb * 128, (tb + 1) * 128)
                    hb_ = h2b[tb % 2]
                    for half in range(2):
                        for cc8 in range(8):
                            cc = half * 8 + cc8
                            c.op("pe", lambda e, cc=cc, cc8=cc8: e.transpose(out=tps[0][:, cc8 * 128:(cc8 + 1) * 128],
                                                                             in_=hT[:, cc, tsg], identity=K.ident_bf[:]),
                                 reads=[t_hT, K.t_identb], writes=[t_tps[0]])
                        c.op("act", lambda e, half=half, hb_=hb_: e.activation(out=hb_[:, half * 1024:(half + 1) * 1024],
                                                                               in_=tps[0][:, 0:1024], func=AF.Copy),
                             reads=[t_tps[0]], writes=[t_h2b[tb % 2]])
                    c.dma(K.d_h2[tsg, :], hb_[:], reads=[t_h2b[tb % 2]], writes=[K.t_h2])
                    for u in range(16):
                        c.op("pe", lambda e, u=u: e.matmul(scps[u // 4][:, (u % 4) * 128:(u % 4 + 1) * 128], lhsT=qf[:, u, ts],
                                                           rhs=keys[:, u % 2, :], start=True, stop=True),
                             reads=[t_qf, t_keys], writes=[t_scps[u // 4]])
                    for b4 in range(4):
                        c.op("act", lambda e, b4=b4: e.activation(out=sc[:, b4 * 512:(b4 + 1) * 512], in_=scps[b4][:, :], func=AF.Copy),
                             reads=[t_scps[b4]], writes=[t_sc])
                    for u in range(16):
                        su = sc[:, u * 128:(u + 1) * 128]
                        c.op("dve", lambda e, u=u, su=su: e.max(out=tops[:, u, 0:8], in_=su), reads=[t_sc], writes=[t_tops])
                        c.op("dve", lambda e, u=u, su=su: e.max_index(out=tiu[:, u, 0:8], in_max=tops[:, u, 0:8], in_values=su),
                             reads=[t_sc, t_tops], writes=[t_tiu])
                        c.op("dve", lambda e, u=u, su=su: e.match_replace(out=sc2[:, 0:128], in_to_replace=tops[:, u, 0:8],
                                                                          in_values=su, imm_value=-1e30),
                             reads=[t_sc, t_tops], writes=[t_sc2])
                        c.op("dve", lambda e, u=u: e.max(out=tops[:, u, 8:16], in_=sc2[:, 0:128]), reads=[t_sc2], writes=[t_tops])
                        c.op("dve", lambda e, u=u: e.max_index(out=tiu[:, u, 8:16], in_max=tops[:, u, 8:16], in_values=sc2[:, 0:128]),
                             reads=[t_sc2, t_tops], writes=[t_tiu])
                    c.op("dve", lambda e: e.tensor_copy(out=topi[:], in_=tiu[:]), reads=[t_tiu], writes=[t_topi])
                    for h in range(8):
                        s0, s1 = tops[:, 2 * h, :], tops[:, 2 * h + 1, :]
                        cs3 = cs[:].rearrange("p (a b) -> p a b", b=16)
                        c.op("dve", lambda e, s0=s0, s1=s1, cs3=cs3: e.tensor_tensor(
                            out=cs3, in0=mkap(s0, [[1, 16], [0, 16]]), in1=mkap(s1, [[0, 16], [1, 16]]), op=ALU.add),
                            reads=[t_tops], writes=[t_cs])
                        c.op("dve", lambda e, h=h: e.max(out=best[:, h, 0:8], in_=cs[:]), reads=[t_cs], writes=[t_best])
                        c.op("dve", lambda e, h=h: e.max_index(out=posu[:, h, 0:8], in_max=best[:, h, 0:8], in_values=cs[:]),
                             reads=[t_cs, t_best], writes=[t_posu])
                        c.op("dve", lambda e, h=h: e.match_replace(out=sc2[:], in_to_replace=best[:, h, 0:8], in_values=cs[:],
                                                                   imm_value=-1e30), reads=[t_cs, t_best], writes=[t_sc2])
                        c.op("dve", lambda e, h=h: e.max(out=best[:, h, 8:16], in_=sc2[:]), reads=[t_sc2], writes=[t_best])
                        c.op("dve", lambda e, h=h: e.max_index(out=posu[:, h, 8:16], in_max=best[:, h, 8:16], in_values=sc2[:]),
                             reads=[t_sc2, t_best], writes=[t_posu])
                    T3 = eq3[:, 0:8, :].rearrange("p h (k a) -> p (h k) a", a=16)
                    T4 = eq3[:, 0:8, :].rearrange("p h (k a) -> p h k a", a=16)
                    iota_b = mkap(K.cst[:, C_IOTA:C_IOTA + 16], [[0, 128], [1, 16]])
                    lo_b = mkap(K.cst[:, C_LO16:C_LO16 + 16], [[0, 128], [1, 16]])
                    c.op("dve", lambda e: e.tensor_copy(out=posf[:, 0, :], in_=posu[:].rearrange("p h k -> p (h k)")),
                         reads=[t_posu], writes=[t_posf])
                    c.op("dve", lambda e: e.tensor_tensor(out=T3, in0=mkap(posf[:, 0, :], [[1, 128], [0, 16]]), in1=lo_b, op=ALU.is_ge),
                         reads=[t_posf, K.t_cst], writes=[t_eq3])
                    c.op("dve", lambda e: e.tensor_reduce(out=posf[:, 1, :], in_=T3, axis=AX.X, op=ALU.add),
                         reads=[t_eq3], writes=[t_posf])
                    c.op("dve", lambda e: e.tensor_scalar(out=posf[:, 1, :], in0=posf[:, 1, :], scalar1=-1.0, scalar2=None, op0=ALU.add),
                         reads=[t_posf], writes=[t_posf])
                    c.op("dve", lambda e: e.scalar_tensor_tensor(out=posf[:, 2, :], in0=posf[:, 1, :], scalar=-16.0, in1=posf[:, 0, :],
                                                                 op0=ALU.mult, op1=ALU.add), reads=[t_posf], writes=[t_posf])
                    for w_, (src_, dst_) in enumerate(((1, 3), (2, 4))):
                        c.op("dve", lambda e, src_=src_: e.tensor_tensor(out=T3, in0=mkap(posf[:, src_, :], [[1, 128], [0, 16]]),
                                                                         in1=iota_b, op=ALU.is_equal),
                             reads=[t_posf, K.t_cst], writes=[t_eq3])
                        c.op("dve", lambda e, w_=w_: e.tensor_tensor(out=T4, in0=T4, in1=mkap(topi[:, w_, :], [[32, 8], [0, 16], [1, 16]]),
                                                                     op=ALU.mult), reads=[t_eq3, t_topi], writes=[t_eq3])
                        c.op("dve", lambda e, dst_=dst_: e.tensor_reduce(out=posf[:, dst_, :], in_=T3, axis=AX.X, op=ALU.add),
                             reads=[t_eq3], writes=[t_posf])
                    c.op("dve", lambda e: e.scalar_tensor_tensor(out=idxf[:], in0=posf[:, 3, :], scalar=128.0, in1=posf[:, 4, :],
                                                                 op0=ALU.mult, op1=ALU.add), reads=[t_posf], writes=[t_idxf])
                    c.op("dve", lambda e: e.tensor_scalar(out=sm[:, 0:8], in0=best[:, :, 0], scalar1=-1.0, scalar2=None, op0=ALU.mult),
                         reads=[t_best], writes=[t_sm])
                    for h in range(8):
                        c.op("act", lambda e, h=h: e.activation(out=eg[:, h, :], in_=best[:, h, :], func=AF.Exp, bias=sm[:, h:h + 1],
                                                                scale=1.0, accum_out=sm[:, 8 + h:9 + h]),
                             reads=[t_best, t_sm], writes=[t_eg, t_sm])
                    c.op("dve", lambda e: e.reciprocal(out=sm[:, 16:24], in_=sm[:, 8:16]), reads=[t_sm], writes=[t_sm])
                    c.op("dve", lambda e: e.tensor_tensor(out=gates[:].rearrange("p (h k) -> p h k", k=16), in0=eg[:],
                                                          in1=mkap(sm[:, 16:24], [[1, 8], [0, 16]]), op=ALU.mult),
                         reads=[t_eg, t_sm], writes=[t_gates])
                    c.op("dve", lambda e: e.tensor_scalar(out=idxf[:], in0=idxf[:], scalar1=0.0, scalar2=float(NEXP - 1),
                                                          op0=ALU.max, op1=ALU.min), reads=[t_idxf], writes=[t_idxf])
                    c.op("pe", lambda e: e.transpose(out=tps[1][:, 0:128], in_=idxf[:], identity=K.ident_f),
                         reads=[t_idxf, K.t_cst], writes=[t_tps[1]])
                    c.op("pe", lambda e: e.transpose(out=tps[1][:, 128:256], in_=gates[:], identity=K.ident_f),
                         reads=[t_gates, K.t_cst], writes=[t_tps[1]])
                    c.op("dve", lambda e: e.tensor_copy(out=idxT[:, tsg], in_=tps[1][:, 0:128]), reads=[t_tps[1]], writes=[t_idxT])
                    c.op("dve", lambda e: e.tensor_copy(out=gT[:, tsg], in_=tps[1][:, 128:256]),
                         reads=[t_tps[1]], writes=[t_gT])
            c.barrier()

        with ExitStack() as es:
            NS = 4
            UV = [SB(K, es, "UV", [128, 2 * D], BF16) for _ in range(NS)]
            t_UV = [c.tok("UV%d" % i) for i in range(NS)]
            hbc = [SB(K, es, "hbc", [128, D], BF16) for _ in range(NS)]
            t_hbc = [c.tok("hbc%d" % i) for i in range(NS)]
            h2 = [SB(K, es, "h2", [128, D], BF16) for _ in range(2)]
            t_h2s = [c.tok("h2s0"), c.tok("h2s1")]
            yT = [PS(K, es, "yT", [128, 16, 128], F32) for _ in range(2)]
            t_yT = [c.tok("yT0"), c.tok("yT1")]
            junk = SB(K, es, "junk", [128, D], BF16)
            t_junk = c.tok("junk")
            acc = [SB(K, es, "acc", [128, 8], F32) for _ in range(2)]
            t_acc = [c.tok("acc0"), c.tok("acc1")]
            wv_ = [SB(K, es, "wv", [128, 2], BF16) for _ in range(2)]
            t_wv = [c.tok("wv0"), c.tok("wv1")]
            xs = SB(K, es, "xs", [128, NCH, 128], F32)
            t_xs = c.tok("xs")
            xo = SB(K, es, "xo2", [128, NCH, 128], F32)
            t_xo = c.tok("xo2")
            xiv = xin.rearrange("(c p) t -> p c t", p=128)
            xov = xout.rearrange("(c p) t -> p c t", p=128)
            for tb in range(NB):
                tsg = slice(tb * 128, (tb + 1) * 128)
                yTt, t_yTt = yT[tb % 2], t_yT[tb % 2]
                h2t, t_h2t = h2[tb % 2], t_h2s[tb % 2]
                c.dma(h2t[:], K.d_h2[tsg, :], reads=[K.t_h2], writes=[t_h2t])
                c.dma(xs[:], xiv[:, :, tsg], reads=([t_xin] if t_xin is not None else []), writes=[t_xs])
                for j in range(128):
                    t = tb * 128 + j
                    s = t % NS
                    s2 = t % 2
                    c.dma(None, None, reads=[t_idxT, K.t_uvb[layer]], writes=[t_UV[s]], q="pool",
                          fn=lambda e, s=s, t=t: e.indirect_dma_start(
                              out=UV[s][:], out_offset=None, in_=K.d_uvb[layer][:, :],
                              in_offset=bass.IndirectOffsetOnAxis(ap=idxT[:, t:t + 1], axis=0),
                              bounds_check=K.bc_reg, oob_is_err=False))
                    if t % 3 != 2:
                        hrow = K.d_h2[t:t + 1, :]
                        src = bass.AP(tensor=hrow.tensor, offset=hrow.offset, ap=[[0, 4], [1, D]])
                        hb0 = hbc[s][0:1, :]
                        dst = bass.AP(tensor=hb0.tensor, offset=hb0.offset, ap=[[32 * D, 4], [1, D]])
                        c.dma(dst, src, reads=[K.t_h2], writes=[t_hbc[s]])
                        c.op("dve", lambda e, s=s: e.stream_shuffle(out=hbc[s][:], in_=hbc[s][:], mask=[0] * 32),
                             reads=[t_hbc[s]], writes=[t_hbc[s]])
                    else:
                        hrow = K.d_h2[t:t + 1, :]
                        src = bass.AP(tensor=hrow.tensor, offset=hrow.offset, ap=[[0, 128], [1, D]])
                        c.dma(hbc[s][:], src, reads=[K.t_h2], writes=[t_hbc[s]])
                    c.op("dve", lambda e, s=s, s2=s2: e.scalar_tensor_tensor(
                        out=junk[:], in0=UV[s][:, 0:D], scalar=1.0, in1=hbc[s][:],
                        op0=ALU.mult, op1=ALU.mult, accum_out=acc[s2][:, 4:5]),
                        reads=[t_UV[s], t_hbc[s]], writes=[t_junk, t_acc[s2]])
                    c.op("act", lambda e, s2=s2: e.activation(out=acc[s2][:, 5:6], in_=acc[s2][:, 4:5], func=AF.Gelu),
                         reads=[t_acc[s2]], writes=[t_acc[s2]])
                    c.op("dve", lambda e, s2=s2, t=t: e.tensor_tensor(out=wv_[s2][:, 0:1], in0=acc[s2][:, 5:6], in1=gT[:, t:t + 1],
                                                                      op=ALU.mult),
                         reads=[t_acc[s2], t_gT], writes=[t_wv[s2]])
                    for cc in range(NCH):
                        c.op("pe", lambda e, cc=cc, s=s, s2=s2, j=j: e.matmul(
                            yTt[:, cc, j:j + 1], lhsT=UV[s][:, D + cc * 128:D + (cc + 1) * 128],
                            rhs=wv_[s2][:, 0:1], start=True, stop=True),
                            reads=[t_UV[s], t_wv[s2]], writes=[t_yTt])
                for cc in range(NCH):
                    c.op("dve", lambda e, cc=cc: e.scalar_tensor_tensor(out=xo[:, cc, :], in0=yTt[:, cc, :],
                                                                        scalar=K.mod[:, gcol + cc:gcol + cc + 1], in1=xs[:, cc, :],
                                                                        op0=ALU.mult, op1=ALU.add),
                         reads=[t_yTt, t_xs, K.t_mod], writes=[t_xo])
                c.dma(xov[:, :, tsg], xo[:], reads=[t_xo], writes=[t_xout])
            c.barrier()


def t5_bucket_np(dist):
    n = np.maximum(dist, 0)
    nf = np.maximum(n, 1).astype(np.float32)
    large = 16 + (np.log(nf / np.float32(16)) / np.float32(math.log(2048 / 16)) * np.float32(16)).astype(np.int32)
    large = np.minimum(large, 31)
    return np.where(n < 16, n, large)


def prep_w(W):
    n = W.shape[1]
    return np.ascontiguousarray(W.reshape(16, 128, n // 128, 128).transpose(2, 1, 0, 3))


def host_prep(inp, S, cores):
    f = lambda a: np.ascontiguousarray(np.asarray(a, dtype=np.float32))
    TQ = min(512, S)
    PADL = TQ - 128
    MW = PADL + S
    shared = {}
    w_ada = f(inp["w_ada"])
    shared["wada"] = np.ascontiguousarray(w_ada.reshape(2, 16, 128, 24, 512).transpose(0, 3, 2, 1, 4))
    w_in = f(inp["even_w_in"])[0]
    shared["win"] = prep_w(np.concatenate([w_in[:, :3072], w_in[:, 3080:]], axis=1))
    shared["wff"] = np.ascontiguousarray(w_in[:, 3072:3080].reshape(16, 128, 8).transpose(1, 0, 2))
    shared["wout0"] = prep_w(f(inp["even_w_out"])[0])
    shared["wqkv"] = prep_w(f(inp["odd_w_qkv"])[0])
    shared["wout1"] = prep_w(f(inp["odd_w_out"])[0])
    wq = f(inp["peer_w_query"])
    shared["wquery"] = np.stack([prep_w(wq[0]), prep_w(wq[1])])
    sk = f(inp["peer_sub_keys"])
    shared["keysT"] = np.ascontiguousarray(sk.transpose(0, 3, 1, 2))
    pu, pv = f(inp["peer_u"]), f(inp["peer_v"])
    shared["uv0"] = np.concatenate([pu[0], pv[0]], axis=1).reshape(64, 128, 4 * D)
    shared["uv1"] = np.concatenate([pu[1], pv[1]], axis=1).reshape(64, 128, 4 * D)
    rb = f(inp["rel_bias"])
    kl = np.arange(128)[:, None]
    jj = np.arange(MW)[None, :]
    delta = jj - PADL - kl
    bidx = t5_bucket_np(delta)
    bt = rb[bidx]
    bt = np.where((delta >= 0)[:, :, None], bt, np.float32(NEG))
    shared["biasT"] = np.ascontiguousarray(bt.transpose(2, 0, 1)).astype(np.float32)
    mult = np.zeros_like(delta, dtype=np.float32)
    for (wdw, dil) in ((128, 1), (512, 4), (2048, 16)):
        mult += ((delta >= 0) & (delta % dil == 0) & (delta // dil <= wdw // dil)).astype(np.float32)
    shared["multT"] = np.ascontiguousarray(mult)
    jj2 = np.arange(PADL + TQ)[None, :]
    shared["triT"] = np.ascontiguousarray(np.where((jj2 - PADL - kl) >= 0, np.float32(0.0), np.float32(NEG)))
    x = np.asarray(inp["x"], dtype=np.float32)
    cvec = f(inp["c"])
    b_ada = f(inp["b_ada"])
    ng = f(inp["norm_gain"])
    maps = []
    for b in cores:
        cst = np.zeros((128, NCST), np.float32)
        cst[:, C_C:C_C + 16] = cvec[b].reshape(16, 128).T
        cst[:, C_BADA:C_BADA + 192] = b_ada.reshape(2, 96, 128).transpose(2, 0, 1).reshape(128, 192)
        cst[:, C_GAIN:C_GAIN + 64] = ng.reshape(2, 2, 16, 128).transpose(3, 0, 1, 2).reshape(128, 64)
        cst[:, C_FOXG:C_FOXG + 2] = f(inp["even_fox_qk_gain"])[0].T
        cst[:, C_DIFFG:C_DIFFG + 2] = f(inp["even_diff_qk_gain"])[0].T
        cst[:, C_LAM:C_LAM + 4] = f(inp["even_diff_lambda"])[0].T
        cst[:, C_SUBLN:C_SUBLN + 2] = f(inp["even_diff_subln_gain"])[0].reshape(2, 128).T
        cst[:, C_ODDG:C_ODDG + 2] = f(inp["odd_qk_gain"])[0].T
        cst[0:8, C_BF] = f(inp["even_b_forget"])[0]
        cst[:, C_ID:C_ID + 128] = np.eye(128, dtype=np.float32)
        cst[:, C_IOTA:C_IOTA + 16] = np.arange(16, dtype=np.float32)[None, :]
        cst[:, C_LO16:C_LO16 + 16] = 16.0 * np.arange(16, dtype=np.float32)[None, :]
        m = dict(shared)
        m["cst"] = cst
        m["xT"] = np.ascontiguousarray(x[b, :S, :].T)
        maps.append(m)
    return maps


def kernel(**inputs):
    S = 2048
    cores = list(range(8))
    maps = host_prep(inputs, S, cores)
    nc = build(S)
    res = run_bass_kernel_spmd(nc, maps, core_ids=cores)
    out = np.stack([np.ascontiguousarray(res.results[i]["y"].T) for i in range(8)], axis=0)
    return out.astype(np.float32)
```

```python
import math
from contextlib import ExitStack
import numpy as np
import concourse.bass as bass
import concourse.mybir as mybir
from concourse.bass_utils import run_bass_kernel_spmd

F32 = mybir.dt.float32
BF16 = mybir.dt.bfloat16
U32 = mybir.dt.uint32
AF = mybir.ActivationFunctionType
ALU = mybir.AluOpType
AX = mybir.AxisListType

D = 2048
NCH = 16
EPS = 1e-6
NEG = -30000.0
SCALE = 128 ** -0.5
NEXP = 16384
SBUF_BCAST = False
C_C, C_BADA, C_GAIN, C_FOXG, C_DIFFG, C_LAM, C_SUBLN, C_ODDG, C_BF, C_ID, NCST = 0, 16, 208, 272, 274, 276, 280, 282, 284, 288, 448
C_IOTA, C_LO16 = 416, 432


class Ev:
    __slots__ = ("sem", "val", "key", "pe")

    def __init__(self, sem, val, key, pe=False):
        self.sem, self.val, self.key, self.pe = sem, val, key, pe


class Tok:
    __slots__ = ("name", "w", "r", "dsem", "dcnt", "dkey")

    def __init__(self, name):
        self.name = name
        self.w = None
        self.r = {}
        self.dsem = None
        self.dcnt = 0
        self.dkey = None


class Eng:
    def __init__(self, ctx, eng, name):
        self.ctx, self.eng, self.name = ctx, eng, name
        self.sem = ctx.nc.alloc_semaphore("s_" + name)
        self.key = "E" + name
        self.n = 0
        self.seen = {}

    def wait(self, ev):
        if ev is None:
            return
        if ev.pe and self.name == "pe":
            return
        if self.seen.get(ev.key, 0) >= ev.val:
            return
        self.eng.wait_ge(ev.sem, ev.val)
        self.seen[ev.key] = ev.val


class Ctx:
    def __init__(self, nc):
        self.nc = nc
        self.pe = Eng(self, nc.tensor, "pe")
        self.act = Eng(self, nc.scalar, "act")
        self.dve = Eng(self, nc.vector, "dve")
        self.pool = Eng(self, nc.gpsimd, "pool")
        self.sp = Eng(self, nc.sync, "sp")
        self.engs = dict(pe=self.pe, act=self.act, dve=self.dve, pool=self.pool, sp=self.sp)
        self.ndsem = 0
        self.dtoks = []

    def tok(self, name):
        return Tok(name)

    def _dsem(self, tok):
        if tok.dsem is None:
            tok.dsem = self.nc.alloc_semaphore("d%d" % self.ndsem)
            tok.dkey = "D%d" % self.ndsem
            self.ndsem += 1
            self.dtoks.append(tok)
        return tok.dsem

    def op(self, en, fn, reads=(), writes=()):
        E = self.engs[en]
        for t in reads:
            E.wait(t.w)
        for t in writes:
            E.wait(t.w)
            for ev in t.r.values():
                E.wait(ev)
        inst = fn(E.eng)
        E.n += 1
        inst.then_inc(E.sem, 1)
        ev = Ev(E.sem, E.n, E.key, pe=(en == "pe"))
        for t in writes:
            t.w = ev
            t.r = {}
        for t in reads:
            t.r[E.key] = ev
        return inst

    def dma(self, out_ap, in_ap, reads=(), writes=(), q="sp", fn=None, **kw):
        E = self.engs[q]
        for t in reads:
            E.wait(t.w)
        for t in writes:
            if not (t.w is not None and t.w.key == t.dkey):
                E.wait(t.w)
            for ev in t.r.values():
                E.wait(ev)
        t0 = writes[0]
        sem = self._dsem(t0)
        if fn is None:
            inst = E.eng.dma_start(out=out_ap, in_=in_ap, **kw)
        else:
            inst = fn(E.eng)
        inst.then_inc(sem, 16)
        t0.dcnt += 16
        ev = Ev(sem, t0.dcnt, t0.dkey)
        for t in writes:
            t.w = ev
            t.r = {}
        for t in reads:
            t.r[t0.dkey] = ev
        return inst

    def barrier(self):
        dve = self.dve
        for t in self.dtoks:
            dve.wait(Ev(t.dsem, t.dcnt, t.dkey))
        others = [self.pe, self.act, self.pool, self.sp]
        for E in others:
            if E.n:
                dve.wait(Ev(E.sem, E.n, E.key))
        if dve.n:
            dve.wait(Ev(dve.sem, dve.n, dve.key))
        inst = dve.eng.memset(self.bar_tile[:, 0:1], 0.0)
        dve.n += 1
        inst.then_inc(dve.sem, 1)
        ev = Ev(dve.sem, dve.n, dve.key)
        for E in others:
            E.wait(ev)


def mkap(ap, dims):
    return bass.AP(tensor=ap.tensor, offset=ap.offset, ap=[list(ap.ap[0])] + [list(d) for d in dims])


class KK:
    pass


_uid = [0]


def _nm(s):
    _uid[0] += 1
    return "%s_%d" % (s, _uid[0])


def SB(K, es, name, shape, dt):
    return es.enter_context(K.nc.sbuf_tensor(_nm(name), list(shape), dt))


def PS(K, es, name, shape=(128, 512), dt=F32):
    return es.enter_context(K.nc.psum_tensor(_nm(name), list(shape), dt))


def build(S, dbg=False, stop=99):
    nc = bass.Bass("TRN2", target_bir_lowering=False)
    K = KK()
    K.nc, K.S = nc, S
    K.c = c = Ctx(nc)
    K.TQ = TQ = min(512, S)
    K.NQ = S // TQ
    K.QB = TQ // 128
    K.NB = S // 128
    K.PADL = TQ - 128
    K.MW = K.PADL + S
    K.dbg = dbg

    def din(name, shape, dt=F32):
        return nc.dram_tensor(name, list(shape), dt, kind="ExternalInput").ap()

    def dint(name, shape, dt=F32):
        return nc.dram_tensor(name, list(shape), dt, kind="Internal").ap()

    K.d_xT = din("xT", [D, S])
    K.d_cst = din("cst", [128, NCST])
    K.d_wada = din("wada", [2, 24, 128, 16, 512])
    K.d_win = din("win", [48, 128, 16, 128])
    K.d_wff = din("wff", [128, 16, 8])
    K.d_wout0 = din("wout0", [16, 128, 16, 128])
    K.d_wqkv = din("wqkv", [48, 128, 16, 128])
    K.d_wout1 = din("wout1", [16, 128, 16, 128])
    K.d_wquery = din("wquery", [2, 16, 128, 16, 128])
    K.d_keys = din("keysT", [2, 128, 2, 128])
    K.d_uv = [din("uv0", [64, 128, 4 * D]), din("uv1", [64, 128, 4 * D])]
    _uvb = dint("uvb", [NEXP, 2 * D], BF16)
    K.d_uvb = [_uvb, _uvb]
    _tuvb = c.tok("uvb")
    K.t_uvb = [_tuvb, _tuvb]
    K.d_biasT = din("biasT", [16, 128, K.MW])
    K.d_multT = din("multT", [128, K.MW])
    K.d_triT = din("triT", [128, K.PADL + TQ])
    K.d_y = nc.dram_tensor("y", [D, S], F32, kind="ExternalOutput").ap()
    dscr = (lambda n, sh: nc.dram_tensor(n, list(sh), F32, kind="ExternalOutput").ap()) if dbg else dint
    K.d_xA = dscr("xA", [D, S])
    K.d_xB = dscr("xB", [D, S])
    K.d_xC = dscr("xC", [D, S])
    K.d_mixed = dint("mixed", [D, S], BF16)
    K.d_fsc = dint("fsc", [4, 8, S], BF16)
    K.d_h2 = dint("h2tok", [S, D], BF16)
    K.t_xA, K.t_xB, K.t_xC, K.t_y = c.tok("xA"), c.tok("xB"), c.tok("xC"), c.tok("y")
    K.t_mixed, K.t_fsc, K.t_h2 = c.tok("mixed"), c.tok("fsc"), c.tok("h2")

    with ExitStack() as pers:
        prologue(K, pers)
        convert_tables(K, 0)
        mixer(K, 0, K.d_xT, None, K.d_xA, K.t_xA)
        if stop >= 3:
            peer(K, 0, K.d_xA, K.t_xA, K.d_xB, K.t_xB)
        if stop >= 4:
            convert_tables(K, 1)
            mixer(K, 1, K.d_xB, K.t_xB, K.d_xC, K.t_xC)
        if stop >= 5:
            peer(K, 1, K.d_xC, K.t_xC, K.d_y, K.t_y)
        c.barrier()
        for t in c.dtoks:
            c.sp.wait(Ev(t.dsem, t.dcnt, t.dkey))
    return nc


def prologue(K, pers):
    c, nc = K.c, K.nc
    c.bar_tile = SB(K, pers, "bar", [128, 2], F32)
    K.bc_reg = nc.gpsimd.to_reg(NEXP - 1)
    K.cst = SB(K, pers, "cst", [128, NCST], F32)
    K.t_cst = c.tok("cst")
    c.dma(K.cst[:], K.d_cst[:, :], writes=[K.t_cst])
    K.ident_f = K.cst[:, C_ID:C_ID + 128]
    K.ones_bf = SB(K, pers, "ones", [128, 128], BF16)
    K.t_ones = c.tok("ones")
    c.op("dve", lambda e: e.memset(K.ones_bf[:], 1.0), writes=[K.t_ones])
    K.ident_bf = SB(K, pers, "identb", [128, 128], BF16)
    K.t_identb = c.tok("identb")
    c.op("dve", lambda e: e.tensor_copy(out=K.ident_bf[:], in_=K.ident_f), reads=[K.t_cst], writes=[K.t_identb])
    K.mod = SB(K, pers, "mod", [128, 192], F32)
    K.t_mod = c.tok("mod")
    K.amod = SB(K, pers, "amod", [128, 64], F32)
    K.misc = SB(K, pers, "misc", [128, 16], F32)
    K.t_misc = c.tok("misc")
    K.wf = [SB(K, pers, "wf", [128, 16, 128], F32) for _ in range(2)]
    K.t_wf = [c.tok("wf0"), c.tok("wf1")]
    K.wb = [SB(K, pers, "wb", [128, 16, 256], BF16) for _ in range(2)]
    K.t_wb = [c.tok("wb0"), c.tok("wb1")]
    K.wcnt = 0

    with ExitStack() as es:
        cond = SB(K, es, "cond", [128, 16], F32)
        t_cond = c.tok("cond")
        c.op("act", lambda e: e.activation(out=cond[:], in_=K.cst[:, C_C:C_C + 16], func=AF.Silu),
             reads=[K.t_cst], writes=[t_cond])
        wad = [SB(K, es, "wad", [128, 16, 512], F32) for _ in range(2)]
        t_wad = [c.tok("wad0"), c.tok("wad1")]
        modps = PS(K, es, "modps")
        t_modps = c.tok("modps")
        condb = SB(K, es, "condb", [128, 16], BF16)
        t_condb = c.tok("condb")
        c.op("dve", lambda e: e.tensor_copy(out=condb[:], in_=cond[:]), reads=[t_cond], writes=[t_condb])
        wadb = [SB(K, es, "wadb", [128, 16, 512], BF16) for _ in range(2)]
        t_wadb = [c.tok("wadb0"), c.tok("wadb1")]
        for i in range(2):
            for g in range(24):
                s = (i * 24 + g) % 2
                c.dma(wad[s][:], K.d_wada[i, g], writes=[t_wad[s]])
                c.op("dve", lambda e, s=s: e.tensor_copy(out=wadb[s][:, 0:6, :], in_=wad[s][:, 0:6, :]),
                     reads=[t_wad[s]], writes=[t_wadb[s]])
                c.op("act", lambda e, s=s: e.activation(out=wadb[s][:, 6:11, :], in_=wad[s][:, 6:11, :], func=AF.Copy),
                     reads=[t_wad[s]], writes=[t_wadb[s]])
                c.op("pool", lambda e, s=s: e.tensor_copy(out=wadb[s][:, 11:16, :], in_=wad[s][:, 11:16, :]),
                     reads=[t_wad[s]], writes=[t_wadb[s]])
                for j in range(4):
                    nb = g * 4 + j
                    col = i * 96 + nb
                    for cc in range(NCH):
                        c.op("pe", lambda e, s=s, j=j, cc=cc, col=col: e.matmul(
                            modps[:, col:col + 1], lhsT=wadb[s][:, cc, j * 128:(j + 1) * 128],
                            rhs=condb[:, cc:cc + 1], start=(cc == 0), stop=(cc == NCH - 1)),
                            reads=[t_wadb[s], t_condb], writes=[t_modps])
        c.op("dve", lambda e: e.tensor_tensor(out=K.mod[:], in0=modps[:, 0:192], in1=K.cst[:, C_BADA:C_BADA + 192],
                                              op=ALU.add), reads=[t_modps, K.t_cst], writes=[K.t_mod])
        for i in range(2):
            for s in range(2):
                sc = K.mod[:, i * 96 + (1 if s == 0 else 4) * 16: i * 96 + (1 if s == 0 else 4) * 16 + 16]
                gn = K.cst[:, C_GAIN + (i * 2 + s) * 16: C_GAIN + (i * 2 + s) * 16 + 16]
                o = K.amod[:, (i * 2 + s) * 16:(i * 2 + s) * 16 + 16]
                c.op("dve", lambda e, sc=sc, gn=gn, o=o: e.scalar_tensor_tensor(
                    out=o, in0=sc, scalar=1.0, in1=gn, op0=ALU.add, op1=ALU.mult),
                    reads=[K.t_mod, K.t_cst], writes=[K.t_mod])
        M = K.misc
        c.op("dve", lambda e: e.tensor_scalar(out=M[:, 0:1], in0=K.cst[:, C_FOXG:C_FOXG + 1], scalar1=SCALE, scalar2=None,
                                              op0=ALU.mult), reads=[K.t_cst], writes=[K.t_misc])
        c.op("dve", lambda e: e.tensor_scalar(out=M[:, 1:2], in0=K.cst[:, C_DIFFG:C_DIFFG + 1], scalar1=SCALE, scalar2=None,
                                              op0=ALU.mult), reads=[K.t_cst], writes=[K.t_misc])
        c.op("dve", lambda e: e.tensor_scalar(out=M[:, 2:3], in0=K.cst[:, C_ODDG:C_ODDG + 1], scalar1=SCALE, scalar2=None,
                                              op0=ALU.mult), reads=[K.t_cst], writes=[K.t_misc])
        c.op("dve", lambda e: e.tensor_scalar(out=M[:, 3:4], in0=K.cst[:, C_BF:C_BF + 1], scalar1=-1.0, scalar2=None,
                                              op0=ALU.mult), reads=[K.t_cst], writes=[K.t_misc])
        lam_init = 0.8 - 0.6 * math.exp(-0.3 * 0)
        K.lam_init = lam_init
        lp = SB(K, es, "lp", [128, 2], F32)
        t_lp = c.tok("lp")
        c.op("dve", lambda e: e.tensor_tensor(out=lp[:, 0:1], in0=K.cst[:, C_LAM:C_LAM + 1], in1=K.cst[:, C_LAM + 1:C_LAM + 2],
                                              op=ALU.mult), reads=[K.t_cst], writes=[t_lp])
        c.op("dve", lambda e: e.tensor_tensor(out=lp[:, 1:2], in0=K.cst[:, C_LAM + 2:C_LAM + 3], in1=K.cst[:, C_LAM + 3:C_LAM + 4],
                                              op=ALU.mult), reads=[K.t_cst], writes=[t_lp])
        onesf = SB(K, es, "onesf", [128, 128], F32)
        t_onesf = c.tok("onesf")
        c.op("dve", lambda e: e.memset(onesf[:], 1.0), writes=[t_onesf])
        c.op("pe", lambda e: e.matmul(modps[:, 200:202], lhsT=onesf[:], rhs=lp[:, 0:2], start=True, stop=True),
             reads=[t_onesf, t_lp], writes=[t_modps])
        le = SB(K, es, "le", [128, 2], F32)
        t_le = c.tok("le")
        c.op("act", lambda e: e.activation(out=le[:], in_=modps[:, 200:202], func=AF.Exp), reads=[t_modps], writes=[t_le])
        c.op("dve", lambda e: e.scalar_tensor_tensor(out=M[:, 4:5], in0=le[:, 1:2], scalar=-lam_init, in1=le[:, 0:1],
                                                     op0=ALU.add, op1=ALU.subtract), reads=[t_le], writes=[K.t_misc])
        c.op("dve", lambda e: e.tensor_scalar(out=M[:, 5:7], in0=K.cst[:, C_SUBLN:C_SUBLN + 2], scalar1=(1.0 - lam_init),
                                              scalar2=None, op0=ALU.mult), reads=[K.t_cst], writes=[K.t_misc])
        c.barrier()


def convert_tables(K, layer):
    c = K.c
    with ExitStack() as es:
        NSL = 3
        W8 = 4 * D
        fin = [SB(K, es, "cvi", [128, W8], F32) for _ in range(NSL)]
        t_fin = [c.tok("cvi%d" % i) for i in range(NSL)]
        fout = [SB(K, es, "cvo", [128, W8], BF16) for _ in range(NSL)]
        t_fout = [c.tok("cvo%d" % i) for i in range(NSL)]
        cnt = 0
        if True:
            for blk in range(64):
                s = cnt % NSL
                cnt += 1
                c.dma(fin[s][:], K.d_uv[layer][blk], writes=[t_fin[s]])
                c.op("dve", lambda e, s=s: e.tensor_copy(out=fout[s][:, 0:3072], in_=fin[s][:, 0:3072]),
                     reads=[t_fin[s]], writes=[t_fout[s]])
                c.op("act", lambda e, s=s: e.activation(out=fout[s][:, 3072:6144], in_=fin[s][:, 3072:6144], func=AF.Copy),
                     reads=[t_fin[s]], writes=[t_fout[s]])
                c.op("pool", lambda e, s=s: e.tensor_copy(out=fout[s][:, 6144:W8], in_=fin[s][:, 6144:W8]),
                     reads=[t_fin[s]], writes=[t_fout[s]])
                base = K.d_uvb[layer]
                dst = bass.AP(tensor=base.tensor, offset=base.offset + blk * 128 * W8, ap=[[W8, 128], [1, W8]])
                c.dma(dst, fout[s][:], reads=[t_fout[s]], writes=[K.t_uvb[layer]])
        c.barrier()

def load_unit(K, src_ap, dv_off=0):
    c = K.c
    s = K.wcnt % 2
    c.dma(K.wf[s][:], src_ap, writes=[K.t_wf[s]])
    return s


def get_weights(K, srcs):
    c = K.c
    ws = K.wcnt % 2
    wb, t_wb = K.wb[ws], K.t_wb[ws]
    for j, src in enumerate(srcs):
        fs = (K.wcnt * 2 + j) % 2
        c.dma(K.wf[fs][:], src, writes=[K.t_wf[fs]])
        eng = "dve" if (j % 2 == 0) else "pool"
        c.op(eng, lambda e, fs=fs, j=j: e.tensor_copy(out=wb[:, :, j * 128:(j + 1) * 128], in_=K.wf[fs][:]),
             reads=[K.t_wf[fs]], writes=[t_wb])
    K.wcnt += 1
    return wb, t_wb


def norm_mod(K, xsrc, t_xsrc, i, s, hT, t_hT):
    c = K.c
    S, TQ, NQ = K.S, K.TQ, K.NQ
    xv = xsrc.rearrange("(c p) t -> p c t", p=128)
    acol = (i * 2 + s) * 16
    shcol = i * 96 + (0 if s == 0 else 3) * 16
    with ExitStack() as es:
        xb = SB(K, es, "xb", [128, 16, TQ], F32)
        t_xb = c.tok("xb")
        sq = [SB(K, es, "sq", [128, TQ], BF16) for _ in range(2)]
        t_sq = [c.tok("sq0"), c.tok("sq1")]
        rs = SB(K, es, "rs", [128, TQ], F32)
        t_rs = c.tok("rs")
        rinv = SB(K, es, "rinv", [128, TQ], F32)
        t_rinv = c.tok("rinv")
        tmp = [SB(K, es, "tmp", [128, TQ], F32) for _ in range(2)]
        t_tmp = [c.tok("tmp0"), c.tok("tmp1")]
        ssq = PS(K, es, "ssq")
        t_ssq = c.tok("ssq")
        for tq in range(NQ):
            sl = slice(tq * TQ, (tq + 1) * TQ)
            c.dma(xb[:], xv[:, :, sl], reads=([t_xsrc] if t_xsrc is not None else []), writes=[t_xb])
            for cc in range(NCH):
                k = cc % 2
                c.op("act", lambda e, cc=cc, k=k: e.activation(out=sq[k][:], in_=xb[:, cc, :], func=AF.Square),
                     reads=[t_xb], writes=[t_sq[k]])
                c.op("pe", lambda e, cc=cc, k=k: e.matmul(ssq[:, 0:TQ], lhsT=K.ones_bf[:], rhs=sq[k][:],
                                                         start=(cc == 0), stop=(cc == NCH - 1)),
                     reads=[t_sq[k], K.t_ones], writes=[t_ssq])
            c.op("act", lambda e: e.activation(out=rs[:], in_=ssq[:, 0:TQ], func=AF.Sqrt, scale=1.0 / D, bias=EPS),
                 reads=[t_ssq], writes=[t_rs])
            c.op("dve", lambda e: e.reciprocal(out=rinv[:], in_=rs[:]), reads=[t_rs], writes=[t_rinv])
            for cc in range(NCH):
                k = cc % 2
                c.op("dve", lambda e, cc=cc, k=k: e.scalar_tensor_tensor(
                    out=tmp[k][:], in0=xb[:, cc, :], scalar=K.amod[:, acol + cc:acol + cc + 1], in1=rinv[:],
                    op0=ALU.mult, op1=ALU.mult), reads=[t_xb, t_rinv, K.t_mod], writes=[t_tmp[k]])
                c.op("act", lambda e, cc=cc, k=k: e.activation(
                    out=hT[:, cc, sl], in_=tmp[k][:], func=AF.Identity, bias=K.mod[:, shcol + cc:shcol + cc + 1], scale=1.0),
                    reads=[t_tmp[k], K.t_mod], writes=[t_hT])
        c.barrier()


def proj_fm(K, wb, t_wb, woff, hT, t_hT, pj, t_pj, post):
    c = K.c
    for tq in range(K.NQ):
        k = K.pjcnt % 2
        K.pjcnt += 1
        sl = slice(tq * K.TQ, (tq + 1) * K.TQ)
        for cc in range(NCH):
            c.op("pe", lambda e, cc=cc, k=k: e.matmul(pj[k][:, 0:K.TQ], lhsT=wb[:, cc, woff:woff + 128], rhs=hT[:, cc, sl],
                                                     start=(cc == 0), stop=(cc == NCH - 1)),
                 reads=[t_wb, t_hT], writes=[t_pj[k]])
        post(pj[k], t_pj[k], tq, sl)


def qknorm_post(K, W, gain_ap, dst, t_dst):
    c = K.c
    TQ = K.TQ

    def post(ps, t_ps, tq, sl):
        c.op("act", lambda e: e.activation(out=W.sq[:], in_=ps[:, 0:TQ], func=AF.Square), reads=[t_ps], writes=[W.t_sq])
        c.op("pe", lambda e: e.matmul(W.ss[:, 0:TQ], lhsT=K.ones_bf[:], rhs=W.sq[:], start=True, stop=True),
             reads=[W.t_sq, K.t_ones], writes=[W.t_ss])
        c.op("act", lambda e: e.activation(out=W.rs[:], in_=W.ss[:, 0:TQ], func=AF.Sqrt, scale=1.0 / 128, bias=EPS),
             reads=[W.t_ss], writes=[W.t_rs])
        c.op("dve", lambda e: e.reciprocal(out=W.rinv[:], in_=W.rs[:]), reads=[W.t_rs], writes=[W.t_rinv])
        c.op("dve", lambda e: e.scalar_tensor_tensor(out=dst[:, sl], in0=ps[:, 0:TQ], scalar=gain_ap, in1=W.rinv[:],
                                                     op0=ALU.mult, op1=ALU.mult),
             reads=[t_ps, W.t_rinv, K.t_misc, K.t_cst], writes=[t_dst])
    return post


def proj_tm(K, wb, t_wb, dv, hT, t_hT, pj, t_pj, V, t_V):
    c = K.c
    per = 512 // dv
    for tb in range(K.NB):
        k = K.pjcnt % 2
        if tb % per == 0:
            K.pjcnt += 1
            k = (K.pjcnt - 1) % 2
            cur = (pj[k], t_pj[k])
        o = (tb % per) * dv
        for cc in range(NCH):
            c.op("pe", lambda e, cc=cc, cur=cur, o=o: e.matmul(cur[0][:, o:o + dv], lhsT=hT[:, cc, tb * 128:(tb + 1) * 128],
                                                               rhs=wb[:, cc, 0:dv], start=(cc == 0), stop=(cc == NCH - 1)),
                 reads=[t_wb, t_hT], writes=[cur[1]])
        if tb % per == per - 1 or tb == K.NB - 1:
            n = (tb % per) + 1
            tb0 = tb - (tb % per)
            c.op("act", lambda e, cur=cur, n=n, tb0=tb0: e.activation(
                out=V[:, tb0:tb0 + n, 0:dv], in_=cur[0][:, 0:n * dv].rearrange("p (a b) -> p a b", b=dv), func=AF.Copy),
                reads=[cur[1]], writes=[t_V])


def attn_core(K, W, qT, t_q, kT, t_k, V, t_V, ndv, mask, t_mask, fin, AB=None):
    c = K.c
    TQ, NQ, QB, PADL = K.TQ, K.NQ, K.QB, K.PADL
    for qc in range(NQ):
        sl = slice(qc * TQ, (qc + 1) * TQ)
        nkb = (qc + 1) * QB
        for kb in range(nkb):
            it = W.it
            W.it += 1
            st, t_st = W.st[it % 2], W.t_st[it % 2]
            ks = slice(kb * 128, (kb + 1) * 128)
            c.op("pe", lambda e, st=st: e.matmul(st[:, 0:TQ], lhsT=kT[:, ks], rhs=qT[:, sl], start=True, stop=(AB is None)),
                 reads=[t_k, t_q], writes=[t_st])
            diag = kb >= qc * QB
            J0 = qc * TQ - kb * 128 + PADL
            if AB is not None:
                A, B, t_AB = AB
                c.op("pe", lambda e, st=st: e.matmul(st[:, 0:TQ], lhsT=A[:, ks], rhs=B[:, sl], start=False, stop=(not diag)),
                     reads=[t_AB], writes=[t_st])
                if diag:
                    c.op("pe", lambda e, st=st: e.matmul(st[:, 0:TQ], lhsT=K.ident_bf[:], rhs=K.tri[:, J0:J0 + TQ],
                                                         start=False, stop=True),
                         reads=[K.t_identb, K.t_tri], writes=[t_st])
            ptb, t_ptb = W.ptb[it % 2], W.t_ptb[it % 2]
            if mask is not None:
                ptf, t_ptf = W.ptf[it % 2], W.t_ptf[it % 2]
                c.op("act", lambda e, st=st, ptf=ptf: e.activation(out=ptf[:], in_=st[:, 0:TQ], func=AF.Exp),
                     reads=[t_st], writes=[t_ptf])
                mk, t_mk = mask[:, J0:J0 + TQ], t_mask
                c.op("dve", lambda e, ptf=ptf, ptb=ptb, mk=mk: e.tensor_tensor(out=ptb[:], in0=ptf[:], in1=mk, op=ALU.mult),
                     reads=[t_ptf, t_mk], writes=[t_ptb])
            else:
                c.op("act", lambda e, st=st, ptb=ptb: e.activation(out=ptb[:], in_=st[:, 0:TQ], func=AF.Exp),
                     reads=[t_st], writes=[t_ptb])
            for j in range(ndv):
                c.op("pe", lambda e, j=j, ptb=ptb: e.matmul(W.o[j][:, 0:TQ], lhsT=V[:, kb, j * 128:(j + 1) * 128], rhs=ptb[:],
                                                            start=(kb == 0), stop=(kb == nkb - 1)),
                     reads=[t_V, t_ptb], writes=[W.t_o[j]])
            c.op("pe", lambda e, ptb=ptb: e.matmul(W.den[:, 0:TQ], lhsT=K.ones_bf[:], rhs=ptb[:],
                                                   start=(kb == 0), stop=(kb == nkb - 1)),
                 reads=[K.t_ones, t_ptb], writes=[W.t_den])
        c.op("dve", lambda e: e.reciprocal(out=W.rden[:], in_=W.den[:, 0:TQ]), reads=[W.t_den], writes=[W.t_rden])
        fin(qc, sl)


class WS:
    pass


def attn_ws(K, es):
    c = K.c
    TQ, S = K.TQ, K.S
    W = WS()
    W.it = 0
    W.pj = [PS(K, es, "pj") for _ in range(2)]
    W.t_pj = [c.tok("pj0"), c.tok("pj1")]
    W.ss = PS(K, es, "ss")
    W.t_ss = c.tok("ss")
    W.st = [PS(K, es, "st") for _ in range(2)]
    W.t_st = [c.tok("st0"), c.tok("st1")]
    W.o = [PS(K, es, "o") for _ in range(2)]
    W.t_o = [c.tok("o0"), c.tok("o1")]
    W.den = PS(K, es, "den")
    W.t_den = c.tok("den")
    W.sq = SB(K, es, "sqn", [128, TQ], BF16)
    W.t_sq = c.tok("sqn")
    W.rs = SB(K, es, "rsn", [128, TQ], F32)
    W.t_rs = c.tok("rsn")
    W.rinv = SB(K, es, "rinvn", [128, TQ], F32)
    W.t_rinv = c.tok("rinvn")
    W.ptf = [SB(K, es, "ptf", [128, TQ], F32) for _ in range(2)]
    W.t_ptf = [c.tok("ptf0"), c.tok("ptf1")]
    W.ptb = [SB(K, es, "ptb", [128, TQ], BF16) for _ in range(2)]
    W.t_ptb = [c.tok("ptb0"), c.tok("ptb1")]
    W.rden = SB(K, es, "rden", [128, TQ], F32)
    W.t_rden = c.tok("rden")
    W.qT = SB(K, es, "qT", [128, S], BF16)
    W.t_qT = c.tok("qT")
    W.kT = SB(K, es, "kT", [128, S], BF16)
    W.t_kT = c.tok("kT")
    W.V = SB(K, es, "V", [128, K.NB, 256], BF16)
    W.t_V = c.tok("V")
    W.ob = SB(K, es, "ob", [128, 2, S], BF16)
    W.t_ob = c.tok("ob")
    W.G = SB(K, es, "G", [128, K.MW], F32)
    W.t_G = c.tok("G")
    return W


def mixer(K, layer, xin, t_xin, xout, t_xout):
    c = K.c
    S, TQ, NQ, NB = K.S, K.TQ, K.NQ, K.NB
    K.pjcnt = 0
    with ExitStack() as es:
        hT = SB(K, es, "hT", [128, NCH, S], BF16)
        t_hT = c.tok("hT")
        norm_mod(K, xin, t_xin, layer, 0, hT, t_hT)
        W = attn_ws(K, es)
        trif = SB(K, es, "trif", [128, K.PADL + TQ], F32)
        t_trif = c.tok("trif")
        c.dma(trif[:], K.d_triT[:, :], writes=[t_trif])
        K.tri = SB(K, es, "tri", [128, K.PADL + TQ], BF16)
        K.t_tri = c.tok("tri")
        c.op("dve", lambda e: e.tensor_copy(out=K.tri[:], in_=trif[:]), reads=[t_trif], writes=[K.t_tri])
        if layer == 0:
            mixer0_heads(K, es, W, hT, t_hT)
        else:
            mixer1_heads(K, es, W, hT, t_hT)
        c.barrier()
    out_proj(K, K.d_wout0 if layer == 0 else K.d_wout1, xin, t_xin, xout, t_xout, layer * 96 + 2 * 16)


def store_mixed(K, W, j, row0):
    K.c.dma(K.d_mixed[row0:row0 + 128, :], W.ob[:, j, :], reads=[W.t_ob], writes=[K.t_mixed])


def mixer0_heads(K, es, W, hT, t_hT):
    c = K.c
    S, TQ, NQ, NB = K.S, K.TQ, K.NQ, K.NB
    M = K.misc
    with ExitStack() as e2:
        wffb = SB(K, e2, "wffb", [128, 16, 8], BF16)
        wfff = SB(K, e2, "wfff", [128, 16, 8], F32)
        t_wff = c.tok("wff")
        c.dma(wfff[:], K.d_wff[:, :, :], writes=[t_wff])
        t_wffb = c.tok("wffb")
        c.op("dve", lambda e: e.tensor_copy(out=wffb[:], in_=wfff[:]), reads=[t_wff], writes=[t_wffb])
        ef = SB(K, e2, "ef", [8, S], F32)
        t_ef = c.tok("ef")
        lf = SB(K, e2, "lf", [8, S], F32)
        t_lf = c.tok("lf")
        Cf = SB(K, e2, "Cf", [8, S], F32)
        t_Cf = c.tok("Cf")
        on8 = SB(K, e2, "on8", [8, S], F32)
        t_on8 = c.tok("on8")
        c.op("pool", lambda e: e.memset(on8[:], 1.0), writes=[t_on8])
        for tq in range(NQ):
            sl = slice(tq * TQ, (tq + 1) * TQ)
            for cc in range(NCH):
                c.op("pe", lambda e, cc=cc: e.matmul(W.ss[0:8, 0:TQ], lhsT=wffb[:, cc, :], rhs=hT[:, cc, sl],
                                                     start=(cc == 0), stop=(cc == NCH - 1)),
                     reads=[t_wffb, t_hT], writes=[W.t_ss])
            c.op("act", lambda e: e.activation(out=ef[:, sl], in_=W.ss[0:8, 0:TQ], func=AF.Exp, scale=-1.0, bias=M[0:8, 3:4]),
                 reads=[W.t_ss, K.t_misc], writes=[t_ef])
        c.op("act", lambda e: e.activation(out=lf[:], in_=ef[:], func=AF.Ln, scale=1.0, bias=1.0), reads=[t_ef], writes=[t_lf])
        c.op("dve", lambda e: e.tensor_tensor_scan(out=Cf[:], data0=on8[:], data1=lf[:], initial=0.0, op0=ALU.mult, op1=ALU.add),
             reads=[t_on8, t_lf], writes=[t_Cf])
        hl = SB(K, e2, "hl", [8, 4, S], BF16)
        t_hl = c.tok("hl")
        c.op("dve", lambda e: e.tensor_copy(out=hl[:, 0, :], in_=Cf[:]), reads=[t_Cf], writes=[t_hl])
        c.op("dve", lambda e: e.tensor_tensor(out=lf[:], in0=Cf[:], in1=hl[:, 0, :], op=ALU.subtract),
             reads=[t_Cf, t_hl], writes=[t_lf])
        c.op("dve", lambda e: e.tensor_copy(out=hl[:, 1, :], in_=lf[:]), reads=[t_lf], writes=[t_hl])
        c.op("dve", lambda e: e.tensor_scalar(out=hl[:, 2:4, :], in0=hl[:, 0:2, :], scalar1=-1.0, scalar2=None, op0=ALU.mult),
             reads=[t_hl], writes=[t_hl])
        for r in range(4):
            c.dma(K.d_fsc[r], hl[:, r, :], reads=[t_hl], writes=[K.t_fsc])
        c.barrier()
    A = SB(K, es, "Afox", [128, S], BF16)
    B = SB(K, es, "Bfox", [128, S], BF16)
    t_AB = c.tok("AB")
    c.op("pool", lambda e: e.memset(A[:], 0.0), writes=[t_AB])
    c.op("pool", lambda e: e.memset(B[:], 0.0), writes=[t_AB])
    c.op("pool", lambda e: e.memset(A[0:4, :], 1.0), writes=[t_AB])
    c.op("pool", lambda e: e.memset(B[0:4, :], 1.0), writes=[t_AB])

    for hh in range(8):
        wq, t_wq = get_weights(K, [K.d_win[hh]])
        proj_fm(K, wq, t_wq, 0, hT, t_hT, W.pj, W.t_pj, qknorm_post(K, W, M[:, 0:1], W.qT, W.t_qT))
        wk, t_wk = get_weights(K, [K.d_win[8 + hh]])
        proj_fm(K, wk, t_wk, 0, hT, t_hT, W.pj, W.t_pj, qknorm_post(K, W, K.cst[:, C_FOXG + 1:C_FOXG + 2], W.kT, W.t_kT))
        wv, t_wv = get_weights(K, [K.d_win[16 + hh]])
        proj_tm(K, wv, t_wv, 128, hT, t_hT, W.pj, W.t_pj, W.V, W.t_V)
        c.dma(A[0:2, :], K.d_fsc[0:2, hh, :], reads=[K.t_fsc], writes=[t_AB])
        c.dma(B[2:4, :], K.d_fsc[2:4, hh, :], reads=[K.t_fsc], writes=[t_AB])

        def fin(qc, sl):
            c.op("dve", lambda e: e.tensor_tensor(out=W.ob[:, 0, sl], in0=W.o[0][:, 0:TQ], in1=W.rden[:], op=ALU.mult),
                 reads=[W.t_o[0], W.t_rden], writes=[W.t_ob])
        attn_core(K, W, W.qT, W.t_qT, W.kT, W.t_kT, W.V, W.t_V, 1, None, None, fin, AB=(A, B, t_AB))
        store_mixed(K, W, 0, hh * 128)

    O = [SB(K, es, "Od", [128, 2, S], F32) for _ in range(2)]
    t_O = [c.tok("Od0"), c.tok("Od1")]
    for hh in range(4):
        c.dma(W.G[:], K.d_biasT[hh], writes=[W.t_G])
        c.op("act", lambda e: e.activation(out=W.G[:], in_=W.G[:], func=AF.Exp), reads=[W.t_G], writes=[W.t_G])
        wv, t_wv = get_weights(K, [K.d_win[40 + 2 * hh], K.d_win[41 + 2 * hh]])
        proj_tm(K, wv, t_wv, 256, hT, t_hT, W.pj, W.t_pj, W.V, W.t_V)
        for m in range(2):
            wq, t_wq = get_weights(K, [K.d_win[24 + 2 * hh + m]])
            proj_fm(K, wq, t_wq, 0, hT, t_hT, W.pj, W.t_pj, qknorm_post(K, W, M[:, 1:2], W.qT, W.t_qT))
            wk, t_wk = get_weights(K, [K.d_win[32 + 2 * hh + m]])
            proj_fm(K, wk, t_wk, 0, hT, t_hT, W.pj, W.t_pj,
                    qknorm_post(K, W, K.cst[:, C_DIFFG + 1:C_DIFFG + 2], W.kT, W.t_kT))

            def fin(qc, sl, m=m):
                for j in range(2):
                    c.op("dve", lambda e, j=j: e.tensor_tensor(out=O[m][:, j, sl], in0=W.o[j][:, 0:TQ], in1=W.rden[:], op=ALU.mult),
                         reads=[W.t_o[j], W.t_rden], writes=[t_O[m]])
            attn_core(K, W, W.qT, W.t_qT, W.kT, W.t_kT, W.V, W.t_V, 2, W.G, W.t_G, fin)
        for tq in range(NQ):
            sl = slice(tq * TQ, (tq + 1) * TQ)
            for j in range(2):
                c.op("dve", lambda e, j=j: e.scalar_tensor_tensor(out=O[0][:, j, sl], in0=O[1][:, j, sl], scalar=M[:, 4:5],
                                                                  in1=O[0][:, j, sl], op0=ALU.mult, op1=ALU.add),
                     reads=[t_O[1], t_O[0], K.t_misc], writes=[t_O[0]])
                c.op("act", lambda e, j=j: e.activation(out=W.sq[:], in_=O[0][:, j, sl], func=AF.Square),
                     reads=[t_O[0]], writes=[W.t_sq])
                c.op("pe", lambda e, j=j: e.matmul(W.ss[:, 0:TQ], lhsT=K.ones_bf[:], rhs=W.sq[:], start=(j == 0), stop=(j == 1)),
                     reads=[W.t_sq, K.t_ones], writes=[W.t_ss])
            c.op("act", lambda e: e.activation(out=W.rs[:], in_=W.ss[:, 0:TQ], func=AF.Sqrt, scale=1.0 / 256, bias=EPS),
                 reads=[W.t_ss], writes=[W.t_rs])
            c.op("dve", lambda e: e.reciprocal(out=W.rinv[:], in_=W.rs[:]), reads=[W.t_rs], writes=[W.t_rinv])
            for j in range(2):
                c.op("dve", lambda e, j=j: e.scalar_tensor_tensor(out=W.ob[:, j, sl], in0=O[0][:, j, sl], scalar=M[:, 5 + j:6 + j],
                                                                  in1=W.rinv[:], op0=ALU.mult, op1=ALU.mult),
                     reads=[t_O[0], W.t_rinv, K.t_misc], writes=[W.t_ob])
        for j in range(2):
            store_mixed(K, W, j, 1024 + hh * 256 + j * 128)


def mixer1_heads(K, es, W, hT, t_hT):
    c = K.c
    S, TQ, NQ, NB = K.S, K.TQ, K.NQ, K.NB
    M = K.misc
    mult = SB(K, es, "mult", [128, K.MW], F32)
    t_mult = c.tok("mult")
    c.dma(mult[:], K.d_multT[:, :], writes=[t_mult])
    for hh in range(16):
        c.dma(W.G[:], K.d_biasT[hh], writes=[W.t_G])
        c.op("act", lambda e: e.activation(out=W.G[:], in_=W.G[:], func=AF.Exp), reads=[W.t_G], writes=[W.t_G])
        c.op("pool", lambda e: e.tensor_tensor(out=W.G[:], in0=W.G[:], in1=mult[:], op=ALU.mult),
             reads=[W.t_G, t_mult], writes=[W.t_G])
        wq, t_wq = get_weights(K, [K.d_wqkv[hh]])
        proj_fm(K, wq, t_wq, 0, hT, t_hT, W.pj, W.t_pj, qknorm_post(K, W, M[:, 2:3], W.qT, W.t_qT))
        wk, t_wk = get_weights(K, [K.d_wqkv[16 + hh]])
        proj_fm(K, wk, t_wk, 0, hT, t_hT, W.pj, W.t_pj, qknorm_post(K, W, K.cst[:, C_ODDG + 1:C_ODDG + 2], W.kT, W.t_kT))
        wv, t_wv = get_weights(K, [K.d_wqkv[32 + hh]])
        proj_tm(K, wv, t_wv, 128, hT, t_hT, W.pj, W.t_pj, W.V, W.t_V)

        def fin(qc, sl):
            c.op("dve", lambda e: e.tensor_tensor(out=W.ob[:, 0, sl], in0=W.o[0][:, 0:TQ], in1=W.rden[:], op=ALU.mult),
                 reads=[W.t_o[0], W.t_rden], writes=[W.t_ob])
        attn_core(K, W, W.qT, W.t_qT, W.kT, W.t_kT, W.V, W.t_V, 1, W.G, W.t_G, fin)
        store_mixed(K, W, 0, hh * 128)


def out_proj(K, wsrc, xin, t_xin, xout, t_xout, gcol):
    c = K.c
    S, TQ, NQ = K.S, K.TQ, K.NQ
    K.pjcnt = 0
    with ExitStack() as es:
        mT = SB(K, es, "mT", [128, NCH, S], BF16)
        t_mT = c.tok("mT")
        mv = K.d_mixed.rearrange("(c p) t -> p c t", p=128)
        for cc in range(NCH):
            c.dma(mT[:, cc, :], mv[:, cc, :], reads=[K.t_mixed], writes=[t_mT])
        pj = [PS(K, es, "pjo") for _ in range(2)]
        t_pj = [c.tok("pjo0"), c.tok("pjo1")]
        xc = [SB(K, es, "xc", [128, TQ], F32) for _ in range(2)]
        t_xc = [c.tok("xc0"), c.tok("xc1")]
        xo = [SB(K, es, "xo", [128, TQ], F32) for _ in range(2)]
        t_xo = [c.tok("xo0"), c.tok("xo1")]
        it = 0
        for nb in range(16):
            wb, t_wb = get_weights(K, [wsrc[nb]])
            rows = slice(nb * 128, (nb + 1) * 128)

            def post(ps, t_ps, tq, sl, nb=nb, rows=rows):
                nonlocal it
                k = it % 2
                it += 1
                c.dma(xc[k][:], xin[rows, sl], reads=([t_xin] if t_xin is not None else []), writes=[t_xc[k]])
                c.op("dve", lambda e: e.scalar_tensor_tensor(out=xo[k][:], in0=ps[:, 0:TQ], scalar=K.mod[:, gcol + nb:gcol + nb + 1],
                                                             in1=xc[k][:], op0=ALU.mult, op1=ALU.add),
                     reads=[t_ps, t_xc[k], K.t_mod], writes=[t_xo[k]])
                c.dma(xout[rows, sl], xo[k][:], reads=[t_xo[k]], writes=[t_xout])
            proj_fm(K, wb, t_wb, 0, mT, t_mT, pj, t_pj, post)
        c.barrier()


def peer(K, layer, xin, t_xin, xout, t_xout):
    c = K.c
    S, TQ, NQ, NB, QB = K.S, K.TQ, K.NQ, K.NB, K.QB
    K.pjcnt = 0
    gcol = layer * 96 + 5 * 16
    with ExitStack() as pe_s:
        idxT = SB(K, pe_s, "idxT", [128, S], U32)
        t_idxT = c.tok("idxT")
        gT = SB(K, pe_s, "gT", [128, S], F32)
        t_gT = c.tok("gT")
        with ExitStack() as es:
            hT = SB(K, es, "hT2", [128, NCH, S], BF16)
            t_hT = c.tok("hT2")
            norm_mod(K, xin, t_xin, layer, 1, hT, t_hT)
            keys = SB(K, es, "keys", [128, 2, 128], F32)
            t_keys = c.tok("keys")
            c.dma(keys[:], K.d_keys[layer], writes=[t_keys])
            pj = [PS(K, es, "pjq") for _ in range(2)]
            t_pj = [c.tok("pjq0"), c.tok("pjq1")]
            scps = [PS(K, es, "scps") for _ in range(4)]
            t_scps = [c.tok("scps%d" % i) for i in range(4)]
            tps = [PS(K, es, "tps", [128, 1024], BF16), PS(K, es, "tps2")]
            t_tps = [c.tok("tpsA"), c.tok("tpsB")]
            qf = SB(K, es, "qf", [128, 16, TQ], F32)
            t_qf = c.tok("qf")
            sc = SB(K, es, "sc", [128, 2048], F32)
            t_sc = c.tok("sc")
            sc2 = SB(K, es, "sc2", [128, 256], F32)
            t_sc2 = c.tok("sc2")
            tops = SB(K, es, "tops", [128, 16, 16], F32)
            t_tops = c.tok("tops")
            tiu = SB(K, es, "tiu", [128, 16, 16], U32)
            t_tiu = c.tok("tiu")
            topi = SB(K, es, "topi", [128, 16, 16], F32)
            t_topi = c.tok("topi")
            cs = SB(K, es, "cs", [128, 256], F32)
            t_cs = c.tok("cs")
            ci = SB(K, es, "ci", [128, 256], F32)
            t_ci = c.tok("ci")
            best = SB(K, es, "best", [128, 8, 16], F32)
            t_best = c.tok("best")
            eq3 = SB(K, es, "eq3", [128, 16, 256], F32)
            t_eq3 = c.tok("eq3")
            idxf = SB(K, es, "idxf", [128, 128], F32)
            t_idxf = c.tok("idxf")
            posu = SB(K, es, "posu", [128, 8, 16], U32)
            t_posu = c.tok("posu")
            posf = SB(K, es, "posf", [128, 5, 128], F32)
            t_posf = c.tok("posf")
            eg = SB(K, es, "eg", [128, 8, 16], F32)
            t_eg = c.tok("eg")
            gates = SB(K, es, "gates", [128, 128], F32)
            t_gates = c.tok("gates")
            sm = SB(K, es, "sm", [128, 32], F32)
            t_sm = c.tok("sm")
            h2b = [SB(K, es, "h2b", [128, D], BF16) for _ in range(2)]
            t_h2b = [c.tok("h2b0"), c.tok("h2b1")]

            for tq in range(NQ):
                sl = slice(tq * TQ, (tq + 1) * TQ)
                for u in range(16):
                    wb, t_wb = get_weights(K, [K.d_wquery[layer, u]])

                    def post(ps, t_ps, tq_, sl_, u=u):
                        c.op("act", lambda e: e.activation(out=qf[:, u, :], in_=ps[:, 0:TQ], func=AF.Copy),
                             reads=[t_ps], writes=[t_qf])
                    k = K.pjcnt % 2
                    K.pjcnt += 1
                    for cc in range(NCH):
                        c.op("pe", lambda e, cc=cc, k=k, wb=wb: e.matmul(pj[k][:, 0:TQ], lhsT=wb[:, cc, 0:128], rhs=hT[:, cc, sl],
                                                                         start=(cc == 0), stop=(cc == NCH - 1)),
                             reads=[t_wb, t_hT], writes=[t_pj[k]])
                    post(pj[k], t_pj[k], tq, sl)
                for tbl in range(QB):
                    tb = tq * QB + tbl
                    ts = slice(tbl * 128, (tbl + 1) * 128)
                    tsg = slice(tb * 128, (tb + 1) * 128)
                    hb_ = h2b[tb % 2]
                    for half in range(2):
                        for cc8 in range(8):
                            cc = half * 8 + cc8
                            c.op("pe", lambda e, cc=cc, cc8=cc8: e.transpose(out=tps[0][:, cc8 * 128:(cc8 + 1) * 128],
                                                                             in_=hT[:, cc, tsg], identity=K.ident_bf[:]),
                                 reads=[t_hT, K.t_identb], writes=[t_tps[0]])
                        c.op("act", lambda e, half=half, hb_=hb_: e.activation(out=hb_[:, half * 1024:(half + 1) * 1024],
                                                                               in_=tps[0][:, 0:1024], func=AF.Copy),
                             reads=[t_tps[0]], writes=[t_h2b[tb % 2]])
                    c.dma(K.d_h2[tsg, :], hb_[:], reads=[t_h2b[tb % 2]], writes=[K.t_h2])
                    for u in range(16):
                        c.op("pe", lambda e, u=u: e.matmul(scps[u // 4][:, (u % 4) * 128:(u % 4 + 1) * 128], lhsT=qf[:, u, ts],
                                                           rhs=keys[:, u % 2, :], start=True, stop=True),
                             reads=[t_qf, t_keys], writes=[t_scps[u // 4]])
                    for b4 in range(4):
                        c.op("act", lambda e, b4=b4: e.activation(out=sc[:, b4 * 512:(b4 + 1) * 512], in_=scps[b4][:, :], func=AF.Copy),
                             reads=[t_scps[b4]], writes=[t_sc])
                    for u in range(16):
                        su = sc[:, u * 128:(u + 1) * 128]
                        c.op("dve", lambda e, u=u, su=su: e.max(out=tops[:, u, 0:8], in_=su), reads=[t_sc], writes=[t_tops])
                        c.op("dve", lambda e, u=u, su=su: e.max_index(out=tiu[:, u, 0:8], in_max=tops[:, u, 0:8], in_values=su),
                             reads=[t_sc, t_tops], writes=[t_tiu])
                        c.op("dve", lambda e, u=u, su=su: e.match_replace(out=sc2[:, 0:128], in_to_replace=tops[:, u, 0:8],
                                                                          in_values=su, imm_value=-1e30),
                             reads=[t_sc, t_tops], writes=[t_sc2])
                        c.op("dve", lambda e, u=u: e.max(out=tops[:, u, 8:16], in_=sc2[:, 0:128]), reads=[t_sc2], writes=[t_tops])
                        c.op("dve", lambda e, u=u: e.max_index(out=tiu[:, u, 8:16], in_max=tops[:, u, 8:16], in_values=sc2[:, 0:128]),
                             reads=[t_sc2, t_tops], writes=[t_tiu])
                    c.op("dve", lambda e: e.tensor_copy(out=topi[:], in_=tiu[:]), reads=[t_tiu], writes=[t_topi])
                    for h in range(8):
                        s0, s1 = tops[:, 2 * h, :], tops[:, 2 * h + 1, :]
                        cs3 = cs[:].rearrange("p (a b) -> p a b", b=16)
                        c.op("dve", lambda e, s0=s0, s1=s1, cs3=cs3: e.tensor_tensor(
                            out=cs3, in0=mkap(s0, [[1, 16], [0, 16]]), in1=mkap(s1, [[0, 16], [1, 16]]), op=ALU.add),
                            reads=[t_tops], writes=[t_cs])
                        c.op("dve", lambda e, h=h: e.max(out=best[:, h, 0:8], in_=cs[:]), reads=[t_cs], writes=[t_best])
                        c.op("dve", lambda e, h=h: e.max_index(out=posu[:, h, 0:8], in_max=best[:, h, 0:8], in_values=cs[:]),
                             reads=[t_cs, t_best], writes=[t_posu])
                        c.op("dve", lambda e, h=h: e.match_replace(out=sc2[:], in_to_replace=best[:, h, 0:8], in_values=cs[:],
                                                                   imm_value=-1e30), reads=[t_cs, t_best], writes=[t_sc2])
                        c.op("dve", lambda e, h=h: e.max(out=best[:, h, 8:16], in_=sc2[:]), reads=[t_sc2], writes=[t_best])
                        c.op("dve", lambda e, h=h: e.max_index(out=posu[:, h, 8:16], in_max=best[:, h, 8:16], in_values=sc2[:]),
                             reads=[t_sc2, t_best], writes=[t_posu])
                    T3 = eq3[:, 0:8, :].rearrange("p h (k a) -> p (h k) a", a=16)
                    T4 = eq3[:, 0:8, :].rearrange("p h (k a) -> p h k a", a=16)
                    iota_b = mkap(K.cst[:, C_IOTA:C_IOTA + 16], [[0, 128], [1, 16]])
                    lo_b = mkap(K.cst[:, C_LO16:C_LO16 + 16], [[0, 128], [1, 16]])
                    c.op("dve", lambda e: e.tensor_copy(out=posf[:, 0, :], in_=posu[:].rearrange("p h k -> p (h k)")),
                         reads=[t_posu], writes=[t_posf])
                    c.op("dve", lambda e: e.tensor_tensor(out=T3, in0=mkap(posf[:, 0, :], [[1, 128], [0, 16]]), in1=lo_b, op=ALU.is_ge),
                         reads=[t_posf, K.t_cst], writes=[t_eq3])
                    c.op("dve", lambda e: e.tensor_reduce(out=posf[:, 1, :], in_=T3, axis=AX.X, op=ALU.add),
                         reads=[t_eq3], writes=[t_posf])
                    c.op("dve", lambda e: e.tensor_scalar(out=posf[:, 1, :], in0=posf[:, 1, :], scalar1=-1.0, scalar2=None, op0=ALU.add),
                         reads=[t_posf], writes=[t_posf])
                    c.op("dve", lambda e: e.scalar_tensor_tensor(out=posf[:, 2, :], in0=posf[:, 1, :], scalar=-16.0, in1=posf[:, 0, :],
                                                                 op0=ALU.mult, op1=ALU.add), reads=[t_posf], writes=[t_posf])
                    for w_, (src_, dst_) in enumerate(((1, 3), (2, 4))):
                        c.op("dve", lambda e, src_=src_: e.tensor_tensor(out=T3, in0=mkap(posf[:, src_, :], [[1, 128], [0, 16]]),
                                                                         in1=iota_b, op=ALU.is_equal),
                             reads=[t_posf, K.t_cst], writes=[t_eq3])
                        c.op("dve", lambda e, w_=w_: e.tensor_tensor(out=T4, in0=T4, in1=mkap(topi[:, w_, :], [[32, 8], [0, 16], [1, 16]]),
                                                                     op=ALU.mult), reads=[t_eq3, t_topi], writes=[t_eq3])
                        c.op("dve", lambda e, dst_=dst_: e.tensor_reduce(out=posf[:, dst_, :], in_=T3, axis=AX.X, op=ALU.add),
                             reads=[t_eq3], writes=[t_posf])
                    c.op("dve", lambda e: e.scalar_tensor_tensor(out=idxf[:], in0=posf[:, 3, :], scalar=128.0, in1=posf[:, 4, :],
                                                                 op0=ALU.mult, op1=ALU.add), reads=[t_posf], writes=[t_idxf])
                    c.op("dve", lambda e: e.tensor_scalar(out=sm[:, 0:8], in0=best[:, :, 0], scalar1=-1.0, scalar2=None, op0=ALU.mult),
                         reads=[t_best], writes=[t_sm])
                    for h in range(8):
                        c.op("act", lambda e, h=h: e.activation(out=eg[:, h, :], in_=best[:, h, :], func=AF.Exp, bias=sm[:, h:h + 1],
                                                                scale=1.0, accum_out=sm[:, 8 + h:9 + h]),
                             reads=[t_best, t_sm], writes=[t_eg, t_sm])
                    c.op("dve", lambda e: e.reciprocal(out=sm[:, 16:24], in_=sm[:, 8:16]), reads=[t_sm], writes=[t_sm])
                    c.op("dve", lambda e: e.tensor_tensor(out=gates[:].rearrange("p (h k) -> p h k", k=16), in0=eg[:],
                                                          in1=mkap(sm[:, 16:24], [[1, 8], [0, 16]]), op=ALU.mult),
                         reads=[t_eg, t_sm], writes=[t_gates])
                    c.op("dve", lambda e: e.tensor_scalar(out=idxf[:], in0=idxf[:], scalar1=0.0, scalar2=float(NEXP - 1),
                                                          op0=ALU.max, op1=ALU.min), reads=[t_idxf], writes=[t_idxf])
                    c.op("pe", lambda e: e.transpose(out=tps[1][:, 0:128], in_=idxf[:], identity=K.ident_f),
                         reads=[t_idxf, K.t_cst], writes=[t_tps[1]])
                    c.op("pe", lambda e: e.transpose(out=tps[1][:, 128:256], in_=gates[:], identity=K.ident_f),
                         reads=[t_gates, K.t_cst], writes=[t_tps[1]])
                    c.op("dve", lambda e: e.tensor_copy(out=idxT[:, tsg], in_=tps[1][:, 0:128]), reads=[t_tps[1]], writes=[t_idxT])
                    c.op("dve", lambda e: e.tensor_copy(out=gT[:, tsg], in_=tps[1][:, 128:256]),
                         reads=[t_tps[1]], writes=[t_gT])
            c.barrier()

        with ExitStack() as es:
            NS = 4
            UV = [SB(K, es, "UV", [128, 2 * D], BF16) for _ in range(NS)]
            t_UV = [c.tok("UV%d" % i) for i in range(NS)]
            hbc = [SB(K, es, "hbc", [128, D], BF16) for _ in range(NS)]
            t_hbc = [c.tok("hbc%d" % i) for i in range(NS)]
            h2 = [SB(K, es, "h2", [128, D], BF16) for _ in range(2)]
            t_h2s = [c.tok("h2s0"), c.tok("h2s1")]
            yT = [PS(K, es, "yT", [128, 16, 128], F32) for _ in range(2)]
            t_yT = [c.tok("yT0"), c.tok("yT1")]
            junk = SB(K, es, "junk", [128, D], BF16)
            t_junk = c.tok("junk")
            acc = [SB(K, es, "acc", [128, 8], F32) for _ in range(2)]
            t_acc = [c.tok("acc0"), c.tok("acc1")]
            wv_ = [SB(K, es, "wv", [128, 2], BF16) for _ in range(2)]
            t_wv = [c.tok("wv0"), c.tok("wv1")]
            xs = SB(K, es, "xs", [128, NCH, 128], F32)
            t_xs = c.tok("xs")
            xo = SB(K, es, "xo2", [128, NCH, 128], F32)
            t_xo = c.tok("xo2")
            xiv = xin.rearrange("(c p) t -> p c t", p=128)
            xov = xout.rearrange("(c p) t -> p c t", p=128)
            for tb in range(NB):
                tsg = slice(tb * 128, (tb + 1) * 128)
                yTt, t_yTt = yT[tb % 2], t_yT[tb % 2]
                h2t, t_h2t = h2[tb % 2], t_h2s[tb % 2]
                c.dma(h2t[:], K.d_h2[tsg, :], reads=[K.t_h2], writes=[t_h2t])
                c.dma(xs[:], xiv[:, :, tsg], reads=([t_xin] if t_xin is not None else []), writes=[t_xs])
                for j in range(128):
                    t = tb * 128 + j
                    s = t % NS
                    s2 = t % 2
                    c.dma(None, None, reads=[t_idxT, K.t_uvb[layer]], writes=[t_UV[s]], q="pool",
                          fn=lambda e, s=s, t=t: e.indirect_dma_start(
                              out=UV[s][:], out_offset=None, in_=K.d_uvb[layer][:, :],
                              in_offset=bass.IndirectOffsetOnAxis(ap=idxT[:, t:t + 1], axis=0),
                              bounds_check=K.bc_reg, oob_is_err=False))
                    if t % 2 != 1:
                        hrow = K.d_h2[t:t + 1, :]
                        src = bass.AP(tensor=hrow.tensor, offset=hrow.offset, ap=[[0, 4], [1, D]])
                        hb0 = hbc[s][0:1, :]
                        dst = bass.AP(tensor=hb0.tensor, offset=hb0.offset, ap=[[32 * D, 4], [1, D]])
                        c.dma(dst, src, reads=[K.t_h2], writes=[t_hbc[s]])
                        c.op("dve", lambda e, s=s: e.stream_shuffle(out=hbc[s][:], in_=hbc[s][:], mask=[0] * 32),
                             reads=[t_hbc[s]], writes=[t_hbc[s]])
                    else:
                        hrow = K.d_h2[t:t + 1, :]
                        src = bass.AP(tensor=hrow.tensor, offset=hrow.offset, ap=[[0, 128], [1, D]])
                        c.dma(hbc[s][:], src, reads=[K.t_h2], writes=[t_hbc[s]])
                    c.op("dve", lambda e, s=s, s2=s2: e.scalar_tensor_tensor(
                        out=junk[:], in0=UV[s][:, 0:D], scalar=1.0, in1=hbc[s][:],
                        op0=ALU.mult, op1=ALU.mult, accum_out=acc[s2][:, 4:5]),
                        reads=[t_UV[s], t_hbc[s]], writes=[t_junk, t_acc[s2]])
                    c.op("act", lambda e, s2=s2: e.activation(out=acc[s2][:, 5:6], in_=acc[s2][:, 4:5], func=AF.Gelu),
                         reads=[t_acc[s2]], writes=[t_acc[s2]])
                    c.op("dve", lambda e, s2=s2, t=t: e.tensor_tensor(out=wv_[s2][:, 0:1], in0=acc[s2][:, 5:6], in1=gT[:, t:t + 1],
                                                                      op=ALU.mult),
                         reads=[t_acc[s2], t_gT], writes=[t_wv[s2]])
                    for cc in range(NCH):
                        c.op("pe", lambda e, cc=cc, s=s, s2=s2, j=j: e.matmul(
                            yTt[:, cc, j:j + 1], lhsT=UV[s][:, D + cc * 128:D + (cc + 1) * 128],
                            rhs=wv_[s2][:, 0:1], start=True, stop=True),
                            reads=[t_UV[s], t_wv[s2]], writes=[t_yTt])
                for cc in range(NCH):
                    c.op("dve", lambda e, cc=cc: e.scalar_tensor_tensor(out=xo[:, cc, :], in0=yTt[:, cc, :],
                                                                        scalar=K.mod[:, gcol + cc:gcol + cc + 1], in1=xs[:, cc, :],
                                                                        op0=ALU.mult, op1=ALU.add),
                         reads=[t_yTt, t_xs, K.t_mod], writes=[t_xo])
                c.dma(xov[:, :, tsg], xo[:], reads=[t_xo], writes=[t_xout])
            c.barrier()


def t5_bucket_np(dist):
    n = np.maximum(dist, 0)
    nf = np.maximum(n, 1).astype(np.float32)
    large = 16 + (np.log(nf / np.float32(16)) / np.float32(math.log(2048 / 16)) * np.float32(16)).astype(np.int32)
    large = np.minimum(large, 31)
    return np.where(n < 16, n, large)


def prep_w(W):
    n = W.shape[1]
    return np.ascontiguousarray(W.reshape(16, 128, n // 128, 128).transpose(2, 1, 0, 3))


def host_prep(inp, S, cores):
    f = lambda a: np.ascontiguousarray(np.asarray(a, dtype=np.float32))
    TQ = min(512, S)
    PADL = TQ - 128
    MW = PADL + S
    shared = {}
    w_ada = f(inp["w_ada"])
    shared["wada"] = np.ascontiguousarray(w_ada.reshape(2, 16, 128, 24, 512).transpose(0, 3, 2, 1, 4))
    w_in = f(inp["even_w_in"])[0]
    shared["win"] = prep_w(np.concatenate([w_in[:, :3072], w_in[:, 3080:]], axis=1))
    shared["wff"] = np.ascontiguousarray(w_in[:, 3072:3080].reshape(16, 128, 8).transpose(1, 0, 2))
    shared["wout0"] = prep_w(f(inp["even_w_out"])[0])
    shared["wqkv"] = prep_w(f(inp["odd_w_qkv"])[0])
    shared["wout1"] = prep_w(f(inp["odd_w_out"])[0])
    wq = f(inp["peer_w_query"])
    shared["wquery"] = np.stack([prep_w(wq[0]), prep_w(wq[1])])
    sk = f(inp["peer_sub_keys"])
    shared["keysT"] = np.ascontiguousarray(sk.transpose(0, 3, 1, 2))
    pu, pv = f(inp["peer_u"]), f(inp["peer_v"])
    shared["uv0"] = np.concatenate([pu[0], pv[0]], axis=1).reshape(64, 128, 4 * D)
    shared["uv1"] = np.concatenate([pu[1], pv[1]], axis=1).reshape(64, 128, 4 * D)
    rb = f(inp["rel_bias"])
    kl = np.arange(128)[:, None]
    jj = np.arange(MW)[None, :]
    delta = jj - PADL - kl
    bidx = t5_bucket_np(delta)
    bt = rb[bidx]
    bt = np.where((delta >= 0)[:, :, None], bt, np.float32(NEG))
    shared["biasT"] = np.ascontiguousarray(bt.transpose(2, 0, 1)).astype(np.float32)
    mult = np.zeros_like(delta, dtype=np.float32)
    for (wdw, dil) in ((128, 1), (512, 4), (2048, 16)):
        mult += ((delta >= 0) & (delta % dil == 0) & (delta // dil <= wdw // dil)).astype(np.float32)
    shared["multT"] = np.ascontiguousarray(mult)
    jj2 = np.arange(PADL + TQ)[None, :]
    shared["triT"] = np.ascontiguousarray(np.where((jj2 - PADL - kl) >= 0, np.float32(0.0), np.float32(NEG)))
    x = np.asarray(inp["x"], dtype=np.float32)
    cvec = f(inp["c"])
    b_ada = f(inp["b_ada"])
    ng = f(inp["norm_gain"])
    maps = []
    for b in cores:
        cst = np.zeros((128, NCST), np.float32)
        cst[:, C_C:C_C + 16] = cvec[b].reshape(16, 128).T
        cst[:, C_BADA:C_BADA + 192] = b_ada.reshape(2, 96, 128).transpose(2, 0, 1).reshape(128, 192)
        cst[:, C_GAIN:C_GAIN + 64] = ng.reshape(2, 2, 16, 128).transpose(3, 0, 1, 2).reshape(128, 64)
        cst[:, C_FOXG:C_FOXG + 2] = f(inp["even_fox_qk_gain"])[0].T
        cst[:, C_DIFFG:C_DIFFG + 2] = f(inp["even_diff_qk_gain"])[0].T
        cst[:, C_LAM:C_LAM + 4] = f(inp["even_diff_lambda"])[0].T
        cst[:, C_SUBLN:C_SUBLN + 2] = f(inp["even_diff_subln_gain"])[0].reshape(2, 128).T
        cst[:, C_ODDG:C_ODDG + 2] = f(inp["odd_qk_gain"])[0].T
        cst[0:8, C_BF] = f(inp["even_b_forget"])[0]
        cst[:, C_ID:C_ID + 128] = np.eye(128, dtype=np.float32)
        cst[:, C_IOTA:C_IOTA + 16] = np.arange(16, dtype=np.float32)[None, :]
        cst[:, C_LO16:C_LO16 + 16] = 16.0 * np.arange(16, dtype=np.float32)[None, :]
        m = dict(shared)
        m["cst"] = cst
        m["xT"] = np.ascontiguousarray(x[b, :S, :].T)
        maps.append(m)
    return maps


def kernel(**inputs):
    S = 2048
    cores = list(range(8))
    maps = host_prep(inputs, S, cores)
    nc = build(S)
    res = run_bass_kernel_spmd(nc, maps, core_ids=cores)
    out = np.stack([np.ascontiguousarray(res.results[i]["y"].T) for i in range(8)], axis=0)
    return out.astype(np.float32)
```
